# Optimizing a Trainium2 kernel written in Bass

```python
import math
import jax, jax.numpy as jnp
from jax import lax
import numpy as np

D_MODEL = 1024
BATCH = 8
SEQ = 2048
DEPTH = 4

N_MIXERS = 2
N_A = (DEPTH + 1) // 2
N_B = DEPTH // 2
CONV_WIDTH = 31
POOL_WINDOWS = (2, 4, 8, 16)
N_POOL_GROUPS = len(POOL_WINDOWS)
POOL_GROUP = D_MODEL // N_POOL_GROUPS
D_FF = int(math.ceil(8 * D_MODEL / 3 / 256) * 256)
DN_ALPHA = float((2 * DEPTH) ** 0.25)
DN_BETA = float((8 * DEPTH) ** -0.25)
LN_EPS = 1e-5

kernel_name = "hybrid_conformer_conv_multiscale_pool_deepnorm"


def layer_norm(x, g, b):
    xf = x.astype(jnp.float32)
    mu = jnp.mean(xf, axis=-1, keepdims=True)
    xc = xf - mu
    var = jnp.mean(xc * xc, axis=-1, keepdims=True)
    y = xc * lax.rsqrt(var + LN_EPS) * g.astype(jnp.float32) + b.astype(jnp.float32)
    return y.astype(x.dtype)


def conformer_conv(x, w_in, b_in, w_dw, b_dw, ln_g, ln_b, w_out, b_out):
    h = jnp.einsum('bsd,de->bse', x, w_in) + b_in
    val, gate = jnp.split(h, 2, axis=-1)
    h = val * jax.nn.sigmoid(gate)
    h = lax.conv_general_dilated(
        h, w_dw, window_strides=(1,), padding=[(CONV_WIDTH - 1, 0)],
        dimension_numbers=('NWC', 'WIO', 'NWC'),
        feature_group_count=D_MODEL) + b_dw
    h = layer_norm(h, ln_g, ln_b)
    h = jax.nn.silu(h)
    return jnp.einsum('bsd,de->bse', h, w_out) + b_out


def multiscale_pool(x, w_grp, b_grp, scale):
    s = x.shape[1]
    pos = jnp.arange(1, s + 1, dtype=jnp.float32)[None, :, None]
    outs = []
    for g, w in enumerate(POOL_WINDOWS):
        xg = x[..., g * POOL_GROUP:(g + 1) * POOL_GROUP].astype(jnp.float32)
        c = jnp.cumsum(xg, axis=1)
        c_lag = jnp.pad(c, ((0, 0), (w, 0), (0, 0)))[:, :s]
        mean = (c - c_lag) / jnp.minimum(pos, float(w))
        outs.append((mean - xg).astype(x.dtype))
    pooled = jnp.stack(outs, axis=2)
    y = jnp.einsum('bsgc,gce->bsge', pooled, w_grp) + b_grp
    y = y.reshape(x.shape)
    return y * scale


def swiglu(x, w_gate, w_up, w_down):
    h = jax.nn.silu(jnp.einsum('bsd,df->bsf', x, w_gate)) * jnp.einsum('bsd,df->bsf', x, w_up)
    return jnp.einsum('bsf,fd->bsd', h, w_down)


def setup_inputs(seed: int = 0) -> dict:
    key = jax.random.key(seed)
    ks = jax.random.split(key, 24)
    f32 = jnp.float32
    D, C = D_MODEL, POOL_GROUP
    nrm = lambda k, shape, s: (jax.random.normal(k, shape, f32) * s).astype(f32)
    return {
        "x": nrm(ks[0], (BATCH, SEQ, D), 1.0),
        "a_w_in": nrm(ks[1], (N_A, D, 2 * D), D ** -0.5),
        "a_b_in": nrm(ks[2], (N_A, 2 * D), 0.01),
        "a_w_dw": nrm(ks[3], (N_A, CONV_WIDTH, 1, D), CONV_WIDTH ** -0.5),
        "a_b_dw": nrm(ks[4], (N_A, D), 0.01),
        "a_ln_g": 1.0 + nrm(ks[5], (N_A, D), 0.01),
        "a_ln_b": nrm(ks[6], (N_A, D), 0.01),
        "a_w_out": nrm(ks[7], (N_A, D, D), DN_BETA * D ** -0.5),
        "a_b_out": nrm(ks[8], (N_A, D), 0.01),
        "p_w": nrm(ks[9], (N_B, N_POOL_GROUPS, C, C), DN_BETA * C ** -0.5),
        "p_b": nrm(ks[10], (N_B, N_POOL_GROUPS, C), 0.01),
        "p_scale": 1.0 + nrm(ks[11], (N_B, D), 0.01),
        "ffn_w_gate": nrm(ks[12], (DEPTH, D, D_FF), D ** -0.5),
        "ffn_w_up": nrm(ks[13], (DEPTH, D, D_FF), D ** -0.5),
        "ffn_w_down": nrm(ks[14], (DEPTH, D_FF, D), DN_BETA * D_FF ** -0.5),
        "ln1_g": 1.0 + nrm(ks[15], (DEPTH, D), 0.01),
        "ln1_b": nrm(ks[16], (DEPTH, D), 0.01),
        "ln2_g": 1.0 + nrm(ks[17], (DEPTH, D), 0.01),
        "ln2_b": nrm(ks[18], (DEPTH, D), 0.01),
    }


def reference(x, a_w_in, a_b_in, a_w_dw, a_b_dw, a_ln_g, a_ln_b, a_w_out, a_b_out,
              p_w, p_b, p_scale, ffn_w_gate, ffn_w_up, ffn_w_down,
              ln1_g, ln1_b, ln2_g, ln2_b):
    alpha = jnp.asarray(DN_ALPHA, x.dtype)
    for i in range(DEPTH):
        j = i // N_MIXERS
        if i % N_MIXERS == 0:
            mix = conformer_conv(x, a_w_in[j], a_b_in[j], a_w_dw[j], a_b_dw[j],
                                 a_ln_g[j], a_ln_b[j], a_w_out[j], a_b_out[j])
        else:
            mix = multiscale_pool(x, p_w[j], p_b[j], p_scale[j])
        x = layer_norm(alpha * x + mix, ln1_g[i], ln1_b[i])
        x = layer_norm(alpha * x + swiglu(x, ffn_w_gate[i], ffn_w_up[i], ffn_w_down[i]),
                       ln2_g[i], ln2_b[i])
    return x
```

```python
import contextlib
import random
import numpy as np
import concourse.bass as bass
import concourse.mybir as mybir
from concourse.bass_utils import run_bass_kernel_spmd

F32 = mybir.dt.float32
BF16 = mybir.dt.bfloat16
AF = mybir.ActivationFunctionType
ALU = mybir.AluOpType

D = 1024
SEQ = 2048
NB = 8
DEPTH = 4
DFF = 2816
NJ = DFF // 128
KW = 31
ALPHA = float((2 * DEPTH) ** 0.25)
EPS = 1e-5
NT = 512
NTT = NT // 128
NQ = SEQ // NT
G = 4
NA = 4
NBS = 12
NLN = 2
UP = 32
PP = 16
POOL_W = (2, 4, 8, 16)
JITTER_SEED = 3
SELF_WIN = 1000000000


class Sched:
    ENGS = ("pe", "act", "dve", "pool", "sp")
    DEF_DUR = {"pe": 2.0, "act": 0.7, "dve": 0.7, "pool": 0.65, "sp": 0.15}
    DMA_LAT = 3.0
    SEM_LAT = 0.35

    def __init__(self):
        self.ops = {e: [] for e in self.ENGS}
        self.cnt = {e: 0 for e in self.ENGS}
        self.dcnt = {}
        self.last_w = {}
        self.readers = {}
        self.seen = {e: {} for e in self.ENGS}
        self.semkeys = set(self.ENGS)
        self.capture = None
        self.eng_free = {e: 0.0 for e in self.ENGS}
        self.t_w = {}
        self.t_r = {}

    def op(self, eng, fn, reads=(), writes=(), dma=None, dur=None):
        desc = (eng, fn, tuple(reads), tuple(writes), dma, dur)
        if self.capture is not None:
            self.capture.append(desc)
            return None
        return self.commit(desc)

    def estimate(self, desc):
        eng, fn, reads, writes, dma, dur = desc
        ready = 0.0
        for r in reads:
            ready = max(ready, self.t_w.get(r, 0.0))
        for w in writes:
            ready = max(ready, self.t_w.get(w, 0.0), self.t_r.get(w, 0.0))
        return max(self.eng_free[eng], ready + self.SEM_LAT)

    def commit(self, desc):
        eng, fn, reads, writes, dma, dur = desc
        start = self.estimate(desc)
        d = self.DEF_DUR[eng] if dur is None else dur
        if dma is None:
            fin = start + d
            self.eng_free[eng] = fin
        else:
            self.eng_free[eng] = start + d
            fin = start + d + self.DMA_LAT
        for r in reads:
            self.t_r[r] = max(self.t_r.get(r, 0.0), fin)
        for w in writes:
            self.t_w[w] = fin
            self.t_r[w] = 0.0
        idx = len(self.ops[eng])
        deps = []
        for r in reads:
            if r in self.last_w:
                deps.append(self.last_w[r])
        for w in writes:
            if w in self.last_w:
                deps.append(self.last_w[w])
            deps.extend(self.readers.get(w, ()))
        need = {}
        for (sk, val, peng, pidx, pdma) in deps:
            if (not pdma) and peng == eng and dma is None:
                if pidx < idx - SELF_WIN or eng == "pe":
                    continue
            if self.seen[eng].get(sk, 0) >= val:
                continue
            if need.get(sk, 0) < val:
                need[sk] = val
        for sk, val in need.items():
            self.seen[eng][sk] = val
        if dma is None:
            self.cnt[eng] += 1
            tick = (eng, self.cnt[eng], eng, idx, False)
            inc = (eng, 1)
        else:
            self.semkeys.add(dma)
            self.dcnt[dma] = self.dcnt.get(dma, 0) + 1
            tick = (dma, 16 * self.dcnt[dma], eng, idx, True)
            inc = (dma, 16)
        for r in reads:
            self.readers.setdefault(r, []).append(tick)
        for w in writes:
            self.last_w[w] = tick
            self.readers[w] = []
        self.ops[eng].append((list(need.items()), fn, inc))
        return tick

    def final_wait(self, eng, keys):
        need = {}
        for k in keys:
            t = self.last_w.get(k)
            cands = list(self.readers.get(k, ()))
            if t is not None:
                cands.append(t)
            for (sk, val, *_r) in cands:
                if need.get(sk, 0) < val:
                    need[sk] = val
        self.ops[eng].append((list(need.items()), None, None))


def build(layers):
    nc = bass.Bass("TRN2", target_bir_lowering=False)
    conv_layers = [L for L in layers if L % 2 == 0]
    pool_layers = [L for L in layers if L % 2 == 1]

    def dram(name, shape, kind="ExternalInput"):
        return nc.dram_tensor(name, list(shape), F32, kind=kind).ap()

    x_d = dram("x", [SEQ, D])
    y_d = dram("y", [SEQ, D], kind="ExternalOutput")
    ktab_d = dram("ktab", [128, 272])
    lnp_d = dram("lnp", [DEPTH, 4, 128, D])
    mixb_d = dram("mixb", [DEPTH, 2, 128, D])
    wg_d = {L: dram(f"wg{L}", [NJ, 128, 8 * 128]) for L in layers}
    wu_d = {L: dram(f"wu{L}", [NJ, 128, 8 * 128]) for L in layers}
    wd_d = {L: dram(f"wd{L}", [NJ, 128, D]) for L in layers}
    win_d = {L: dram(f"win{L}", [8, 128, 2 * 8 * 128]) for L in conv_layers}
    wout_d = {L: dram(f"wout{L}", [2, 4, 128, D]) for L in conv_layers}
    convc_d = {L: dram(f"convc{L}", [128, 288]) for L in conv_layers}
    wp_d = {L: dram(f"wp{L}", [128, 8 * 256]) for L in pool_layers}

    S = Sched()
    es = contextlib.ExitStack()
    with es:
        def sb(name, shape, dt=F32):
            return es.enter_context(nc.sbuf_tensor("sb_" + name, list(shape), dt))

        xtok = [sb(f"xtok{b}", [128, NTT, D]) for b in range(2)]
        xT = [sb(f"xT{b}", [128, 8, NT], BF16) for b in range(2)]
        s1 = sb("s1", [128, 8, NT + UP])
        s2 = sb("s2", [128, 8, NT])
        dgb = sb("dgb", [128, 2, KW * 128], BF16)
        sTm = sb("sTm", [128, 8, NT], BF16)
        hT = sb("hT", [128, 8, NT], BF16)
        NAF, NAM, NBF, NBM = 4, 2, 12, 4
        ringAF = sb("ringAF", [128, NAF, 2048], BF16)
        ringAM = sb("ringAM", [128, NAM, 2048], BF16)
        ringBF = sb("ringBF", [128, NBF, D], BF16)
        ringBM = sb("ringBM", [128, NBM, D], BF16)
        lnb = sb("lnb", [128, NLN, 2, D])
        mixb = sb("mixb", [128, 2, D])
        sgtF = sb("sgtF", [128, 2, 512])
        sgtM = sb("sgtM", [128, 1, 512])
        vbh = sb("vbh", [128, 1, 512])
        ktab = sb("ktab", [128, 272])
        convc = {L: sb(f"convc{L}", [128, 288]) for L in conv_layers}
        hbias = {L: sb(f"hb{L}", [128, 16]) for L in conv_layers}
        uh = {L: sb(f"uh{L}", [128, 8, UP], BF16) for L in conv_layers}
        xh = {L: sb(f"xh{L}", [128, 8, PP]) for L in pool_layers}
        stt = sb("stt", [128, 2, NTT, 2, 6])
        mv = sb("mv", [128, 2, NTT, 2])
        rs = sb("rs", [128, 2, 2, NTT])
        fx = sb("fx", [128, 2, 16])
        ps = es.enter_context(nc.psum_tensor("ps", [128, 8, 512], F32))

        ident = ktab[:, 0:128]
        onesm = ktab[:, 128:256]
        invc = ktab[:, 256:272]
        s1b = s1[:].bitcast(BF16)

        S.op("sp", lambda e: e.dma_start(out=ktab[:], in_=ktab_d), writes=[("KT",)], dma="d_kt")
        for L in conv_layers:
            S.op("sp", lambda e, L=L: e.dma_start(out=convc[L][:], in_=convc_d[L]),
                 writes=[("CC", L)], dma=f"d_cc{L}")
            S.op("dve", lambda e, L=L: e.tensor_scalar(
                out=hbias[L][:], in0=convc[L][:, 0:16], scalar1=0.5, scalar2=None, op0=ALU.mult),
                reads=[("CC", L)], writes=[("HB", L)])

        ra_n = [0]
        rb_n = [0]
        ln_n = [0]
        pp_n = {}

        def flip(k, n=2):
            v = pp_n.get(k, 0)
            pp_n[k] = v + 1
            return v % n

        def xt_keys(b):
            return [("XT", b, tt, kh) for tt in range(NTT) for kh in range(2)]

        rn = {"AF": 0, "AM": 0, "BF": 0, "BM": 0}

        def load_AM(src_ap):
            slot = rn["AM"] % NAM
            rn["AM"] += 1
            S.op("pool", lambda e: e.dma_start(out=ringAM[:, slot, :], in_=src_ap),
                 writes=[("RAM", slot)], dma=f"d_ram{slot}")
            return slot

        def load_AF2(src0, src1):
            slot = rn["AF"] % NAF
            rn["AF"] += 1
            S.op("pool", lambda e: e.dma_start(out=ringAF[:, slot, 0:1024], in_=src0),
                 writes=[("RAF", slot, 0)], dma=f"d_rafg{slot}")
            S.op("pool", lambda e: e.dma_start(out=ringAF[:, slot, 1024:2048], in_=src1),
                 writes=[("RAF", slot, 1)], dma=f"d_rafu{slot}")
            return slot

        def load_BF(src_ap):
            slot = rn["BF"] % NBF
            rn["BF"] += 1
            S.op("pool", lambda e: e.dma_start(out=ringBF[:, slot, :], in_=src_ap),
                 writes=[("RBF", slot)], dma=f"d_rbf{slot}")
            return slot

        def load_BM(src_ap):
            slot = rn["BM"] % NBM
            rn["BM"] += 1
            S.op("pool", lambda e: e.dma_start(out=ringBM[:, slot, :], in_=src_ap),
                 writes=[("RBM", slot)], dma=f"d_rbm{slot}")
            return slot

        def load_ln(L, which):
            par = 0 if which == 1 else 1
            src = lnp_d[L, 2 * which:2 * which + 2].rearrange("a p n -> p a n")
            S.op("sp", lambda e: e.dma_start(out=lnb[:, par, :, :], in_=src),
                 writes=[("LNB", par)], dma=f"d_ln{par}")
            return par

        def transposes(b, tt, banks, to_xT, to_s1):
            for kh in range(2):
                bank = banks[kh]

                def pe_fn(e, bank=bank, kh=kh):
                    ins = None
                    for q in range(4):
                        k = kh * 4 + q
                        ins = e.transpose(out=ps[:, bank, q * 128:(q + 1) * 128],
                                          in_=xtok[b][:, tt, k * 128:(k + 1) * 128],
                                          identity=ident)
                    return ins
                S.op("pe", pe_fn, reads=[("X", b, tt, kh), ("KT",)], writes=[("PS", bank)], dur=0.5)
                if to_xT:
                    S.op("act", lambda e, bank=bank, kh=kh: e.activation(
                        out=xT[b][:, kh * 4:(kh + 1) * 4, tt * 128:(tt + 1) * 128],
                        in_=ps[:, bank, :].rearrange("p (a b) -> p a b", a=4),
                        func=AF.Copy),
                        reads=[("PS", bank)], writes=[("XT", b, tt, kh)])
                if to_s1:
                    S.op("act", lambda e, bank=bank, kh=kh: e.activation(
                        out=s1[:, kh * 4:(kh + 1) * 4, PP + tt * 128:PP + (tt + 1) * 128],
                        in_=ps[:, bank, :].rearrange("p (a b) -> p a b", a=4), func=AF.Copy),
                        reads=[("PS", bank)],
                        writes=[("S1", c) for c in range(kh * 4, kh * 4 + 4)])

        def ln_stats(st, b, tt):
            for eh in range(2):
                S.op("dve", lambda e, eh=eh: e.bn_stats(
                    out=stt[:, st, tt, eh, :], in_=xtok[b][:, tt, eh * 512:(eh + 1) * 512]),
                    reads=[("X", b, tt, eh)], writes=[("STT", st, tt, eh)])
            S.op("dve", lambda e: e.bn_aggr(
                out=mv[:, st, tt, :], in_=stt[:, st, tt, :, :].rearrange("p a b -> p (a b)")),
                reads=[("STT", st, tt, 0), ("STT", st, tt, 1)], writes=[("MV", st, tt)], dur=0.25)

        def ln_finish(st, b, par, banks, do_tr, store_rows=None):
            S.op("act", lambda e: e.activation(out=rs[:, st, 0, :], in_=mv[:, st, :, 1], func=AF.Sqrt,
                                               bias=EPS, scale=1.0),
                 reads=[("MV", st, tt) for tt in range(NTT)], writes=[("RS", st, 0)], dur=3.0)
            S.op("dve", lambda e: e.reciprocal(out=rs[:, st, 1, :], in_=rs[:, st, 0, :]),
                 reads=[("RS", st, 0)], writes=[("RS", st, 1)], dur=0.2)
            yield
            for tt in range(NTT):
                xs = xtok[b][:, tt, :]
                xk = [("X", b, tt, 0), ("X", b, tt, 1)]
                S.op("dve", lambda e, tt=tt, xs=xs: e.scalar_tensor_tensor(
                    out=xs, in0=xs, scalar=mv[:, st, tt, 0:1],
                    in1=lnb[:, par, 0, :], op0=ALU.subtract, op1=ALU.mult),
                    reads=xk + [("MV", st, tt), ("LNB", par)], writes=xk, dur=1.25)
                S.op("dve", lambda e, tt=tt, xs=xs: e.scalar_tensor_tensor(
                    out=xs, in0=xs, scalar=rs[:, st, 1, tt:tt + 1],
                    in1=lnb[:, par, 1, :], op0=ALU.mult, op1=ALU.add),
                    reads=xk + [("RS", st, 1), ("LNB", par)], writes=xk, dur=1.25)
                if do_tr:
                    transposes(b, tt, banks, True, False)
                if store_rows is not None:
                    r0 = store_rows + tt * 128
                    S.op("sp", lambda e, tt=tt, r0=r0: e.dma_start(
                        out=y_d[r0:r0 + 128, :], in_=xtok[b][:, tt, :]),
                        reads=xk, dma=f"d_st{b}_{tt}")
                yield

        def ffn_stage(b, L, q, do_tr, store):
            groups = []
            j = 0
            first = NJ % G if NJ % G else G
            while j < NJ:
                gs = first if j == 0 else G
                groups.append(list(range(j, j + gs)))
                j += gs
            slotsB = {}

            def gu(gi):
                for jj, j in enumerate(groups[gi]):
                    sa = load_AF2(wg_d[L][j], wu_d[L][j])
                    slotsB[j] = load_BF(wd_d[L][j])
                    hidx = (gi % 2) * 4 + jj
                    bg = flip("Fg")
                    bu = 2

                    def mm(e, off, bank, sa=sa):
                        ins = None
                        for k in range(8):
                            ins = e.matmul(ps[:, bank, :],
                                           ringAF[:, sa, off + k * 128:off + (k + 1) * 128],
                                           xT[b][:, k, :], start=(k == 0), stop=(k == 7))
                        return ins
                    S.op("pe", lambda e, mm=mm, bg=bg: mm(e, 0, bg),
                         reads=[("RAF", sa, 0)] + xt_keys(b), writes=[("PS", bg)])
                    S.op("pe", lambda e, mm=mm, bu=bu: mm(e, 1024, bu),
                         reads=[("RAF", sa, 1)] + xt_keys(b), writes=[("PS", bu)])
                    qq = flip("Fs")
                    S.op("act", lambda e, bg=bg, qq=qq: e.activation(
                        out=sgtF[:, qq, :], in_=ps[:, bg, :], func=AF.Silu),
                        reads=[("PS", bg)], writes=[("SGF", qq)])
                    S.op("dve", lambda e, bu=bu, qq=qq, hidx=hidx: e.tensor_tensor(
                        out=hT[:, hidx, :], in0=ps[:, bu, :], in1=sgtF[:, qq, :], op=ALU.mult),
                        reads=[("PS", bu), ("SGF", qq)], writes=[("HT", hidx)])
                    yield

            def down(gi, last):
                grp = groups[gi]
                for tt in range(NTT):
                    def mm(e, tt=tt):
                        ins = None
                        for eh in range(2):
                            for jj, j in enumerate(grp):
                                hidx = (gi % 2) * 4 + jj
                                ins = e.matmul(ps[:, 3 + eh, :],
                                               hT[:, hidx, tt * 128:(tt + 1) * 128],
                                               ringBF[:, slotsB[j], eh * 512:(eh + 1) * 512],
                                               start=(jj == 0), stop=(jj == len(grp) - 1))
                        return ins
                    S.op("pe", mm,
                         reads=[("RBF", slotsB[j]) for j in grp] +
                               [("HT", (gi % 2) * 4 + jj) for jj in range(len(grp))],
                         writes=[("PS", 3), ("PS", 4)], dur=0.5 * len(grp))
                    xs = xtok[b][:, tt, :]
                    pv = ps[:, 3:5, :].rearrange("p a n -> p (a n)")
                    xk = [("X", b, tt, 0), ("X", b, tt, 1)]
                    if gi == 0:
                        S.op("dve", lambda e, xs=xs, pv=pv: e.scalar_tensor_tensor(
                            out=xs, in0=xs, scalar=ALPHA, in1=pv, op0=ALU.mult, op1=ALU.add),
                            reads=[("PS", 3), ("PS", 4)] + xk, writes=xk, dur=1.25)
                    else:
                        S.op("dve", lambda e, xs=xs, pv=pv: e.tensor_tensor(
                            out=xs, in0=xs, in1=pv, op=ALU.add),
                            reads=[("PS", 3), ("PS", 4)] + xk, writes=xk, dur=1.25)
                    if last:
                        ln_stats(0, b, tt)
                    yield

            ng = len(groups)
            for gi in range(ng):
                g_it = gu(gi)
                d_it = down(gi - 1, False) if gi >= 1 else iter(())
                g_alive = d_alive = True
                while g_alive or d_alive:
                    if g_alive:
                        try:
                            next(g_it)
                            yield
                        except StopIteration:
                            g_alive = False
                    if d_alive:
                        try:
                            next(d_it)
                            yield
                        except StopIteration:
                            d_alive = False
            par = load_ln(L, 1)
            yield from down(ng - 1, True)
            inline = yield "DECIDE"
            banks = (0, 1) if inline else (5, 6)
            yield from ln_finish(0, b, par, banks, do_tr,
                                 store_rows=(q * NT if store else None))

        def load_block(b, q):
            for tt in range(NTT):
                r0 = q * NT + tt * 128
                S.op("sp", lambda e, tt=tt, r0=r0: e.dma_start(out=xtok[b][:, tt, :], in_=x_d[r0:r0 + 128, :]),
                     writes=[("X", b, tt, 0), ("X", b, tt, 1)], dma=f"d_x{b}_{tt}")

        def conv_stage(b, L, q, first):
            cc = convc[L]
            hb = hbias[L]
            CK = ("CC", L)
            if first:
                load_block(b, q)
                for tt in range(NTT):
                    transposes(b, tt, (5, 6), True, False)
                    yield
            if q == 0:
                S.op("dve", lambda e: e.memset(s1b[:, :, 0:UP], 0.0),
                     writes=[("S1", c) for c in range(8)])
            else:
                S.op("dve", lambda e: e.tensor_copy(out=s1b[:, :, 0:UP], in_=uh[L][:]),
                     reads=[("UH", L)], writes=[("S1", c) for c in range(8)])
            S.op("sp", lambda e: e.dma_start(out=mixb[:, 0, :], in_=mixb_d[L, 0]),
                 writes=[("MB", 0)], dma="d_mb")
            par = load_ln(L, 0)

            def mkdiag(c):
                i = c % 2
                dst = dgb[:, i, :].rearrange("p (k j) -> p k j", k=KW)
                wv = cc[:, 16 + c * KW:16 + (c + 1) * KW]
                S.op("dve", lambda e: e.tensor_tensor(
                    out=dst, in0=ident.unsqueeze(1).to_broadcast([128, KW, 128]),
                    in1=wv.unsqueeze(2).to_broadcast([128, KW, 128]), op=ALU.mult),
                    reads=[CK, ("KT",)], writes=[("DG", i)], dur=4.3)

            def p1(c):
                sa = load_AM(win_d[L][c])

                def mm(e, off, bank):
                    ins = None
                    for k in range(8):
                        ins = e.matmul(ps[:, bank, :],
                                       ringAM[:, sa, off + k * 128:off + (k + 1) * 128],
                                       xT[b][:, k, :], start=(k == 0), stop=(k == 7))
                    return ins
                S.op("pe", lambda e: mm(e, 0, 5), reads=[("RAM", sa)] + xt_keys(b), writes=[("PS", 5)])
                S.op("pe", lambda e: mm(e, 1024, 6), reads=[("RAM", sa)] + xt_keys(b), writes=[("PS", 6)])
                qq = 0
                S.op("act", lambda e: e.activation(
                    out=vbh[:, qq, :], in_=ps[:, 5, :], func=AF.Identity,
                    bias=hb[:, c:c + 1], scale=0.5),
                    reads=[("PS", 5), ("HB", L)], writes=[("VB", qq)])
                S.op("act", lambda e: e.activation(
                    out=sgtM[:, qq, :], in_=ps[:, 6, :], func=AF.Tanh,
                    bias=hb[:, 8 + c:9 + c], scale=0.5),
                    reads=[("PS", 6), ("HB", L)], writes=[("SGM", qq)])
                S.op("dve", lambda e: e.scalar_tensor_tensor(
                    out=s1b[:, c, UP:UP + NT], in0=sgtM[:, qq, :], scalar=1.0, in1=vbh[:, qq, :],
                    op0=ALU.add, op1=ALU.mult),
                    reads=[("SGM", qq), ("VB", qq)], writes=[("S1", c)])

            def conv(c):
                i = c % 2

                def mm(e):
                    ins = None
                    for k in range(KW):
                        ins = e.matmul(ps[:, 7, :], dgb[:, i, k * 128:(k + 1) * 128],
                                       s1b[:, c, 2 + k:2 + k + NT], start=(k == 0), stop=(k == KW - 1))
                    return ins
                S.op("pe", mm, reads=[("S1", c), ("DG", i)], writes=[("PS", 7)], dur=7.0)
                S.op("act", lambda e: e.activation(
                    out=s2[:, c, :], in_=ps[:, 7, :], func=AF.Identity,
                    bias=cc[:, 264 + c:265 + c], scale=1.0),
                    reads=[("PS", 7), CK], writes=[("S2", c)])

            for c in range(8):
                mkdiag(c)
                p1(c)
                yield
                if c >= 1:
                    conv(c - 1)
                    yield
            if q < NQ - 1:
                S.op("dve", lambda e: e.tensor_copy(out=uh[L][:], in_=s1b[:, :, NT:NT + UP]),
                     reads=[("S1", c) for c in range(8)], writes=[("UH", L)])
            conv(7)
            yield
            sB = {}
            for kp in range(4):
                sB[(0, kp)] = load_BM(wout_d[L][0, kp])
            sq = [s1[:, 4 + i, 0:NT] for i in range(2)]
            mean_t = s1[:, 0, 0:NT]
            rstd_t = s1[:, 1, 0:NT]
            tmp_t = s1[:, 2, 0:NT]
            for c in range(8):
                i = c % 2
                S.op("act", lambda e, c=c, i=i: e.activation(out=sq[i], in_=s2[:, c, :], func=AF.Square),
                     reads=[("S2", c)], writes=[("S1", 4 + i)])

                def mm(e, c=c, i=i):
                    e.matmul(ps[:, 5, :], onesm, s2[:, c, :], start=(c == 0), stop=(c == 7))
                    return e.matmul(ps[:, 6, :], onesm, sq[i], start=(c == 0), stop=(c == 7))
                S.op("pe", mm, reads=[("S2", c), ("S1", 4 + i), ("KT",)], writes=[("PS", 5), ("PS", 6)])
                if c % 2 == 1:
                    yield
            for kp in range(4):
                S.op("pool", lambda e, kp=kp: e.dma_start(out=s1b[:, 4 + kp, 0:D], in_=wout_d[L][1, kp]),
                     writes=[("S1", 4 + kp)], dma=f"d_w1_{kp}")
            S.op("dve", lambda e: e.tensor_copy(out=mean_t, in_=ps[:, 5, :]),
                 reads=[("PS", 5)], writes=[("S1", 0)])
            S.op("dve", lambda e: e.tensor_tensor(out=tmp_t, in0=mean_t, in1=mean_t, op=ALU.mult),
                 reads=[("S1", 0)], writes=[("S1", 2)])
            S.op("dve", lambda e: e.tensor_tensor(out=tmp_t, in0=ps[:, 6, :], in1=tmp_t, op=ALU.subtract),
                 reads=[("PS", 6), ("S1", 2)], writes=[("S1", 2)])
            S.op("act", lambda e: e.activation(out=tmp_t, in_=tmp_t, func=AF.Sqrt, bias=EPS, scale=1.0),
                 reads=[("S1", 2)], writes=[("S1", 2)], dur=3.0)
            S.op("dve", lambda e: e.reciprocal(out=rstd_t, in_=tmp_t),
                 reads=[("S1", 2)], writes=[("S1", 1)])
            yield
            for c0 in range(0, 8, 2):
                for c in (c0, c0 + 1):
                    S.op("dve", lambda e, c=c: e.tensor_tensor(out=s2[:, c, :], in0=s2[:, c, :],
                                                               in1=mean_t, op=ALU.subtract),
                         reads=[("S2", c), ("S1", 0)], writes=[("S2", c)])
                for c in (c0, c0 + 1):
                    S.op("dve", lambda e, c=c: e.tensor_tensor(out=s2[:, c, :], in0=s2[:, c, :],
                                                               in1=rstd_t, op=ALU.mult),
                         reads=[("S2", c), ("S1", 1)], writes=[("S2", c)])
                for c in (c0, c0 + 1):
                    S.op("act", lambda e, c=c: e.activation(
                        out=sTm[:, c, :], in_=s2[:, c, :], func=AF.Silu,
                        bias=cc[:, 280 + c:281 + c], scale=cc[:, 272 + c:273 + c]),
                        reads=[("S2", c), CK], writes=[("STM", c)])
                yield
            for eh in range(2):
                for tt in range(NTT):
                    bank = 5 + flip("Mo", 3)

                    def mm(e, bank=bank, tt=tt, eh=eh):
                        ins = None
                        for c in range(8):
                            if eh == 0:
                                w = ringBM[:, sB[(0, c // 2)], (c % 2) * 512:(c % 2 + 1) * 512]
                            else:
                                w = s1b[:, 4 + c // 2, (c % 2) * 512:(c % 2 + 1) * 512]
                            ins = e.matmul(ps[:, bank, :], sTm[:, c, tt * 128:(tt + 1) * 128], w,
                                           start=(c == 0), stop=(c == 7))
                        return ins
                    wk = ([("RBM", sB[(0, kp)]) for kp in range(4)] if eh == 0
                          else [("S1", 4 + kp) for kp in range(4)])
                    S.op("pe", mm, reads=wk + [("STM", c) for c in range(8)], writes=[("PS", bank)])
                    xs = xtok[b][:, tt, eh * 512:(eh + 1) * 512]
                    S.op("dve", lambda e, xs=xs, bank=bank: e.scalar_tensor_tensor(
                        out=xs, in0=xs, scalar=ALPHA, in1=ps[:, bank, :], op0=ALU.mult, op1=ALU.add),
                        reads=[("PS", bank), ("X", b, tt, eh)], writes=[("X", b, tt, eh)])
                    S.op("dve", lambda e, xs=xs, eh=eh: e.tensor_tensor(
                        out=xs, in0=xs, in1=mixb[:, 0, eh * 512:(eh + 1) * 512], op=ALU.add),
                        reads=[("MB", 0), ("X", b, tt, eh)], writes=[("X", b, tt, eh)])
                    if eh == 1:
                        ln_stats(1, b, tt)
                    yield
            yield from ln_finish(1, b, par, (5, 6), True)

        def pool_stage(b, L, q, first):
            if first:
                load_block(b, q)
            for tt in range(NTT):
                transposes(b, tt, (5, 6), False, True)
                yield
            if q == 0:
                S.op("dve", lambda e: e.memset(s1[:, :, 0:PP], 0.0),
                     writes=[("S1", c) for c in range(8)])
            else:
                S.op("dve", lambda e: e.tensor_copy(out=s1[:, :, 0:PP], in_=xh[L][:]),
                     reads=[("XH", L)], writes=[("S1", c) for c in range(8)])
            if q < NQ - 1:
                S.op("dve", lambda e: e.tensor_copy(out=xh[L][:], in_=s1[:, :, NT:NT + PP]),
                     reads=[("S1", c) for c in range(8)], writes=[("XH", L)])
            S.op("sp", lambda e: e.dma_start(out=mixb[:, :, :],
                                             in_=mixb_d[L].rearrange("a p n -> p a n")),
                 writes=[("MB", 0), ("MB", 1)], dma="d_mb")
            S.op("dve", lambda e: e.tensor_tensor(out=mixb[:, 0, :], in0=mixb[:, 0, :],
                                                  in1=mixb[:, 1, :], op=ALU.mult),
                 reads=[("MB", 0), ("MB", 1)], writes=[("MB", 0)])
            sa = load_AM(wp_d[L])
            par = load_ln(L, 0)
            W = NT + PP
            for g in range(4):
                w = POOL_W[g]
                cs = (2 * g, 2 * g + 1)
                cur = {c: (s1[:, c, 0:W], [("S1", c)]) for c in cs}
                shift = 1
                for step in range(g + 1):
                    lo = 2 * shift - 1
                    for i, c in enumerate(cs):
                        r0 = 4 * i + 2 * (step % 2)
                        dst = s2[:, r0:r0 + 2, :].rearrange("p a n -> p (a n)")[:, 0:W]
                        dkeys = [("S2", r0), ("S2", r0 + 1)]
                        src, skeys = cur[c]
                        S.op("dve", lambda e, dst=dst, src=src, lo=lo, shift=shift: e.tensor_tensor(
                            out=dst[:, lo:W], in0=src[:, lo:W], in1=src[:, lo - shift:W - shift],
                            op=ALU.add),
                            reads=skeys, writes=dkeys)
                        cur[c] = (dst, dkeys)
                    shift *= 2
                for i, c in enumerate(cs):
                    src, skeys = cur[c]
                    S.op("dve", lambda e, c=c, src=src, w=w: e.scalar_tensor_tensor(
                        out=sTm[:, c, :], in0=src[:, PP:PP + NT], scalar=1.0 / w,
                        in1=s1[:, c, PP:PP + NT], op0=ALU.mult, op1=ALU.subtract),
                        reads=[("S1", c)] + skeys, writes=[("STM", c)])
                if q == 0:
                    for i, c in enumerate(cs):
                        src, skeys = cur[c]
                        S.op("dve", lambda e, i=i, src=src, w=w: e.tensor_tensor(
                            out=fx[:, i, 0:w - 1], in0=src[:, PP:PP + w - 1], in1=invc[:, 0:w - 1],
                            op=ALU.mult),
                            reads=skeys + [("KT",)], writes=[("FX", i)])
                    for i, c in enumerate(cs):
                        S.op("dve", lambda e, i=i, c=c, w=w: e.tensor_tensor(
                            out=sTm[:, c, 0:w - 1], in0=fx[:, i, 0:w - 1], in1=s1[:, c, PP:PP + w - 1],
                            op=ALU.subtract),
                            reads=[("FX", i), ("S1", c), ("STM", c)], writes=[("STM", c)])
                yield
            for tt in range(NTT):
                for gh in range(2):
                    bank = 5 + flip("Mo", 3)

                    def mm(e, bank=bank, tt=tt, gh=gh):
                        ins = None
                        for gg in range(2):
                            g = 2 * gh + gg
                            for kk in range(2):
                                off = (g * 2 + kk) * 256
                                ins = e.matmul(ps[:, bank, gg * 256:(gg + 1) * 256],
                                               sTm[:, 2 * g + kk, tt * 128:(tt + 1) * 128],
                                               ringAM[:, sa, off:off + 256],
                                               start=(kk == 0), stop=(kk == 1))
                        return ins
                    S.op("pe", mm, reads=[("RAM", sa)] + [("STM", c) for c in range(4 * gh, 4 * gh + 4)],
                         writes=[("PS", bank)], dur=0.6)
                    qq = 0
                    sl = slice(gh * 512, (gh + 1) * 512)
                    xs = xtok[b][:, tt, sl]
                    S.op("dve", lambda e, bank=bank, qq=qq, sl=sl: e.tensor_tensor(
                        out=sgtM[:, qq, :], in0=ps[:, bank, :], in1=mixb[:, 1, sl], op=ALU.mult),
                        reads=[("PS", bank), ("MB", 1)], writes=[("SGM", qq)])
                    S.op("dve", lambda e, xs=xs, qq=qq: e.scalar_tensor_tensor(
                        out=xs, in0=xs, scalar=ALPHA, in1=sgtM[:, qq, :], op0=ALU.mult, op1=ALU.add),
                        reads=[("SGM", qq), ("X", b, tt, gh)], writes=[("X", b, tt, gh)])
                    S.op("dve", lambda e, xs=xs, sl=sl: e.tensor_tensor(
                        out=xs, in0=xs, in1=mixb[:, 0, sl], op=ALU.add),
                        reads=[("MB", 0), ("X", b, tt, gh)], writes=[("X", b, tt, gh)])
                ln_stats(1, b, tt)
                yield
            yield from ln_finish(1, b, par, (5, 6), True)

        stages = []
        for pair in ((0, 1), (2, 3)):
            for li, L in enumerate(layers):
                for q in pair:
                    stages.append((q, li, L))

        def mk_mixer(k):
            q, li, L = stages[k]
            b = q % 2
            if L % 2 == 0:
                return conv_stage(b, L, q, li == 0)
            return pool_stage(b, L, q, li == 0)

        def mk_ffn(k):
            q, li, L = stages[k]
            b = q % 2
            is_last = (li == len(layers) - 1)
            nxt_conv = (not is_last) and (layers[li + 1] % 2 == 0)
            return ffn_stage(b, L, q, nxt_conv, is_last)

        def drain(g):
            for _ in g:
                pass

        class Stream:
            def __init__(self, g, eager=False):
                self.g = g
                self.fifo = []
                self.alive = True
                self.at_decide = False
                self.send = None
                if eager:
                    while self.alive:
                        self.pump()

            def pump(self):
                S.capture = self.fifo
                try:
                    if self.send is not None:
                        r = self.g.send(self.send)
                        self.send = None
                    else:
                        r = next(self.g)
                    if r == "DECIDE":
                        self.at_decide = True
                        self.alive = False
                except StopIteration:
                    self.alive = False
                S.capture = None

            def fill(self):
                while self.alive and not self.fifo:
                    self.pump()
                return bool(self.fifo)

            def pe_left(self):
                return sum((S.DEF_DUR["pe"] if d[5] is None else d[5]) for d in self.fifo if d[0] == "pe")

        def merge(streams, until=None):
            while True:
                live = [s for s in streams if s.fill()]
                if until is not None and not until.fill():
                    return
                if not live:
                    return
                def cost(s):
                    d = s.fifo[0]
                    st = S.estimate(d)
                    if d[0] == "pe":
                        st += PE_STALL_W * max(0.0, st - S.eng_free["pe"])
                    return st + _jit.uniform(0.0, JITTER_US)
                best = min(live, key=cost)
                S.commit(best.fifo.pop(0))

        TAIL_INLINE_US = 0.0
        JITTER_US = 0.6
        _jit = random.Random(JITTER_SEED)
        PE_STALL_W = 0.0
        drain(mk_mixer(0))
        ftail = None
        for k in range(len(stages)):
            fs = Stream(mk_ffn(k))
            if ftail is not None:
                merge([fs, ftail], until=ftail)
                ftail = None
            ms = Stream(mk_mixer(k + 1), eager=True) if k + 1 < len(stages) else None
            while True:
                merge([s for s in (fs, ms) if s is not None], until=fs)
                if fs.at_decide:
                    fs.at_decide = False
                    inline = ms is not None and ms.pe_left() > TAIL_INLINE_US
                    fs.send = bool(inline)
                    fs.alive = True
                    if not inline:
                        ftail = fs
                        break
                else:
                    break
            if ms is not None:
                merge([ms])
        if ftail is not None:
            merge([ftail])
        S.final_wait("sp", [("X", b, tt, eh) for b in range(2) for tt in range(NTT) for eh in range(2)])

        sems = {k: es.enter_context(nc.semaphore(f"s_{k}")) for k in sorted(S.semkeys)}
        block = es.enter_context(nc.Block())

        class FirstIns:
            def __init__(self, e):
                self.e = e
                self.first = None

            def matmul(self, *a, **k):
                r = self.e.matmul(*a, **k)
                if self.first is None:
                    self.first = r
                return r

            def transpose(self, *a, **k):
                r = self.e.transpose(*a, **k)
                if self.first is None:
                    self.first = r
                return r

        def run(name, e):
            for waits, fn, inc in S.ops[name]:
                waits = list(waits)
                fused = None
                if fn is not None and waits:
                    fused = waits.pop()
                for sk, val in waits:
                    e.wait_ge(sems[sk], val)
                if fn is not None:
                    if name == "pe":
                        px = FirstIns(e)
                        ins = fn(px)
                        first = px.first
                    else:
                        ins = fn(e)
                        first = ins
                    if fused is not None:
                        first._wait_ge(sems[fused[0]], fused[1])
                    ins.then_inc(sems[inc[0]], inc[1])

        @block.tensor
        def _(e):
            run("pe", e)

        @block.scalar
        def _(e):
            run("act", e)

        @block.vector
        def _(e):
            run("dve", e)

        @block.gpsimd
        def _(e):
            run("pool", e)

        @block.sync
        def _(e):
            run("sp", e)
    return nc


def _bc(v):
    return np.ascontiguousarray(np.broadcast_to(np.asarray(v, np.float32).reshape(1, -1), (128, D)))


def _prep(inputs, layers):
    f = lambda a: np.ascontiguousarray(np.asarray(a, dtype=np.float32))
    m = {}
    ktab = np.zeros((128, 272), np.float32)
    ktab[:, 0:128] = np.eye(128, dtype=np.float32)
    ktab[:, 128:256] = 1.0 / D
    ktab[:, 256:272] = (1.0 / np.arange(1, 17, dtype=np.float32))[None, :]
    m["ktab"] = ktab
    lnp = np.zeros((DEPTH, 4, 128, D), np.float32)
    mixb = np.zeros((DEPTH, 2, 128, D), np.float32)
    for L in range(DEPTH):
        lnp[L, 0] = _bc(inputs["ln1_g"][L])
        lnp[L, 1] = _bc(inputs["ln1_b"][L])
        lnp[L, 2] = _bc(inputs["ln2_g"][L])
        lnp[L, 3] = _bc(inputs["ln2_b"][L])
        l = L // 2
        if L % 2 == 0:
            mixb[L, 0] = _bc(inputs["a_b_out"][l])
        else:
            mixb[L, 0] = _bc(np.asarray(inputs["p_b"][l]).reshape(-1))
            mixb[L, 1] = _bc(inputs["p_scale"][l])
    m["lnp"] = lnp
    m["mixb"] = mixb
    for L in layers:
        l = L // 2
        wg = f(inputs["ffn_w_gate"][L]).reshape(8, 128, NJ, 128).transpose(2, 1, 0, 3)
        m[f"wg{L}"] = np.ascontiguousarray(wg).reshape(NJ, 128, 1024)
        wu = f(inputs["ffn_w_up"][L]).reshape(8, 128, NJ, 128).transpose(2, 1, 0, 3)
        m[f"wu{L}"] = np.ascontiguousarray(wu).reshape(NJ, 128, 1024)
        m[f"wd{L}"] = f(inputs["ffn_w_down"][L]).reshape(NJ, 128, D)
        if L % 2 == 0:
            win = f(inputs["a_w_in"][l]).reshape(8, 128, 2, 8, 128).transpose(3, 1, 2, 0, 4)
            m[f"win{L}"] = np.ascontiguousarray(win).reshape(8, 128, 2048)
            wo = f(inputs["a_w_out"][l]).reshape(4, 2, 128, 2, 512).transpose(3, 0, 2, 1, 4)
            m[f"wout{L}"] = np.ascontiguousarray(wo).reshape(2, 4, 128, 1024)
            cc = np.zeros((128, 288), np.float32)
            cc[:, 0:16] = f(inputs["a_b_in"][l]).reshape(16, 128).T
            wdw = f(inputs["a_w_dw"][l])[:, 0, :]
            cc[:, 16:264] = wdw.T.reshape(8, 128, KW).transpose(1, 0, 2).reshape(128, 8 * KW)
            cc[:, 264:272] = f(inputs["a_b_dw"][l]).reshape(8, 128).T
            cc[:, 272:280] = f(inputs["a_ln_g"][l]).reshape(8, 128).T
            cc[:, 280:288] = f(inputs["a_ln_b"][l]).reshape(8, 128).T
            m[f"convc{L}"] = cc
        else:
            wp = f(inputs["p_w"][l]).reshape(4, 2, 128, 256).transpose(2, 0, 1, 3)
            m[f"wp{L}"] = np.ascontiguousarray(wp).reshape(128, 2048)
    return m


_CACHE = {}


def _get_nc(layers):
    key = tuple(layers)
    if key not in _CACHE:
        _CACHE[key] = build(list(layers))
    return _CACHE[key]


def _run(inputs, x, layers):
    common = _prep(inputs, layers)
    nc = _get_nc(layers)
    in_maps = []
    for b in range(NB):
        mm = dict(common)
        mm["x"] = np.ascontiguousarray(x[b])
        in_maps.append(mm)
    res = run_bass_kernel_spmd(nc, in_maps, core_ids=list(range(NB)))
    return np.stack([np.asarray(r["y"], dtype=np.float32) for r in res.results], axis=0)


def kernel(**inputs):
    x = np.asarray(inputs["x"], dtype=np.float32)
    return _run(inputs, x, [0, 1, 2, 3])
```

```python
import contextlib
import random
import numpy as np
import concourse.bass as bass
import concourse.mybir as mybir
from concourse.bass_utils import run_bass_kernel_spmd

F32 = mybir.dt.float32
BF16 = mybir.dt.bfloat16
AF = mybir.ActivationFunctionType
ALU = mybir.AluOpType

D = 1024
SEQ = 2048
NB = 8
DEPTH = 4
DFF = 2816
NJ = DFF // 128
KW = 31
ALPHA = float((2 * DEPTH) ** 0.25)
EPS = 1e-5
NT = 512
NTT = NT // 128
NQ = SEQ // NT
G = 4
NA = 4
NBS = 12
NLN = 2
UP = 32
PP = 16
POOL_W = (2, 4, 8, 16)
JITTER_SEED = 1
SELF_WIN = 1000000000


class Sched:
    ENGS = ("pe", "act", "dve", "pool", "sp")
    DEF_DUR = {"pe": 2.0, "act": 0.7, "dve": 0.7, "pool": 0.65, "sp": 0.15}
    DMA_LAT = 3.0
    SEM_LAT = 0.35

    def __init__(self):
        self.ops = {e: [] for e in self.ENGS}
        self.cnt = {e: 0 for e in self.ENGS}
        self.dcnt = {}
        self.last_w = {}
        self.readers = {}
        self.seen = {e: {} for e in self.ENGS}
        self.semkeys = set(self.ENGS)
        self.capture = None
        self.eng_free = {e: 0.0 for e in self.ENGS}
        self.t_w = {}
        self.t_r = {}

    def op(self, eng, fn, reads=(), writes=(), dma=None, dur=None):
        desc = (eng, fn, tuple(reads), tuple(writes), dma, dur)
        if self.capture is not None:
            self.capture.append(desc)
            return None
        return self.commit(desc)

    def estimate(self, desc):
        eng, fn, reads, writes, dma, dur = desc
        ready = 0.0
        for r in reads:
            ready = max(ready, self.t_w.get(r, 0.0))
        for w in writes:
            ready = max(ready, self.t_w.get(w, 0.0), self.t_r.get(w, 0.0))
        return max(self.eng_free[eng], ready + self.SEM_LAT)

    def commit(self, desc):
        eng, fn, reads, writes, dma, dur = desc
        start = self.estimate(desc)
        d = self.DEF_DUR[eng] if dur is None else dur
        if dma is None:
            fin = start + d
            self.eng_free[eng] = fin
        else:
            self.eng_free[eng] = start + d
            fin = start + d + self.DMA_LAT
        for r in reads:
            self.t_r[r] = max(self.t_r.get(r, 0.0), fin)
        for w in writes:
            self.t_w[w] = fin
            self.t_r[w] = 0.0
        idx = len(self.ops[eng])
        deps = []
        for r in reads:
            if r in self.last_w:
                deps.append(self.last_w[r])
        for w in writes:
            if w in self.last_w:
                deps.append(self.last_w[w])
            deps.extend(self.readers.get(w, ()))
        need = {}
        for (sk, val, peng, pidx, pdma) in deps:
            if (not pdma) and peng == eng and dma is None:
                if pidx < idx - SELF_WIN or eng == "pe":
                    continue
            if self.seen[eng].get(sk, 0) >= val:
                continue
            if need.get(sk, 0) < val:
                need[sk] = val
        for sk, val in need.items():
            self.seen[eng][sk] = val
        if dma is None:
            self.cnt[eng] += 1
            tick = (eng, self.cnt[eng], eng, idx, False)
            inc = (eng, 1)
        else:
            self.semkeys.add(dma)
            self.dcnt[dma] = self.dcnt.get(dma, 0) + 1
            tick = (dma, 16 * self.dcnt[dma], eng, idx, True)
            inc = (dma, 16)
        for r in reads:
            self.readers.setdefault(r, []).append(tick)
        for w in writes:
            self.last_w[w] = tick
            self.readers[w] = []
        self.ops[eng].append((list(need.items()), fn, inc))
        return tick

    def final_wait(self, eng, keys):
        need = {}
        for k in keys:
            t = self.last_w.get(k)
            cands = list(self.readers.get(k, ()))
            if t is not None:
                cands.append(t)
            for (sk, val, *_r) in cands:
                if need.get(sk, 0) < val:
                    need[sk] = val
        self.ops[eng].append((list(need.items()), None, None))


def build(layers):
    nc = bass.Bass("TRN2", target_bir_lowering=False)
    conv_layers = [L for L in layers if L % 2 == 0]
    pool_layers = [L for L in layers if L % 2 == 1]

    def dram(name, shape, kind="ExternalInput"):
        return nc.dram_tensor(name, list(shape), F32, kind=kind).ap()

    x_d = dram("x", [SEQ, D])
    y_d = dram("y", [SEQ, D], kind="ExternalOutput")
    ktab_d = dram("ktab", [128, 272])
    lnp_d = dram("lnp", [DEPTH, 4, 128, D])
    mixb_d = dram("mixb", [DEPTH, 2, 128, D])
    wg_d = {L: dram(f"wg{L}", [NJ, 128, 8 * 128]) for L in layers}
    wu_d = {L: dram(f"wu{L}", [NJ, 128, 8 * 128]) for L in layers}
    wd_d = {L: dram(f"wd{L}", [NJ, 128, D]) for L in layers}
    win_d = {L: dram(f"win{L}", [8, 128, 2 * 8 * 128]) for L in conv_layers}
    wout_d = {L: dram(f"wout{L}", [2, 4, 128, D]) for L in conv_layers}
    convc_d = {L: dram(f"convc{L}", [128, 288]) for L in conv_layers}
    wp_d = {L: dram(f"wp{L}", [128, 8 * 256]) for L in pool_layers}

    S = Sched()
    es = contextlib.ExitStack()
    with es:
        def sb(name, shape, dt=F32):
            return es.enter_context(nc.sbuf_tensor("sb_" + name, list(shape), dt))

        xtok = [sb(f"xtok{b}", [128, NTT, D]) for b in range(2)]
        xT = [sb(f"xT{b}", [128, 8, NT], BF16) for b in range(2)]
        s1 = sb("s1", [128, 8, NT + UP])
        s2 = sb("s2", [128, 8, NT])
        dgb = sb("dgb", [128, 2, KW * 128], BF16)
        sTm = sb("sTm", [128, 8, NT], BF16)
        hT = sb("hT", [128, 8, NT], BF16)
        NAF, NAM, NBF, NBM = 4, 2, 12, 4
        ringAF = sb("ringAF", [128, NAF, 2048], BF16)
        ringAM = sb("ringAM", [128, NAM, 2048], BF16)
        ringBF = sb("ringBF", [128, NBF, D], BF16)
        ringBM = sb("ringBM", [128, NBM, D], BF16)
        lnb = sb("lnb", [128, NLN, 2, D])
        mixb = sb("mixb", [128, 2, D])
        sgtF = sb("sgtF", [128, 2, 512])
        sgtM = sb("sgtM", [128, 1, 512])
        vbh = sb("vbh", [128, 1, 512])
        ktab = sb("ktab", [128, 272])
        convc = {L: sb(f"convc{L}", [128, 288]) for L in conv_layers}
        hbias = {L: sb(f"hb{L}", [128, 16]) for L in conv_layers}
        uh = {L: sb(f"uh{L}", [128, 8, UP], BF16) for L in conv_layers}
        xh = {L: sb(f"xh{L}", [128, 8, PP]) for L in pool_layers}
        stt = sb("stt", [128, 2, NTT, 2, 6])
        mv = sb("mv", [128, 2, NTT, 2])
        rs = sb("rs", [128, 2, 2, NTT])
        fx = sb("fx", [128, 2, 16])
        ps = es.enter_context(nc.psum_tensor("ps", [128, 8, 512], F32))

        ident = ktab[:, 0:128]
        onesm = ktab[:, 128:256]
        invc = ktab[:, 256:272]
        s1b = s1[:].bitcast(BF16)

        S.op("sp", lambda e: e.dma_start(out=ktab[:], in_=ktab_d), writes=[("KT",)], dma="d_kt")
        for L in conv_layers:
            S.op("sp", lambda e, L=L: e.dma_start(out=convc[L][:], in_=convc_d[L]),
                 writes=[("CC", L)], dma=f"d_cc{L}")
            S.op("dve", lambda e, L=L: e.tensor_scalar(
                out=hbias[L][:], in0=convc[L][:, 0:16], scalar1=0.5, scalar2=None, op0=ALU.mult),
                reads=[("CC", L)], writes=[("HB", L)])

        ra_n = [0]
        rb_n = [0]
        ln_n = [0]
        pp_n = {}

        def flip(k, n=2):
            v = pp_n.get(k, 0)
            pp_n[k] = v + 1
            return v % n

        def xt_keys(b):
            return [("XT", b, tt, kh) for tt in range(NTT) for kh in range(2)]

        rn = {"AF": 0, "AM": 0, "BF": 0, "BM": 0}

        def load_AM(src_ap):
            slot = rn["AM"] % NAM
            rn["AM"] += 1
            S.op("pool", lambda e: e.dma_start(out=ringAM[:, slot, :], in_=src_ap),
                 writes=[("RAM", slot)], dma=f"d_ram{slot}")
            return slot

        def load_AF2(src0, src1):
            slot = rn["AF"] % NAF
            rn["AF"] += 1
            S.op("pool", lambda e: e.dma_start(out=ringAF[:, slot, 0:1024], in_=src0),
                 writes=[("RAF", slot, 0)], dma=f"d_rafg{slot}")
            S.op("pool", lambda e: e.dma_start(out=ringAF[:, slot, 1024:2048], in_=src1),
                 writes=[("RAF", slot, 1)], dma=f"d_rafu{slot}")
            return slot

        def load_BF(src_ap):
            slot = rn["BF"] % NBF
            rn["BF"] += 1
            S.op("pool", lambda e: e.dma_start(out=ringBF[:, slot, :], in_=src_ap),
                 writes=[("RBF", slot)], dma=f"d_rbf{slot}")
            return slot

        def load_BM(src_ap):
            slot = rn["BM"] % NBM
            rn["BM"] += 1
            S.op("pool", lambda e: e.dma_start(out=ringBM[:, slot, :], in_=src_ap),
                 writes=[("RBM", slot)], dma=f"d_rbm{slot}")
            return slot

        def load_ln(L, which):
            par = 0 if which == 1 else 1
            src = lnp_d[L, 2 * which:2 * which + 2].rearrange("a p n -> p a n")
            S.op("sp", lambda e: e.dma_start(out=lnb[:, par, :, :], in_=src),
                 writes=[("LNB", par)], dma=f"d_ln{par}")
            return par

        def transposes(b, tt, banks, to_xT, to_s1):
            for kh in range(2):
                bank = banks[kh]

                def pe_fn(e, bank=bank, kh=kh):
                    ins = None
                    for q in range(4):
                        k = kh * 4 + q
                        ins = e.transpose(out=ps[:, bank, q * 128:(q + 1) * 128],
                                          in_=xtok[b][:, tt, k * 128:(k + 1) * 128],
                                          identity=ident)
                    return ins
                S.op("pe", pe_fn, reads=[("X", b, tt, kh), ("KT",)], writes=[("PS", bank)], dur=0.5)
                if to_xT:
                    S.op("act", lambda e, bank=bank, kh=kh: e.activation(
                        out=xT[b][:, kh * 4:(kh + 1) * 4, tt * 128:(tt + 1) * 128],
                        in_=ps[:, bank, :].rearrange("p (a b) -> p a b", a=4),
                        func=AF.Copy),
                        reads=[("PS", bank)], writes=[("XT", b, tt, kh)])
                if to_s1:
                    S.op("act", lambda e, bank=bank, kh=kh: e.activation(
                        out=s1[:, kh * 4:(kh + 1) * 4, PP + tt * 128:PP + (tt + 1) * 128],
                        in_=ps[:, bank, :].rearrange("p (a b) -> p a b", a=4), func=AF.Copy),
                        reads=[("PS", bank)],
                        writes=[("S1", c) for c in range(kh * 4, kh * 4 + 4)])

        def ln_stats(st, b, tt):
            for eh in range(2):
                S.op("dve", lambda e, eh=eh: e.bn_stats(
                    out=stt[:, st, tt, eh, :], in_=xtok[b][:, tt, eh * 512:(eh + 1) * 512]),
                    reads=[("X", b, tt, eh)], writes=[("STT", st, tt, eh)])
            S.op("dve", lambda e: e.bn_aggr(
                out=mv[:, st, tt, :], in_=stt[:, st, tt, :, :].rearrange("p a b -> p (a b)")),
                reads=[("STT", st, tt, 0), ("STT", st, tt, 1)], writes=[("MV", st, tt)], dur=0.25)

        def ln_finish(st, b, par, banks, do_tr, store_rows=None):
            S.op("act", lambda e: e.activation(out=rs[:, st, 0, :], in_=mv[:, st, :, 1], func=AF.Sqrt,
                                               bias=EPS, scale=1.0),
                 reads=[("MV", st, tt) for tt in range(NTT)], writes=[("RS", st, 0)], dur=3.0)
            S.op("dve", lambda e: e.reciprocal(out=rs[:, st, 1, :], in_=rs[:, st, 0, :]),
                 reads=[("RS", st, 0)], writes=[("RS", st, 1)], dur=0.2)
            yield
            for tt in range(NTT):
                xs = xtok[b][:, tt, :]
                xk = [("X", b, tt, 0), ("X", b, tt, 1)]
                S.op("dve", lambda e, tt=tt, xs=xs: e.scalar_tensor_tensor(
                    out=xs, in0=xs, scalar=mv[:, st, tt, 0:1],
                    in1=lnb[:, par, 0, :], op0=ALU.subtract, op1=ALU.mult),
                    reads=xk + [("MV", st, tt), ("LNB", par)], writes=xk, dur=1.25)
                S.op("dve", lambda e, tt=tt, xs=xs: e.scalar_tensor_tensor(
                    out=xs, in0=xs, scalar=rs[:, st, 1, tt:tt + 1],
                    in1=lnb[:, par, 1, :], op0=ALU.mult, op1=ALU.add),
                    reads=xk + [("RS", st, 1), ("LNB", par)], writes=xk, dur=1.25)
                if do_tr:
                    transposes(b, tt, banks, True, False)
                if store_rows is not None:
                    r0 = store_rows + tt * 128
                    S.op("sp", lambda e, tt=tt, r0=r0: e.dma_start(
                        out=y_d[r0:r0 + 128, :], in_=xtok[b][:, tt, :]),
                        reads=xk, dma=f"d_st{b}_{tt}")
                yield

        def ffn_stage(b, L, q, do_tr, store):
            groups = []
            j = 0
            first = NJ % G if NJ % G else G
            while j < NJ:
                gs = first if j == 0 else G
                groups.append(list(range(j, j + gs)))
                j += gs
            slotsB = {}

            def gu(gi):
                for jj, j in enumerate(groups[gi]):
                    sa = load_AF2(wg_d[L][j], wu_d[L][j])
                    slotsB[j] = load_BF(wd_d[L][j])
                    hidx = (gi % 2) * 4 + jj
                    bg = flip("Fg")
                    bu = 2

                    def mm(e, off, bank, sa=sa):
                        ins = None
                        for k in range(8):
                            ins = e.matmul(ps[:, bank, :],
                                           ringAF[:, sa, off + k * 128:off + (k + 1) * 128],
                                           xT[b][:, k, :], start=(k == 0), stop=(k == 7))
                        return ins
                    S.op("pe", lambda e, mm=mm, bg=bg: mm(e, 0, bg),
                         reads=[("RAF", sa, 0)] + xt_keys(b), writes=[("PS", bg)])
                    S.op("pe", lambda e, mm=mm, bu=bu: mm(e, 1024, bu),
                         reads=[("RAF", sa, 1)] + xt_keys(b), writes=[("PS", bu)])
                    qq = flip("Fs")
                    S.op("act", lambda e, bg=bg, qq=qq: e.activation(
                        out=sgtF[:, qq, :], in_=ps[:, bg, :], func=AF.Silu),
                        reads=[("PS", bg)], writes=[("SGF", qq)])
                    S.op("dve", lambda e, bu=bu, qq=qq, hidx=hidx: e.tensor_tensor(
                        out=hT[:, hidx, :], in0=ps[:, bu, :], in1=sgtF[:, qq, :], op=ALU.mult),
                        reads=[("PS", bu), ("SGF", qq)], writes=[("HT", hidx)])
                    yield

            def down(gi, last):
                grp = groups[gi]
                for tt in range(NTT):
                    def mm(e, tt=tt):
                        ins = None
                        for eh in range(2):
                            for jj, j in enumerate(grp):
                                hidx = (gi % 2) * 4 + jj
                                ins = e.matmul(ps[:, 3 + eh, :],
                                               hT[:, hidx, tt * 128:(tt + 1) * 128],
                                               ringBF[:, slotsB[j], eh * 512:(eh + 1) * 512],
                                               start=(jj == 0), stop=(jj == len(grp) - 1))
                        return ins
                    S.op("pe", mm,
                         reads=[("RBF", slotsB[j]) for j in grp] +
                               [("HT", (gi % 2) * 4 + jj) for jj in range(len(grp))],
                         writes=[("PS", 3), ("PS", 4)], dur=0.5 * len(grp))
                    xs = xtok[b][:, tt, :]
                    pv = ps[:, 3:5, :].rearrange("p a n -> p (a n)")
                    xk = [("X", b, tt, 0), ("X", b, tt, 1)]
                    if gi == 0:
                        S.op("dve", lambda e, xs=xs, pv=pv: e.scalar_tensor_tensor(
                            out=xs, in0=xs, scalar=ALPHA, in1=pv, op0=ALU.mult, op1=ALU.add),
                            reads=[("PS", 3), ("PS", 4)] + xk, writes=xk, dur=1.25)
                    else:
                        S.op("dve", lambda e, xs=xs, pv=pv: e.tensor_tensor(
                            out=xs, in0=xs, in1=pv, op=ALU.add),
                            reads=[("PS", 3), ("PS", 4)] + xk, writes=xk, dur=1.25)
                    if last:
                        ln_stats(0, b, tt)
                    yield

            ng = len(groups)
            for gi in range(ng):
                g_it = gu(gi)
                d_it = down(gi - 1, False) if gi >= 1 else iter(())
                g_alive = d_alive = True
                while g_alive or d_alive:
                    if g_alive:
                        try:
                            next(g_it)
                            yield
                        except StopIteration:
                            g_alive = False
                    if d_alive:
                        try:
                            next(d_it)
                            yield
                        except StopIteration:
                            d_alive = False
            par = load_ln(L, 1)
            yield from down(ng - 1, True)
            inline = yield "DECIDE"
            banks = (0, 1) if inline else (5, 6)
            yield from ln_finish(0, b, par, banks, do_tr,
                                 store_rows=(q * NT if store else None))

        def load_block(b, q):
            for tt in range(NTT):
                r0 = q * NT + tt * 128
                S.op("sp", lambda e, tt=tt, r0=r0: e.dma_start(out=xtok[b][:, tt, :], in_=x_d[r0:r0 + 128, :]),
                     writes=[("X", b, tt, 0), ("X", b, tt, 1)], dma=f"d_x{b}_{tt}")

        def conv_stage(b, L, q, first):
            cc = convc[L]
            hb = hbias[L]
            CK = ("CC", L)
            if first:
                load_block(b, q)
                for tt in range(NTT):
                    transposes(b, tt, (5, 6), True, False)
                    yield
            if q == 0:
                S.op("dve", lambda e: e.memset(s1b[:, :, 0:UP], 0.0),
                     writes=[("S1", c) for c in range(8)])
            else:
                S.op("dve", lambda e: e.tensor_copy(out=s1b[:, :, 0:UP], in_=uh[L][:]),
                     reads=[("UH", L)], writes=[("S1", c) for c in range(8)])
            S.op("sp", lambda e: e.dma_start(out=mixb[:, 0, :], in_=mixb_d[L, 0]),
                 writes=[("MB", 0)], dma="d_mb")
            par = load_ln(L, 0)

            def mkdiag(c):
                i = c % 2
                dst = dgb[:, i, :].rearrange("p (k j) -> p k j", k=KW)
                wv = cc[:, 16 + c * KW:16 + (c + 1) * KW]
                S.op("dve", lambda e: e.tensor_tensor(
                    out=dst, in0=ident.unsqueeze(1).to_broadcast([128, KW, 128]),
                    in1=wv.unsqueeze(2).to_broadcast([128, KW, 128]), op=ALU.mult),
                    reads=[CK, ("KT",)], writes=[("DG", i)], dur=4.3)

            def p1(c):
                sa = load_AM(win_d[L][c])

                def mm(e, off, bank):
                    ins = None
                    for k in range(8):
                        ins = e.matmul(ps[:, bank, :],
                                       ringAM[:, sa, off + k * 128:off + (k + 1) * 128],
                                       xT[b][:, k, :], start=(k == 0), stop=(k == 7))
                    return ins
                S.op("pe", lambda e: mm(e, 0, 5), reads=[("RAM", sa)] + xt_keys(b), writes=[("PS", 5)])
                S.op("pe", lambda e: mm(e, 1024, 6), reads=[("RAM", sa)] + xt_keys(b), writes=[("PS", 6)])
                qq = 0
                S.op("act", lambda e: e.activation(
                    out=vbh[:, qq, :], in_=ps[:, 5, :], func=AF.Identity,
                    bias=hb[:, c:c + 1], scale=0.5),
                    reads=[("PS", 5), ("HB", L)], writes=[("VB", qq)])
                S.op("act", lambda e: e.activation(
                    out=sgtM[:, qq, :], in_=ps[:, 6, :], func=AF.Tanh,
                    bias=hb[:, 8 + c:9 + c], scale=0.5),
                    reads=[("PS", 6), ("HB", L)], writes=[("SGM", qq)])
                S.op("dve", lambda e: e.scalar_tensor_tensor(
                    out=s1b[:, c, UP:UP + NT], in0=sgtM[:, qq, :], scalar=1.0, in1=vbh[:, qq, :],
                    op0=ALU.add, op1=ALU.mult),
                    reads=[("SGM", qq), ("VB", qq)], writes=[("S1", c)])

            def conv(c):
                i = c % 2

                def mm(e):
                    ins = None
                    for k in range(KW):
                        ins = e.matmul(ps[:, 7, :], dgb[:, i, k * 128:(k + 1) * 128],
                                       s1b[:, c, 2 + k:2 + k + NT], start=(k == 0), stop=(k == KW - 1))
                    return ins
                S.op("pe", mm, reads=[("S1", c), ("DG", i)], writes=[("PS", 7)], dur=7.0)
                S.op("act", lambda e: e.activation(
                    out=s2[:, c, :], in_=ps[:, 7, :], func=AF.Identity,
                    bias=cc[:, 264 + c:265 + c], scale=1.0),
                    reads=[("PS", 7), CK], writes=[("S2", c)])

            for c in range(8):
                mkdiag(c)
                p1(c)
                yield
                if c >= 1:
                    conv(c - 1)
                    yield
            if q < NQ - 1:
                S.op("dve", lambda e: e.tensor_copy(out=uh[L][:], in_=s1b[:, :, NT:NT + UP]),
                     reads=[("S1", c) for c in range(8)], writes=[("UH", L)])
            conv(7)
            yield
            sB = {}
            for kp in range(4):
                sB[(0, kp)] = load_BM(wout_d[L][0, kp])
            sq = [s1[:, 4 + i, 0:NT] for i in range(2)]
            mean_t = s1[:, 0, 0:NT]
            rstd_t = s1[:, 1, 0:NT]
            tmp_t = s1[:, 2, 0:NT]
            for c in range(8):
                i = c % 2
                S.op("act", lambda e, c=c, i=i: e.activation(out=sq[i], in_=s2[:, c, :], func=AF.Square),
                     reads=[("S2", c)], writes=[("S1", 4 + i)])

                def mm(e, c=c, i=i):
                    e.matmul(ps[:, 5, :], onesm, s2[:, c, :], start=(c == 0), stop=(c == 7))
                    return e.matmul(ps[:, 6, :], onesm, sq[i], start=(c == 0), stop=(c == 7))
                S.op("pe", mm, reads=[("S2", c), ("S1", 4 + i), ("KT",)], writes=[("PS", 5), ("PS", 6)])
                if c % 2 == 1:
                    yield
            for kp in range(4):
                S.op("pool", lambda e, kp=kp: e.dma_start(out=s1b[:, 4 + kp, 0:D], in_=wout_d[L][1, kp]),
                     writes=[("S1", 4 + kp)], dma=f"d_w1_{kp}")
            S.op("dve", lambda e: e.tensor_copy(out=mean_t, in_=ps[:, 5, :]),
                 reads=[("PS", 5)], writes=[("S1", 0)])
            S.op("dve", lambda e: e.tensor_tensor(out=tmp_t, in0=mean_t, in1=mean_t, op=ALU.mult),
                 reads=[("S1", 0)], writes=[("S1", 2)])
            S.op("dve", lambda e: e.tensor_tensor(out=tmp_t, in0=ps[:, 6, :], in1=tmp_t, op=ALU.subtract),
                 reads=[("PS", 6), ("S1", 2)], writes=[("S1", 2)])
            S.op("act", lambda e: e.activation(out=tmp_t, in_=tmp_t, func=AF.Sqrt, bias=EPS, scale=1.0),
                 reads=[("S1", 2)], writes=[("S1", 2)], dur=3.0)
            S.op("dve", lambda e: e.reciprocal(out=rstd_t, in_=tmp_t),
                 reads=[("S1", 2)], writes=[("S1", 1)])
            yield
            for c0 in range(0, 8, 2):
                for c in (c0, c0 + 1):
                    S.op("dve", lambda e, c=c: e.tensor_tensor(out=s2[:, c, :], in0=s2[:, c, :],
                                                               in1=mean_t, op=ALU.subtract),
                         reads=[("S2", c), ("S1", 0)], writes=[("S2", c)])
                for c in (c0, c0 + 1):
                    S.op("dve", lambda e, c=c: e.tensor_tensor(out=s2[:, c, :], in0=s2[:, c, :],
                                                               in1=rstd_t, op=ALU.mult),
                         reads=[("S2", c), ("S1", 1)], writes=[("S2", c)])
                for c in (c0, c0 + 1):
                    S.op("act", lambda e, c=c: e.activation(
                        out=sTm[:, c, :], in_=s2[:, c, :], func=AF.Silu,
                        bias=cc[:, 280 + c:281 + c], scale=cc[:, 272 + c:273 + c]),
                        reads=[("S2", c), CK], writes=[("STM", c)])
                yield
            for eh in range(2):
                for tt in range(NTT):
                    bank = 5 + flip("Mo", 3)

                    def mm(e, bank=bank, tt=tt, eh=eh):
                        ins = None
                        for c in range(8):
                            if eh == 0:
                                w = ringBM[:, sB[(0, c // 2)], (c % 2) * 512:(c % 2 + 1) * 512]
                            else:
                                w = s1b[:, 4 + c // 2, (c % 2) * 512:(c % 2 + 1) * 512]
                            ins = e.matmul(ps[:, bank, :], sTm[:, c, tt * 128:(tt + 1) * 128], w,
                                           start=(c == 0), stop=(c == 7))
                        return ins
                    wk = ([("RBM", sB[(0, kp)]) for kp in range(4)] if eh == 0
                          else [("S1", 4 + kp) for kp in range(4)])
                    S.op("pe", mm, reads=wk + [("STM", c) for c in range(8)], writes=[("PS", bank)])
                    xs = xtok[b][:, tt, eh * 512:(eh + 1) * 512]
                    S.op("dve", lambda e, xs=xs, bank=bank: e.scalar_tensor_tensor(
                        out=xs, in0=xs, scalar=ALPHA, in1=ps[:, bank, :], op0=ALU.mult, op1=ALU.add),
                        reads=[("PS", bank), ("X", b, tt, eh)], writes=[("X", b, tt, eh)])
                    S.op("dve", lambda e, xs=xs, eh=eh: e.tensor_tensor(
                        out=xs, in0=xs, in1=mixb[:, 0, eh * 512:(eh + 1) * 512], op=ALU.add),
                        reads=[("MB", 0), ("X", b, tt, eh)], writes=[("X", b, tt, eh)])
                    if eh == 1:
                        ln_stats(1, b, tt)
                    yield
            yield from ln_finish(1, b, par, (5, 6), True)

        def pool_stage(b, L, q, first):
            if first:
                load_block(b, q)
            for tt in range(NTT):
                transposes(b, tt, (5, 6), False, True)
                yield
            if q == 0:
                S.op("dve", lambda e: e.memset(s1[:, :, 0:PP], 0.0),
                     writes=[("S1", c) for c in range(8)])
            else:
                S.op("dve", lambda e: e.tensor_copy(out=s1[:, :, 0:PP], in_=xh[L][:]),
                     reads=[("XH", L)], writes=[("S1", c) for c in range(8)])
            if q < NQ - 1:
                S.op("dve", lambda e: e.tensor_copy(out=xh[L][:], in_=s1[:, :, NT:NT + PP]),
                     reads=[("S1", c) for c in range(8)], writes=[("XH", L)])
            S.op("sp", lambda e: e.dma_start(out=mixb[:, :, :],
                                             in_=mixb_d[L].rearrange("a p n -> p a n")),
                 writes=[("MB", 0), ("MB", 1)], dma="d_mb")
            S.op("dve", lambda e: e.tensor_tensor(out=mixb[:, 0, :], in0=mixb[:, 0, :],
                                                  in1=mixb[:, 1, :], op=ALU.mult),
                 reads=[("MB", 0), ("MB", 1)], writes=[("MB", 0)])
            sa = load_AM(wp_d[L])
            par = load_ln(L, 0)
            W = NT + PP
            for g in range(4):
                w = POOL_W[g]
                cs = (2 * g, 2 * g + 1)
                cur = {c: (s1[:, c, 0:W], [("S1", c)]) for c in cs}
                shift = 1
                for step in range(g + 1):
                    lo = 2 * shift - 1
                    for i, c in enumerate(cs):
                        r0 = 4 * i + 2 * (step % 2)
                        dst = s2[:, r0:r0 + 2, :].rearrange("p a n -> p (a n)")[:, 0:W]
                        dkeys = [("S2", r0), ("S2", r0 + 1)]
                        src, skeys = cur[c]
                        S.op("dve", lambda e, dst=dst, src=src, lo=lo, shift=shift: e.tensor_tensor(
                            out=dst[:, lo:W], in0=src[:, lo:W], in1=src[:, lo - shift:W - shift],
                            op=ALU.add),
                            reads=skeys, writes=dkeys)
                        cur[c] = (dst, dkeys)
                    shift *= 2
                for i, c in enumerate(cs):
                    src, skeys = cur[c]
                    S.op("dve", lambda e, c=c, src=src, w=w: e.scalar_tensor_tensor(
                        out=sTm[:, c, :], in0=src[:, PP:PP + NT], scalar=1.0 / w,
                        in1=s1[:, c, PP:PP + NT], op0=ALU.mult, op1=ALU.subtract),
                        reads=[("S1", c)] + skeys, writes=[("STM", c)])
                if q == 0:
                    for i, c in enumerate(cs):
                        src, skeys = cur[c]
                        S.op("dve", lambda e, i=i, src=src, w=w: e.tensor_tensor(
                            out=fx[:, i, 0:w - 1], in0=src[:, PP:PP + w - 1], in1=invc[:, 0:w - 1],
                            op=ALU.mult),
                            reads=skeys + [("KT",)], writes=[("FX", i)])
                    for i, c in enumerate(cs):
                        S.op("dve", lambda e, i=i, c=c, w=w: e.tensor_tensor(
                            out=sTm[:, c, 0:w - 1], in0=fx[:, i, 0:w - 1], in1=s1[:, c, PP:PP + w - 1],
                            op=ALU.subtract),
                            reads=[("FX", i), ("S1", c), ("STM", c)], writes=[("STM", c)])
                yield
            for tt in range(NTT):
                for gh in range(2):
                    bank = 5 + flip("Mo", 3)

                    def mm(e, bank=bank, tt=tt, gh=gh):
                        ins = None
                        for gg in range(2):
                            g = 2 * gh + gg
                            for kk in range(2):
                                off = (g * 2 + kk) * 256
                                ins = e.matmul(ps[:, bank, gg * 256:(gg + 1) * 256],
                                               sTm[:, 2 * g + kk, tt * 128:(tt + 1) * 128],
                                               ringAM[:, sa, off:off + 256],
                                               start=(kk == 0), stop=(kk == 1))
                        return ins
                    S.op("pe", mm, reads=[("RAM", sa)] + [("STM", c) for c in range(4 * gh, 4 * gh + 4)],
                         writes=[("PS", bank)], dur=0.6)
                    qq = 0
                    sl = slice(gh * 512, (gh + 1) * 512)
                    xs = xtok[b][:, tt, sl]
                    S.op("dve", lambda e, bank=bank, qq=qq, sl=sl: e.tensor_tensor(
                        out=sgtM[:, qq, :], in0=ps[:, bank, :], in1=mixb[:, 1, sl], op=ALU.mult),
                        reads=[("PS", bank), ("MB", 1)], writes=[("SGM", qq)])
                    S.op("dve", lambda e, xs=xs, qq=qq: e.scalar_tensor_tensor(
                        out=xs, in0=xs, scalar=ALPHA, in1=sgtM[:, qq, :], op0=ALU.mult, op1=ALU.add),
                        reads=[("SGM", qq), ("X", b, tt, gh)], writes=[("X", b, tt, gh)])
                    S.op("dve", lambda e, xs=xs, sl=sl: e.tensor_tensor(
                        out=xs, in0=xs, in1=mixb[:, 0, sl], op=ALU.add),
                        reads=[("MB", 0), ("X", b, tt, gh)], writes=[("X", b, tt, gh)])
                ln_stats(1, b, tt)
                yield
            yield from ln_finish(1, b, par, (5, 6), True)

        stages = []
        for pair in ((0, 1), (2, 3)):
            for li, L in enumerate(layers):
                for q in pair:
                    stages.append((q, li, L))

        def mk_mixer(k):
            q, li, L = stages[k]
            b = q % 2
            if L % 2 == 0:
                return conv_stage(b, L, q, li == 0)
            return pool_stage(b, L, q, li == 0)

        def mk_ffn(k):
            q, li, L = stages[k]
            b = q % 2
            is_last = (li == len(layers) - 1)
            nxt_conv = (not is_last) and (layers[li + 1] % 2 == 0)
            return ffn_stage(b, L, q, nxt_conv, is_last)

        def drain(g):
            for _ in g:
                pass

        class Stream:
            def __init__(self, g, eager=False):
                self.g = g
                self.fifo = []
                self.alive = True
                self.at_decide = False
                self.send = None
                if eager:
                    while self.alive:
                        self.pump()

            def pump(self):
                S.capture = self.fifo
                try:
                    if self.send is not None:
                        r = self.g.send(self.send)
                        self.send = None
                    else:
                        r = next(self.g)
                    if r == "DECIDE":
                        self.at_decide = True
                        self.alive = False
                except StopIteration:
                    self.alive = False
                S.capture = None

            def fill(self):
                while self.alive and not self.fifo:
                    self.pump()
                return bool(self.fifo)

            def pe_left(self):
                return sum((S.DEF_DUR["pe"] if d[5] is None else d[5]) for d in self.fifo if d[0] == "pe")

        def merge(streams, until=None):
            while True:
                live = [s for s in streams if s.fill()]
                if until is not None and not until.fill():
                    return
                if not live:
                    return
                def cost(s):
                    d = s.fifo[0]
                    st = S.estimate(d)
                    if d[0] == "pe":
                        st += PE_STALL_W * max(0.0, st - S.eng_free["pe"])
                    return st + _jit.uniform(0.0, JITTER_US)
                best = min(live, key=cost)
                S.commit(best.fifo.pop(0))

        TAIL_INLINE_US = 0.0
        JITTER_US = 1.0
        _jit = random.Random(JITTER_SEED)
        PE_STALL_W = 0.0
        drain(mk_mixer(0))
        ftail = None
        for k in range(len(stages)):
            fs = Stream(mk_ffn(k))
            if ftail is not None:
                merge([fs, ftail], until=ftail)
                ftail = None
            ms = Stream(mk_mixer(k + 1), eager=True) if k + 1 < len(stages) else None
            while True:
                merge([s for s in (fs, ms) if s is not None], until=fs)
                if fs.at_decide:
                    fs.at_decide = False
                    inline = ms is not None and ms.pe_left() > TAIL_INLINE_US
                    fs.send = bool(inline)
                    fs.alive = True
                    if not inline:
                        ftail = fs
                        break
                else:
                    break
            if ms is not None:
                merge([ms])
        if ftail is not None:
            merge([ftail])
        S.final_wait("sp", [("X", b, tt, eh) for b in range(2) for tt in range(NTT) for eh in range(2)])

        sems = {k: es.enter_context(nc.semaphore(f"s_{k}")) for k in sorted(S.semkeys)}
        block = es.enter_context(nc.Block())

        class FirstIns:
            def __init__(self, e):
                self.e = e
                self.first = None

            def matmul(self, *a, **k):
                r = self.e.matmul(*a, **k)
                if self.first is None:
                    self.first = r
                return r

            def transpose(self, *a, **k):
                r = self.e.transpose(*a, **k)
                if self.first is None:
                    self.first = r
                return r

        def run(name, e):
            for waits, fn, inc in S.ops[name]:
                waits = list(waits)
                fused = None
                if fn is not None and waits:
                    fused = waits.pop()
                for sk, val in waits:
                    e.wait_ge(sems[sk], val)
                if fn is not None:
                    if name == "pe":
                        px = FirstIns(e)
                        ins = fn(px)
                        first = px.first
                    else:
                        ins = fn(e)
                        first = ins
                    if fused is not None:
                        first._wait_ge(sems[fused[0]], fused[1])
                    ins.then_inc(sems[inc[0]], inc[1])

        @block.tensor
        def _(e):
            run("pe", e)

        @block.scalar
        def _(e):
            run("act", e)

        @block.vector
        def _(e):
            run("dve", e)

        @block.gpsimd
        def _(e):
            run("pool", e)

        @block.sync
        def _(e):
            run("sp", e)
    return nc


def _bc(v):
    return np.ascontiguousarray(np.broadcast_to(np.asarray(v, np.float32).reshape(1, -1), (128, D)))


def _prep(inputs, layers):
    f = lambda a: np.ascontiguousarray(np.asarray(a, dtype=np.float32))
    m = {}
    ktab = np.zeros((128, 272), np.float32)
    ktab[:, 0:128] = np.eye(128, dtype=np.float32)
    ktab[:, 128:256] = 1.0 / D
    ktab[:, 256:272] = (1.0 / np.arange(1, 17, dtype=np.float32))[None, :]
    m["ktab"] = ktab
    lnp = np.zeros((DEPTH, 4, 128, D), np.float32)
    mixb = np.zeros((DEPTH, 2, 128, D), np.float32)
    for L in range(DEPTH):
        lnp[L, 0] = _bc(inputs["ln1_g"][L])
        lnp[L, 1] = _bc(inputs["ln1_b"][L])
        lnp[L, 2] = _bc(inputs["ln2_g"][L])
        lnp[L, 3] = _bc(inputs["ln2_b"][L])
        l = L // 2
        if L % 2 == 0:
            mixb[L, 0] = _bc(inputs["a_b_out"][l])
        else:
            mixb[L, 0] = _bc(np.asarray(inputs["p_b"][l]).reshape(-1))
            mixb[L, 1] = _bc(inputs["p_scale"][l])
    m["lnp"] = lnp
    m["mixb"] = mixb
    for L in layers:
        l = L // 2
        wg = f(inputs["ffn_w_gate"][L]).reshape(8, 128, NJ, 128).transpose(2, 1, 0, 3)
        m[f"wg{L}"] = np.ascontiguousarray(wg).reshape(NJ, 128, 1024)
        wu = f(inputs["ffn_w_up"][L]).reshape(8, 128, NJ, 128).transpose(2, 1, 0, 3)
        m[f"wu{L}"] = np.ascontiguousarray(wu).reshape(NJ, 128, 1024)
        m[f"wd{L}"] = f(inputs["ffn_w_down"][L]).reshape(NJ, 128, D)
        if L % 2 == 0:
            win = f(inputs["a_w_in"][l]).reshape(8, 128, 2, 8, 128).transpose(3, 1, 2, 0, 4)
            m[f"win{L}"] = np.ascontiguousarray(win).reshape(8, 128, 2048)
            wo = f(inputs["a_w_out"][l]).reshape(4, 2, 128, 2, 512).transpose(3, 0, 2, 1, 4)
            m[f"wout{L}"] = np.ascontiguousarray(wo).reshape(2, 4, 128, 1024)
            cc = np.zeros((128, 288), np.float32)
            cc[:, 0:16] = f(inputs["a_b_in"][l]).reshape(16, 128).T
            wdw = f(inputs["a_w_dw"][l])[:, 0, :]
            cc[:, 16:264] = wdw.T.reshape(8, 128, KW).transpose(1, 0, 2).reshape(128, 8 * KW)
            cc[:, 264:272] = f(inputs["a_b_dw"][l]).reshape(8, 128).T
            cc[:, 272:280] = f(inputs["a_ln_g"][l]).reshape(8, 128).T
            cc[:, 280:288] = f(inputs["a_ln_b"][l]).reshape(8, 128).T
            m[f"convc{L}"] = cc
        else:
            wp = f(inputs["p_w"][l]).reshape(4, 2, 128, 256).transpose(2, 0, 1, 3)
            m[f"wp{L}"] = np.ascontiguousarray(wp).reshape(128, 2048)
    return m


_CACHE = {}


def _get_nc(layers):
    key = tuple(layers)
    if key not in _CACHE:
        _CACHE[key] = build(list(layers))
    return _CACHE[key]


def _run(inputs, x, layers):
    common = _prep(inputs, layers)
    nc = _get_nc(layers)
    in_maps = []
    for b in range(NB):
        mm = dict(common)
        mm["x"] = np.ascontiguousarray(x[b])
        in_maps.append(mm)
    res = run_bass_kernel_spmd(nc, in_maps, core_ids=list(range(NB)))
    return np.stack([np.asarray(r["y"], dtype=np.float32) for r in res.results], axis=0)


def kernel(**inputs):
    x = np.asarray(inputs["x"], dtype=np.float32)
    return _run(inputs, x, [0, 1, 2, 3])
```

```python
import contextlib
import random
import numpy as np
import concourse.bass as bass
import concourse.mybir as mybir
from concourse.bass_utils import run_bass_kernel_spmd

F32 = mybir.dt.float32
BF16 = mybir.dt.bfloat16
AF = mybir.ActivationFunctionType
ALU = mybir.AluOpType

D = 1024
SEQ = 2048
NB = 8
DEPTH = 4
DFF = 2816
NJ = DFF // 128
KW = 31
ALPHA = float((2 * DEPTH) ** 0.25)
EPS = 1e-5
NT = 512
NTT = NT // 128
NQ = SEQ // NT
G = 4
NA = 4
NBS = 12
NLN = 2
UP = 32
PP = 16
POOL_W = (2, 4, 8, 16)
JITTER_SEED = 1
SELF_WIN = 1000000000


class Sched:
    ENGS = ("pe", "act", "dve", "pool", "sp")
    DEF_DUR = {"pe": 2.0, "act": 0.7, "dve": 0.7, "pool": 0.65, "sp": 0.15}
    DMA_LAT = 3.0
    SEM_LAT = 0.35

    def __init__(self):
        self.ops = {e: [] for e in self.ENGS}
        self.cnt = {e: 0 for e in self.ENGS}
        self.dcnt = {}
        self.last_w = {}
        self.readers = {}
        self.seen = {e: {} for e in self.ENGS}
        self.semkeys = set(self.ENGS)
        self.capture = None
        self.eng_free = {e: 0.0 for e in self.ENGS}
        self.t_w = {}
        self.t_r = {}

    def op(self, eng, fn, reads=(), writes=(), dma=None, dur=None):
        desc = (eng, fn, tuple(reads), tuple(writes), dma, dur)
        if self.capture is not None:
            self.capture.append(desc)
            return None
        return self.commit(desc)

    def estimate(self, desc):
        eng, fn, reads, writes, dma, dur = desc
        ready = 0.0
        for r in reads:
            ready = max(ready, self.t_w.get(r, 0.0))
        for w in writes:
            ready = max(ready, self.t_w.get(w, 0.0), self.t_r.get(w, 0.0))
        return max(self.eng_free[eng], ready + self.SEM_LAT)

    def commit(self, desc):
        eng, fn, reads, writes, dma, dur = desc
        start = self.estimate(desc)
        d = self.DEF_DUR[eng] if dur is None else dur
        if dma is None:
            fin = start + d
            self.eng_free[eng] = fin
        else:
            self.eng_free[eng] = start + d
            fin = start + d + self.DMA_LAT
        for r in reads:
            self.t_r[r] = max(self.t_r.get(r, 0.0), fin)
        for w in writes:
            self.t_w[w] = fin
            self.t_r[w] = 0.0
        idx = len(self.ops[eng])
        deps = []
        for r in reads:
            if r in self.last_w:
                deps.append(self.last_w[r])
        for w in writes:
            if w in self.last_w:
                deps.append(self.last_w[w])
            deps.extend(self.readers.get(w, ()))
        need = {}
        for (sk, val, peng, pidx, pdma) in deps:
            if (not pdma) and peng == eng and dma is None:
                if pidx < idx - SELF_WIN or eng == "pe":
                    continue
            if self.seen[eng].get(sk, 0) >= val:
                continue
            if need.get(sk, 0) < val:
                need[sk] = val
        for sk, val in need.items():
            self.seen[eng][sk] = val
        if dma is None:
            self.cnt[eng] += 1
            tick = (eng, self.cnt[eng], eng, idx, False)
            inc = (eng, 1)
        else:
            self.semkeys.add(dma)
            self.dcnt[dma] = self.dcnt.get(dma, 0) + 1
            tick = (dma, 16 * self.dcnt[dma], eng, idx, True)
            inc = (dma, 16)
        for r in reads:
            self.readers.setdefault(r, []).append(tick)
        for w in writes:
            self.last_w[w] = tick
            self.readers[w] = []
        self.ops[eng].append((list(need.items()), fn, inc))
        return tick

    def final_wait(self, eng, keys):
        need = {}
        for k in keys:
            t = self.last_w.get(k)
            cands = list(self.readers.get(k, ()))
            if t is not None:
                cands.append(t)
            for (sk, val, *_r) in cands:
                if need.get(sk, 0) < val:
                    need[sk] = val
        self.ops[eng].append((list(need.items()), None, None))


def build(layers):
    nc = bass.Bass("TRN2", target_bir_lowering=False)
    conv_layers = [L for L in layers if L % 2 == 0]
    pool_layers = [L for L in layers if L % 2 == 1]

    def dram(name, shape, kind="ExternalInput"):
        return nc.dram_tensor(name, list(shape), F32, kind=kind).ap()

    x_d = dram("x", [SEQ, D])
    y_d = dram("y", [SEQ, D], kind="ExternalOutput")
    ktab_d = dram("ktab", [128, 272])
    lnp_d = dram("lnp", [DEPTH, 4, 128, D])
    mixb_d = dram("mixb", [DEPTH, 2, 128, D])
    wg_d = {L: dram(f"wg{L}", [NJ, 128, 8 * 128]) for L in layers}
    wu_d = {L: dram(f"wu{L}", [NJ, 128, 8 * 128]) for L in layers}
    wd_d = {L: dram(f"wd{L}", [NJ, 128, D]) for L in layers}
    win_d = {L: dram(f"win{L}", [8, 128, 2 * 8 * 128]) for L in conv_layers}
    wout_d = {L: dram(f"wout{L}", [2, 4, 128, D]) for L in conv_layers}
    convc_d = {L: dram(f"convc{L}", [128, 288]) for L in conv_layers}
    wp_d = {L: dram(f"wp{L}", [128, 8 * 256]) for L in pool_layers}

    S = Sched()
    es = contextlib.ExitStack()
    with es:
        def sb(name, shape, dt=F32):
            return es.enter_context(nc.sbuf_tensor("sb_" + name, list(shape), dt))

        xtok = [sb(f"xtok{b}", [128, NTT, D]) for b in range(2)]
        xT = [sb(f"xT{b}", [128, 8, NT], BF16) for b in range(2)]
        s1 = sb("s1", [128, 8, NT + UP])
        s2 = sb("s2", [128, 8, NT])
        dgb = sb("dgb", [128, 2, KW * 128], BF16)
        sTm = sb("sTm", [128, 8, NT], BF16)
        hT = sb("hT", [128, 8, NT], BF16)
        NAF, NAM, NBF, NBM = 4, 2, 12, 4
        ringAF = sb("ringAF", [128, NAF, 2048], BF16)
        ringAM = sb("ringAM", [128, NAM, 2048], BF16)
        ringBF = sb("ringBF", [128, NBF, D], BF16)
        ringBM = sb("ringBM", [128, NBM, D], BF16)
        lnb = sb("lnb", [128, NLN, 2, D])
        mixb = sb("mixb", [128, 2, D])
        sgtF = sb("sgtF", [128, 2, 512])
        sgtM = sb("sgtM", [128, 1, 512])
        vbh = sb("vbh", [128, 1, 512])
        ktab = sb("ktab", [128, 272])
        convc = {L: sb(f"convc{L}", [128, 288]) for L in conv_layers}
        hbias = {L: sb(f"hb{L}", [128, 16]) for L in conv_layers}
        uh = {L: sb(f"uh{L}", [128, 8, UP], BF16) for L in conv_layers}
        xh = {L: sb(f"xh{L}", [128, 8, PP]) for L in pool_layers}
        stt = sb("stt", [128, 2, NTT, 2, 6])
        mv = sb("mv", [128, 2, NTT, 2])
        rs = sb("rs", [128, 2, 2, NTT])
        fx = sb("fx", [128, 2, 16])
        ps = es.enter_context(nc.psum_tensor("ps", [128, 8, 512], F32))

        ident = ktab[:, 0:128]
        onesm = ktab[:, 128:256]
        invc = ktab[:, 256:272]
        s1b = s1[:].bitcast(BF16)

        S.op("sp", lambda e: e.dma_start(out=ktab[:], in_=ktab_d), writes=[("KT",)], dma="d_kt")
        for L in conv_layers:
            S.op("sp", lambda e, L=L: e.dma_start(out=convc[L][:], in_=convc_d[L]),
                 writes=[("CC", L)], dma=f"d_cc{L}")
            S.op("dve", lambda e, L=L: e.tensor_scalar(
                out=hbias[L][:], in0=convc[L][:, 0:16], scalar1=0.5, scalar2=None, op0=ALU.mult),
                reads=[("CC", L)], writes=[("HB", L)])

        ra_n = [0]
        rb_n = [0]
        ln_n = [0]
        pp_n = {}

        def flip(k, n=2):
            v = pp_n.get(k, 0)
            pp_n[k] = v + 1
            return v % n

        def xt_keys(b):
            return [("XT", b, tt, kh) for tt in range(NTT) for kh in range(2)]

        rn = {"AF": 0, "AM": 0, "BF": 0, "BM": 0}

        def load_AM(src_ap):
            slot = rn["AM"] % NAM
            rn["AM"] += 1
            S.op("pool", lambda e: e.dma_start(out=ringAM[:, slot, :], in_=src_ap),
                 writes=[("RAM", slot)], dma=f"d_ram{slot}")
            return slot

        def load_AF2(src0, src1):
            slot = rn["AF"] % NAF
            rn["AF"] += 1
            S.op("pool", lambda e: e.dma_start(out=ringAF[:, slot, 0:1024], in_=src0),
                 writes=[("RAF", slot, 0)], dma=f"d_rafg{slot}")
            S.op("pool", lambda e: e.dma_start(out=ringAF[:, slot, 1024:2048], in_=src1),
                 writes=[("RAF", slot, 1)], dma=f"d_rafu{slot}")
            return slot

        def load_BF(src_ap):
            slot = rn["BF"] % NBF
            rn["BF"] += 1
            S.op("pool", lambda e: e.dma_start(out=ringBF[:, slot, :], in_=src_ap),
                 writes=[("RBF", slot)], dma=f"d_rbf{slot}")
            return slot

        def load_BM(src_ap):
            slot = rn["BM"] % NBM
            rn["BM"] += 1
            S.op("pool", lambda e: e.dma_start(out=ringBM[:, slot, :], in_=src_ap),
                 writes=[("RBM", slot)], dma=f"d_rbm{slot}")
            return slot

        def load_ln(L, which):
            par = 0 if which == 1 else 1
            src = lnp_d[L, 2 * which:2 * which + 2].rearrange("a p n -> p a n")
            S.op("sp", lambda e: e.dma_start(out=lnb[:, par, :, :], in_=src),
                 writes=[("LNB", par)], dma=f"d_ln{par}")
            return par

        def transposes(b, tt, banks, to_xT, to_s1):
            for kh in range(2):
                bank = banks[kh]

                def pe_fn(e, bank=bank, kh=kh):
                    ins = None
                    for q in range(4):
                        k = kh * 4 + q
                        ins = e.transpose(out=ps[:, bank, q * 128:(q + 1) * 128],
                                          in_=xtok[b][:, tt, k * 128:(k + 1) * 128],
                                          identity=ident)
                    return ins
                S.op("pe", pe_fn, reads=[("X", b, tt, kh), ("KT",)], writes=[("PS", bank)], dur=0.5)
                if to_xT:
                    S.op("act", lambda e, bank=bank, kh=kh: e.activation(
                        out=xT[b][:, kh * 4:(kh + 1) * 4, tt * 128:(tt + 1) * 128],
                        in_=ps[:, bank, :].rearrange("p (a b) -> p a b", a=4),
                        func=AF.Copy),
                        reads=[("PS", bank)], writes=[("XT", b, tt, kh)])
                if to_s1:
                    S.op("act", lambda e, bank=bank, kh=kh: e.activation(
                        out=s1[:, kh * 4:(kh + 1) * 4, PP + tt * 128:PP + (tt + 1) * 128],
                        in_=ps[:, bank, :].rearrange("p (a b) -> p a b", a=4), func=AF.Copy),
                        reads=[("PS", bank)],
                        writes=[("S1", c) for c in range(kh * 4, kh * 4 + 4)])

        def ln_stats(st, b, tt):
            for eh in range(2):
                S.op("dve", lambda e, eh=eh: e.bn_stats(
                    out=stt[:, st, tt, eh, :], in_=xtok[b][:, tt, eh * 512:(eh + 1) * 512]),
                    reads=[("X", b, tt, eh)], writes=[("STT", st, tt, eh)])
            S.op("dve", lambda e: e.bn_aggr(
                out=mv[:, st, tt, :], in_=stt[:, st, tt, :, :].rearrange("p a b -> p (a b)")),
                reads=[("STT", st, tt, 0), ("STT", st, tt, 1)], writes=[("MV", st, tt)], dur=0.25)

        def ln_finish(st, b, par, banks, do_tr, store_rows=None):
            S.op("act", lambda e: e.activation(out=rs[:, st, 0, :], in_=mv[:, st, :, 1], func=AF.Sqrt,
                                               bias=EPS, scale=1.0),
                 reads=[("MV", st, tt) for tt in range(NTT)], writes=[("RS", st, 0)], dur=3.0)
            S.op("dve", lambda e: e.reciprocal(out=rs[:, st, 1, :], in_=rs[:, st, 0, :]),
                 reads=[("RS", st, 0)], writes=[("RS", st, 1)], dur=0.2)
            yield
            for tt in range(NTT):
                xs = xtok[b][:, tt, :]
                xk = [("X", b, tt, 0), ("X", b, tt, 1)]
                S.op("dve", lambda e, tt=tt, xs=xs: e.scalar_tensor_tensor(
                    out=xs, in0=xs, scalar=mv[:, st, tt, 0:1],
                    in1=lnb[:, par, 0, :], op0=ALU.subtract, op1=ALU.mult),
                    reads=xk + [("MV", st, tt), ("LNB", par)], writes=xk, dur=1.25)
                S.op("dve", lambda e, tt=tt, xs=xs: e.scalar_tensor_tensor(
                    out=xs, in0=xs, scalar=rs[:, st, 1, tt:tt + 1],
                    in1=lnb[:, par, 1, :], op0=ALU.mult, op1=ALU.add),
                    reads=xk + [("RS", st, 1), ("LNB", par)], writes=xk, dur=1.25)
                if do_tr:
                    transposes(b, tt, banks, True, False)
                if store_rows is not None:
                    r0 = store_rows + tt * 128
                    S.op("sp", lambda e, tt=tt, r0=r0: e.dma_start(
                        out=y_d[r0:r0 + 128, :], in_=xtok[b][:, tt, :]),
                        reads=xk, dma=f"d_st{b}_{tt}")
                yield

        def ffn_stage(b, L, q, do_tr, store):
            groups = []
            j = 0
            first = NJ % G if NJ % G else G
            while j < NJ:
                gs = first if j == 0 else G
                groups.append(list(range(j, j + gs)))
                j += gs
            slotsB = {}

            def gu(gi):
                for jj, j in enumerate(groups[gi]):
                    sa = load_AF2(wg_d[L][j], wu_d[L][j])
                    slotsB[j] = load_BF(wd_d[L][j])
                    hidx = (gi % 2) * 4 + jj
                    bg = flip("Fg")
                    bu = 2

                    def mm(e, off, bank, sa=sa):
                        ins = None
                        for k in range(8):
                            ins = e.matmul(ps[:, bank, :],
                                           ringAF[:, sa, off + k * 128:off + (k + 1) * 128],
                                           xT[b][:, k, :], start=(k == 0), stop=(k == 7))
                        return ins
                    S.op("pe", lambda e, mm=mm, bg=bg: mm(e, 0, bg),
                         reads=[("RAF", sa, 0)] + xt_keys(b), writes=[("PS", bg)])
                    S.op("pe", lambda e, mm=mm, bu=bu: mm(e, 1024, bu),
                         reads=[("RAF", sa, 1)] + xt_keys(b), writes=[("PS", bu)])
                    qq = flip("Fs")
                    S.op("act", lambda e, bg=bg, qq=qq: e.activation(
                        out=sgtF[:, qq, :], in_=ps[:, bg, :], func=AF.Silu),
                        reads=[("PS", bg)], writes=[("SGF", qq)])
                    S.op("dve", lambda e, bu=bu, qq=qq, hidx=hidx: e.tensor_tensor(
                        out=hT[:, hidx, :], in0=ps[:, bu, :], in1=sgtF[:, qq, :], op=ALU.mult),
                        reads=[("PS", bu), ("SGF", qq)], writes=[("HT", hidx)])
                    yield

            def down(gi, last):
                grp = groups[gi]
                for tt in range(NTT):
                    def mm(e, tt=tt):
                        ins = None
                        for eh in range(2):
                            for jj, j in enumerate(grp):
                                hidx = (gi % 2) * 4 + jj
                                ins = e.matmul(ps[:, 3 + eh, :],
                                               hT[:, hidx, tt * 128:(tt + 1) * 128],
                                               ringBF[:, slotsB[j], eh * 512:(eh + 1) * 512],
                                               start=(jj == 0), stop=(jj == len(grp) - 1))
                        return ins
                    S.op("pe", mm,
                         reads=[("RBF", slotsB[j]) for j in grp] +
                               [("HT", (gi % 2) * 4 + jj) for jj in range(len(grp))],
                         writes=[("PS", 3), ("PS", 4)], dur=0.5 * len(grp))
                    xs = xtok[b][:, tt, :]
                    pv = ps[:, 3:5, :].rearrange("p a n -> p (a n)")
                    xk = [("X", b, tt, 0), ("X", b, tt, 1)]
                    if gi == 0:
                        S.op("dve", lambda e, xs=xs, pv=pv: e.scalar_tensor_tensor(
                            out=xs, in0=xs, scalar=ALPHA, in1=pv, op0=ALU.mult, op1=ALU.add),
                            reads=[("PS", 3), ("PS", 4)] + xk, writes=xk, dur=1.25)
                    else:
                        S.op("dve", lambda e, xs=xs, pv=pv: e.tensor_tensor(
                            out=xs, in0=xs, in1=pv, op=ALU.add),
                            reads=[("PS", 3), ("PS", 4)] + xk, writes=xk, dur=1.25)
                    if last:
                        ln_stats(0, b, tt)
                    yield

            ng = len(groups)
            for gi in range(ng):
                g_it = gu(gi)
                d_it = down(gi - 1, False) if gi >= 1 else iter(())
                g_alive = d_alive = True
                while g_alive or d_alive:
                    if g_alive:
                        try:
                            next(g_it)
                            yield
                        except StopIteration:
                            g_alive = False
                    if d_alive:
                        try:
                            next(d_it)
                            yield
                        except StopIteration:
                            d_alive = False
            par = load_ln(L, 1)
            yield from down(ng - 1, True)
            inline = yield "DECIDE"
            banks = (0, 1) if inline else (5, 6)
            yield from ln_finish(0, b, par, banks, do_tr,
                                 store_rows=(q * NT if store else None))

        def load_block(b, q):
            for tt in range(NTT):
                r0 = q * NT + tt * 128
                S.op("sp", lambda e, tt=tt, r0=r0: e.dma_start(out=xtok[b][:, tt, :], in_=x_d[r0:r0 + 128, :]),
                     writes=[("X", b, tt, 0), ("X", b, tt, 1)], dma=f"d_x{b}_{tt}")

        def conv_stage(b, L, q, first):
            cc = convc[L]
            hb = hbias[L]
            CK = ("CC", L)
            if first:
                load_block(b, q)
                for tt in range(NTT):
                    transposes(b, tt, (5, 6), True, False)
                    yield
            if q == 0:
                S.op("dve", lambda e: e.memset(s1b[:, :, 0:UP], 0.0),
                     writes=[("S1", c) for c in range(8)])
            else:
                S.op("dve", lambda e: e.tensor_copy(out=s1b[:, :, 0:UP], in_=uh[L][:]),
                     reads=[("UH", L)], writes=[("S1", c) for c in range(8)])
            S.op("sp", lambda e: e.dma_start(out=mixb[:, 0, :], in_=mixb_d[L, 0]),
                 writes=[("MB", 0)], dma="d_mb")
            par = load_ln(L, 0)

            def mkdiag(c):
                i = c % 2
                dst = dgb[:, i, :].rearrange("p (k j) -> p k j", k=KW)
                wv = cc[:, 16 + c * KW:16 + (c + 1) * KW]
                S.op("dve", lambda e: e.tensor_tensor(
                    out=dst, in0=ident.unsqueeze(1).to_broadcast([128, KW, 128]),
                    in1=wv.unsqueeze(2).to_broadcast([128, KW, 128]), op=ALU.mult),
                    reads=[CK, ("KT",)], writes=[("DG", i)], dur=4.3)

            def p1(c):
                sa = load_AM(win_d[L][c])

                def mm(e, off, bank):
                    ins = None
                    for k in range(8):
                        ins = e.matmul(ps[:, bank, :],
                                       ringAM[:, sa, off + k * 128:off + (k + 1) * 128],
                                       xT[b][:, k, :], start=(k == 0), stop=(k == 7))
                    return ins
                S.op("pe", lambda e: mm(e, 0, 5), reads=[("RAM", sa)] + xt_keys(b), writes=[("PS", 5)])
                S.op("pe", lambda e: mm(e, 1024, 6), reads=[("RAM", sa)] + xt_keys(b), writes=[("PS", 6)])
                qq = 0
                S.op("act", lambda e: e.activation(
                    out=vbh[:, qq, :], in_=ps[:, 5, :], func=AF.Identity,
                    bias=hb[:, c:c + 1], scale=0.5),
                    reads=[("PS", 5), ("HB", L)], writes=[("VB", qq)])
                S.op("act", lambda e: e.activation(
                    out=sgtM[:, qq, :], in_=ps[:, 6, :], func=AF.Tanh,
                    bias=hb[:, 8 + c:9 + c], scale=0.5),
                    reads=[("PS", 6), ("HB", L)], writes=[("SGM", qq)])
                S.op("dve", lambda e: e.scalar_tensor_tensor(
                    out=s1b[:, c, UP:UP + NT], in0=sgtM[:, qq, :], scalar=1.0, in1=vbh[:, qq, :],
                    op0=ALU.add, op1=ALU.mult),
                    reads=[("SGM", qq), ("VB", qq)], writes=[("S1", c)])

            def conv(c):
                i = c % 2

                def mm(e):
                    ins = None
                    for k in range(KW):
                        ins = e.matmul(ps[:, 7, :], dgb[:, i, k * 128:(k + 1) * 128],
                                       s1b[:, c, 2 + k:2 + k + NT], start=(k == 0), stop=(k == KW - 1))
                    return ins
                S.op("pe", mm, reads=[("S1", c), ("DG", i)], writes=[("PS", 7)], dur=7.0)
                S.op("act", lambda e: e.activation(
                    out=s2[:, c, :], in_=ps[:, 7, :], func=AF.Identity,
                    bias=cc[:, 264 + c:265 + c], scale=1.0),
                    reads=[("PS", 7), CK], writes=[("S2", c)])

            for c in range(8):
                mkdiag(c)
                p1(c)
                yield
                if c >= 1:
                    conv(c - 1)
                    yield
            if q < NQ - 1:
                S.op("dve", lambda e: e.tensor_copy(out=uh[L][:], in_=s1b[:, :, NT:NT + UP]),
                     reads=[("S1", c) for c in range(8)], writes=[("UH", L)])
            conv(7)
            yield
            sB = {}
            for kp in range(4):
                sB[(0, kp)] = load_BM(wout_d[L][0, kp])
            sq = [s1[:, 4 + i, 0:NT] for i in range(2)]
            mean_t = s1[:, 0, 0:NT]
            rstd_t = s1[:, 1, 0:NT]
            tmp_t = s1[:, 2, 0:NT]
            for c in range(8):
                i = c % 2
                S.op("act", lambda e, c=c, i=i: e.activation(out=sq[i], in_=s2[:, c, :], func=AF.Square),
                     reads=[("S2", c)], writes=[("S1", 4 + i)])

                def mm(e, c=c, i=i):
                    e.matmul(ps[:, 5, :], onesm, s2[:, c, :], start=(c == 0), stop=(c == 7))
                    return e.matmul(ps[:, 6, :], onesm, sq[i], start=(c == 0), stop=(c == 7))
                S.op("pe", mm, reads=[("S2", c), ("S1", 4 + i), ("KT",)], writes=[("PS", 5), ("PS", 6)])
                if c % 2 == 1:
                    yield
            for kp in range(4):
                S.op("pool", lambda e, kp=kp: e.dma_start(out=s1b[:, 4 + kp, 0:D], in_=wout_d[L][1, kp]),
                     writes=[("S1", 4 + kp)], dma=f"d_w1_{kp}")
            S.op("dve", lambda e: e.tensor_copy(out=mean_t, in_=ps[:, 5, :]),
                 reads=[("PS", 5)], writes=[("S1", 0)])
            S.op("dve", lambda e: e.tensor_tensor(out=tmp_t, in0=mean_t, in1=mean_t, op=ALU.mult),
                 reads=[("S1", 0)], writes=[("S1", 2)])
            S.op("dve", lambda e: e.tensor_tensor(out=tmp_t, in0=ps[:, 6, :], in1=tmp_t, op=ALU.subtract),
                 reads=[("PS", 6), ("S1", 2)], writes=[("S1", 2)])
            S.op("act", lambda e: e.activation(out=tmp_t, in_=tmp_t, func=AF.Sqrt, bias=EPS, scale=1.0),
                 reads=[("S1", 2)], writes=[("S1", 2)], dur=3.0)
            S.op("dve", lambda e: e.reciprocal(out=rstd_t, in_=tmp_t),
                 reads=[("S1", 2)], writes=[("S1", 1)])
            yield
            for c0 in range(0, 8, 2):
                for c in (c0, c0 + 1):
                    S.op("dve", lambda e, c=c: e.tensor_tensor(out=s2[:, c, :], in0=s2[:, c, :],
                                                               in1=mean_t, op=ALU.subtract),
                         reads=[("S2", c), ("S1", 0)], writes=[("S2", c)])
                for c in (c0, c0 + 1):
                    S.op("dve", lambda e, c=c: e.tensor_tensor(out=s2[:, c, :], in0=s2[:, c, :],
                                                               in1=rstd_t, op=ALU.mult),
                         reads=[("S2", c), ("S1", 1)], writes=[("S2", c)])
                for c in (c0, c0 + 1):
                    S.op("act", lambda e, c=c: e.activation(
                        out=sTm[:, c, :], in_=s2[:, c, :], func=AF.Silu,
                        bias=cc[:, 280 + c:281 + c], scale=cc[:, 272 + c:273 + c]),
                        reads=[("S2", c), CK], writes=[("STM", c)])
                yield
            for eh in range(2):
                for tt in range(NTT):
                    bank = 5 + flip("Mo", 3)

                    def mm(e, bank=bank, tt=tt, eh=eh):
                        ins = None
                        for c in range(8):
                            if eh == 0:
                                w = ringBM[:, sB[(0, c // 2)], (c % 2) * 512:(c % 2 + 1) * 512]
                            else:
                                w = s1b[:, 4 + c // 2, (c % 2) * 512:(c % 2 + 1) * 512]
                            ins = e.matmul(ps[:, bank, :], sTm[:, c, tt * 128:(tt + 1) * 128], w,
                                           start=(c == 0), stop=(c == 7))
                        return ins
                    wk = ([("RBM", sB[(0, kp)]) for kp in range(4)] if eh == 0
                          else [("S1", 4 + kp) for kp in range(4)])
                    S.op("pe", mm, reads=wk + [("STM", c) for c in range(8)], writes=[("PS", bank)])
                    xs = xtok[b][:, tt, eh * 512:(eh + 1) * 512]
                    S.op("dve", lambda e, xs=xs, bank=bank: e.scalar_tensor_tensor(
                        out=xs, in0=xs, scalar=ALPHA, in1=ps[:, bank, :], op0=ALU.mult, op1=ALU.add),
                        reads=[("PS", bank), ("X", b, tt, eh)], writes=[("X", b, tt, eh)])
                    S.op("dve", lambda e, xs=xs, eh=eh: e.tensor_tensor(
                        out=xs, in0=xs, in1=mixb[:, 0, eh * 512:(eh + 1) * 512], op=ALU.add),
                        reads=[("MB", 0), ("X", b, tt, eh)], writes=[("X", b, tt, eh)])
                    if eh == 1:
                        ln_stats(1, b, tt)
                    yield
            yield from ln_finish(1, b, par, (5, 6), True)

        def pool_stage(b, L, q, first):
            if first:
                load_block(b, q)
            for tt in range(NTT):
                transposes(b, tt, (5, 6), False, True)
                yield
            if q == 0:
                S.op("dve", lambda e: e.memset(s1[:, :, 0:PP], 0.0),
                     writes=[("S1", c) for c in range(8)])
            else:
                S.op("dve", lambda e: e.tensor_copy(out=s1[:, :, 0:PP], in_=xh[L][:]),
                     reads=[("XH", L)], writes=[("S1", c) for c in range(8)])
            if q < NQ - 1:
                S.op("dve", lambda e: e.tensor_copy(out=xh[L][:], in_=s1[:, :, NT:NT + PP]),
                     reads=[("S1", c) for c in range(8)], writes=[("XH", L)])
            S.op("sp", lambda e: e.dma_start(out=mixb[:, :, :],
                                             in_=mixb_d[L].rearrange("a p n -> p a n")),
                 writes=[("MB", 0), ("MB", 1)], dma="d_mb")
            S.op("dve", lambda e: e.tensor_tensor(out=mixb[:, 0, :], in0=mixb[:, 0, :],
                                                  in1=mixb[:, 1, :], op=ALU.mult),
                 reads=[("MB", 0), ("MB", 1)], writes=[("MB", 0)])
            sa = load_AM(wp_d[L])
            par = load_ln(L, 0)
            W = NT + PP
            for g in range(4):
                w = POOL_W[g]
                cs = (2 * g, 2 * g + 1)
                cur = {c: (s1[:, c, 0:W], [("S1", c)]) for c in cs}
                shift = 1
                for step in range(g + 1):
                    lo = 2 * shift - 1
                    for i, c in enumerate(cs):
                        r0 = 4 * i + 2 * (step % 2)
                        dst = s2[:, r0:r0 + 2, :].rearrange("p a n -> p (a n)")[:, 0:W]
                        dkeys = [("S2", r0), ("S2", r0 + 1)]
                        src, skeys = cur[c]
                        S.op("dve", lambda e, dst=dst, src=src, lo=lo, shift=shift: e.tensor_tensor(
                            out=dst[:, lo:W], in0=src[:, lo:W], in1=src[:, lo - shift:W - shift],
                            op=ALU.add),
                            reads=skeys, writes=dkeys)
                        cur[c] = (dst, dkeys)
                    shift *= 2
                for i, c in enumerate(cs):
                    src, skeys = cur[c]
                    S.op("dve", lambda e, c=c, src=src, w=w: e.scalar_tensor_tensor(
                        out=sTm[:, c, :], in0=src[:, PP:PP + NT], scalar=1.0 / w,
                        in1=s1[:, c, PP:PP + NT], op0=ALU.mult, op1=ALU.subtract),
                        reads=[("S1", c)] + skeys, writes=[("STM", c)])
                if q == 0:
                    for i, c in enumerate(cs):
                        src, skeys = cur[c]
                        S.op("dve", lambda e, i=i, src=src, w=w: e.tensor_tensor(
                            out=fx[:, i, 0:w - 1], in0=src[:, PP:PP + w - 1], in1=invc[:, 0:w - 1],
                            op=ALU.mult),
                            reads=skeys + [("KT",)], writes=[("FX", i)])
                    for i, c in enumerate(cs):
                        S.op("dve", lambda e, i=i, c=c, w=w: e.tensor_tensor(
                            out=sTm[:, c, 0:w - 1], in0=fx[:, i, 0:w - 1], in1=s1[:, c, PP:PP + w - 1],
                            op=ALU.subtract),
                            reads=[("FX", i), ("S1", c), ("STM", c)], writes=[("STM", c)])
                yield
            for tt in range(NTT):
                for gh in range(2):
                    bank = 5 + flip("Mo", 3)

                    def mm(e, bank=bank, tt=tt, gh=gh):
                        ins = None
                        for gg in range(2):
                            g = 2 * gh + gg
                            for kk in range(2):
                                off = (g * 2 + kk) * 256
                                ins = e.matmul(ps[:, bank, gg * 256:(gg + 1) * 256],
                                               sTm[:, 2 * g + kk, tt * 128:(tt + 1) * 128],
                                               ringAM[:, sa, off:off + 256],
                                               start=(kk == 0), stop=(kk == 1))
                        return ins
                    S.op("pe", mm, reads=[("RAM", sa)] + [("STM", c) for c in range(4 * gh, 4 * gh + 4)],
                         writes=[("PS", bank)], dur=0.6)
                    qq = 0
                    sl = slice(gh * 512, (gh + 1) * 512)
                    xs = xtok[b][:, tt, sl]
                    S.op("dve", lambda e, bank=bank, qq=qq, sl=sl: e.tensor_tensor(
                        out=sgtM[:, qq, :], in0=ps[:, bank, :], in1=mixb[:, 1, sl], op=ALU.mult),
                        reads=[("PS", bank), ("MB", 1)], writes=[("SGM", qq)])
                    S.op("dve", lambda e, xs=xs, qq=qq: e.scalar_tensor_tensor(
                        out=xs, in0=xs, scalar=ALPHA, in1=sgtM[:, qq, :], op0=ALU.mult, op1=ALU.add),
                        reads=[("SGM", qq), ("X", b, tt, gh)], writes=[("X", b, tt, gh)])
                    S.op("dve", lambda e, xs=xs, sl=sl: e.tensor_tensor(
                        out=xs, in0=xs, in1=mixb[:, 0, sl], op=ALU.add),
                        reads=[("MB", 0), ("X", b, tt, gh)], writes=[("X", b, tt, gh)])
                ln_stats(1, b, tt)
                yield
            yield from ln_finish(1, b, par, (5, 6), True)

        stages = []
        for pair in ((0, 1), (2, 3)):
            for li, L in enumerate(layers):
                for q in pair:
                    stages.append((q, li, L))

        def mk_mixer(k):
            q, li, L = stages[k]
            b = q % 2
            if L % 2 == 0:
                return conv_stage(b, L, q, li == 0)
            return pool_stage(b, L, q, li == 0)

        def mk_ffn(k):
            q, li, L = stages[k]
            b = q % 2
            is_last = (li == len(layers) - 1)
            nxt_conv = (not is_last) and (layers[li + 1] % 2 == 0)
            return ffn_stage(b, L, q, nxt_conv, is_last)

        def drain(g):
            for _ in g:
                pass

        class Stream:
            def __init__(self, g, eager=False):
                self.g = g
                self.fifo = []
                self.alive = True
                self.at_decide = False
                self.send = None
                if eager:
                    while self.alive:
                        self.pump()

            def pump(self):
                S.capture = self.fifo
                try:
                    if self.send is not None:
                        r = self.g.send(self.send)
                        self.send = None
                    else:
                        r = next(self.g)
                    if r == "DECIDE":
                        self.at_decide = True
                        self.alive = False
                except StopIteration:
                    self.alive = False
                S.capture = None

            def fill(self):
                while self.alive and not self.fifo:
                    self.pump()
                return bool(self.fifo)

            def pe_left(self):
                return sum((S.DEF_DUR["pe"] if d[5] is None else d[5]) for d in self.fifo if d[0] == "pe")

        def merge(streams, until=None):
            while True:
                live = [s for s in streams if s.fill()]
                if until is not None and not until.fill():
                    return
                if not live:
                    return
                def cost(s):
                    d = s.fifo[0]
                    st = S.estimate(d)
                    if d[0] == "pe":
                        st += PE_STALL_W * max(0.0, st - S.eng_free["pe"])
                    return st + _jit.uniform(0.0, JITTER_US)
                best = min(live, key=cost)
                S.commit(best.fifo.pop(0))

        TAIL_INLINE_US = 0.0
        JITTER_US = 0.15
        _jit = random.Random(JITTER_SEED)
        PE_STALL_W = 0.0
        drain(mk_mixer(0))
        ftail = None
        for k in range(len(stages)):
            fs = Stream(mk_ffn(k))
            if ftail is not None:
                merge([fs, ftail], until=ftail)
                ftail = None
            ms = Stream(mk_mixer(k + 1), eager=True) if k + 1 < len(stages) else None
            while True:
                merge([s for s in (fs, ms) if s is not None], until=fs)
                if fs.at_decide:
                    fs.at_decide = False
                    inline = ms is not None and ms.pe_left() > TAIL_INLINE_US
                    fs.send = bool(inline)
                    fs.alive = True
                    if not inline:
                        ftail = fs
                        break
                else:
                    break
            if ms is not None:
                merge([ms])
        if ftail is not None:
            merge([ftail])
        S.final_wait("sp", [("X", b, tt, eh) for b in range(2) for tt in range(NTT) for eh in range(2)])

        sems = {k: es.enter_context(nc.semaphore(f"s_{k}")) for k in sorted(S.semkeys)}
        block = es.enter_context(nc.Block())

        class FirstIns:
            def __init__(self, e):
                self.e = e
                self.first = None

            def matmul(self, *a, **k):
                r = self.e.matmul(*a, **k)
                if self.first is None:
                    self.first = r
                return r

            def transpose(self, *a, **k):
                r = self.e.transpose(*a, **k)
                if self.first is None:
                    self.first = r
                return r

        def run(name, e):
            for waits, fn, inc in S.ops[name]:
                waits = list(waits)
                fused = None
                if fn is not None and waits:
                    fused = waits.pop()
                for sk, val in waits:
                    e.wait_ge(sems[sk], val)
                if fn is not None:
                    if name == "pe":
                        px = FirstIns(e)
                        ins = fn(px)
                        first = px.first
                    else:
                        ins = fn(e)
                        first = ins
                    if fused is not None:
                        first._wait_ge(sems[fused[0]], fused[1])
                    ins.then_inc(sems[inc[0]], inc[1])

        @block.tensor
        def _(e):
            run("pe", e)

        @block.scalar
        def _(e):
            run("act", e)

        @block.vector
        def _(e):
            run("dve", e)

        @block.gpsimd
        def _(e):
            run("pool", e)

        @block.sync
        def _(e):
            run("sp", e)
    return nc


def _bc(v):
    return np.ascontiguousarray(np.broadcast_to(np.asarray(v, np.float32).reshape(1, -1), (128, D)))


def _prep(inputs, layers):
    f = lambda a: np.ascontiguousarray(np.asarray(a, dtype=np.float32))
    m = {}
    ktab = np.zeros((128, 272), np.float32)
    ktab[:, 0:128] = np.eye(128, dtype=np.float32)
    ktab[:, 128:256] = 1.0 / D
    ktab[:, 256:272] = (1.0 / np.arange(1, 17, dtype=np.float32))[None, :]
    m["ktab"] = ktab
    lnp = np.zeros((DEPTH, 4, 128, D), np.float32)
    mixb = np.zeros((DEPTH, 2, 128, D), np.float32)
    for L in range(DEPTH):
        lnp[L, 0] = _bc(inputs["ln1_g"][L])
        lnp[L, 1] = _bc(inputs["ln1_b"][L])
        lnp[L, 2] = _bc(inputs["ln2_g"][L])
        lnp[L, 3] = _bc(inputs["ln2_b"][L])
        l = L // 2
        if L % 2 == 0:
            mixb[L, 0] = _bc(inputs["a_b_out"][l])
        else:
            mixb[L, 0] = _bc(np.asarray(inputs["p_b"][l]).reshape(-1))
            mixb[L, 1] = _bc(inputs["p_scale"][l])
    m["lnp"] = lnp
    m["mixb"] = mixb
    for L in layers:
        l = L // 2
        wg = f(inputs["ffn_w_gate"][L]).reshape(8, 128, NJ, 128).transpose(2, 1, 0, 3)
        m[f"wg{L}"] = np.ascontiguousarray(wg).reshape(NJ, 128, 1024)
        wu = f(inputs["ffn_w_up"][L]).reshape(8, 128, NJ, 128).transpose(2, 1, 0, 3)
        m[f"wu{L}"] = np.ascontiguousarray(wu).reshape(NJ, 128, 1024)
        m[f"wd{L}"] = f(inputs["ffn_w_down"][L]).reshape(NJ, 128, D)
        if L % 2 == 0:
            win = f(inputs["a_w_in"][l]).reshape(8, 128, 2, 8, 128).transpose(3, 1, 2, 0, 4)
            m[f"win{L}"] = np.ascontiguousarray(win).reshape(8, 128, 2048)
            wo = f(inputs["a_w_out"][l]).reshape(4, 2, 128, 2, 512).transpose(3, 0, 2, 1, 4)
            m[f"wout{L}"] = np.ascontiguousarray(wo).reshape(2, 4, 128, 1024)
            cc = np.zeros((128, 288), np.float32)
            cc[:, 0:16] = f(inputs["a_b_in"][l]).reshape(16, 128).T
            wdw = f(inputs["a_w_dw"][l])[:, 0, :]
            cc[:, 16:264] = wdw.T.reshape(8, 128, KW).transpose(1, 0, 2).reshape(128, 8 * KW)
            cc[:, 264:272] = f(inputs["a_b_dw"][l]).reshape(8, 128).T
            cc[:, 272:280] = f(inputs["a_ln_g"][l]).reshape(8, 128).T
            cc[:, 280:288] = f(inputs["a_ln_b"][l]).reshape(8, 128).T
            m[f"convc{L}"] = cc
        else:
            wp = f(inputs["p_w"][l]).reshape(4, 2, 128, 256).transpose(2, 0, 1, 3)
            m[f"wp{L}"] = np.ascontiguousarray(wp).reshape(128, 2048)
    return m


_CACHE = {}


def _get_nc(layers):
    key = tuple(layers)
    if key not in _CACHE:
        _CACHE[key] = build(list(layers))
    return _CACHE[key]


def _run(inputs, x, layers):
    common = _prep(inputs, layers)
    nc = _get_nc(layers)
    in_maps = []
    for b in range(NB):
        mm = dict(common)
        mm["x"] = np.ascontiguousarray(x[b])
        in_maps.append(mm)
    res = run_bass_kernel_spmd(nc, in_maps, core_ids=list(range(NB)))
    return np.stack([np.asarray(r["y"], dtype=np.float32) for r in res.results], axis=0)


def kernel(**inputs):
    x = np.asarray(inputs["x"], dtype=np.float32)
    return _run(inputs, x, [0, 1, 2, 3])
```

```python
import contextlib
import random
import numpy as np
import concourse.bass as bass
import concourse.mybir as mybir
from concourse.bass_utils import run_bass_kernel_spmd

F32 = mybir.dt.float32
BF16 = mybir.dt.bfloat16
AF = mybir.ActivationFunctionType
ALU = mybir.AluOpType

D = 1024
SEQ = 2048
NB = 8
DEPTH = 4
DFF = 2816
NJ = DFF // 128
KW = 31
ALPHA = float((2 * DEPTH) ** 0.25)
EPS = 1e-5
NT = 512
NTT = NT // 128
NQ = SEQ // NT
G = 4
NA = 4
NBS = 12
NLN = 2
UP = 32
PP = 16
POOL_W = (2, 4, 8, 16)
JITTER_SEED = 21
SELF_WIN = 1000000000


class Sched:
    ENGS = ("pe", "act", "dve", "pool", "sp")
    DEF_DUR = {"pe": 2.0, "act": 0.7, "dve": 0.7, "pool": 0.65, "sp": 0.15}
    DMA_LAT = 3.0
    SEM_LAT = 0.35

    def __init__(self):
        self.ops = {e: [] for e in self.ENGS}
        self.cnt = {e: 0 for e in self.ENGS}
        self.dcnt = {}
        self.last_w = {}
        self.readers = {}
        self.seen = {e: {} for e in self.ENGS}
        self.semkeys = set(self.ENGS)
        self.capture = None
        self.eng_free = {e: 0.0 for e in self.ENGS}
        self.t_w = {}
        self.t_r = {}

    def op(self, eng, fn, reads=(), writes=(), dma=None, dur=None):
        desc = (eng, fn, tuple(reads), tuple(writes), dma, dur)
        if self.capture is not None:
            self.capture.append(desc)
            return None
        return self.commit(desc)

    def estimate(self, desc):
        eng, fn, reads, writes, dma, dur = desc
        ready = 0.0
        for r in reads:
            ready = max(ready, self.t_w.get(r, 0.0))
        for w in writes:
            ready = max(ready, self.t_w.get(w, 0.0), self.t_r.get(w, 0.0))
        return max(self.eng_free[eng], ready + self.SEM_LAT)

    def commit(self, desc):
        eng, fn, reads, writes, dma, dur = desc
        start = self.estimate(desc)
        d = self.DEF_DUR[eng] if dur is None else dur
        if dma is None:
            fin = start + d
            self.eng_free[eng] = fin
        else:
            self.eng_free[eng] = start + d
            fin = start + d + self.DMA_LAT
        for r in reads:
            self.t_r[r] = max(self.t_r.get(r, 0.0), fin)
        for w in writes:
            self.t_w[w] = fin
            self.t_r[w] = 0.0
        idx = len(self.ops[eng])
        deps = []
        for r in reads:
            if r in self.last_w:
                deps.append(self.last_w[r])
        for w in writes:
            if w in self.last_w:
                deps.append(self.last_w[w])
            deps.extend(self.readers.get(w, ()))
        need = {}
        for (sk, val, peng, pidx, pdma) in deps:
            if (not pdma) and peng == eng and dma is None:
                if pidx < idx - SELF_WIN or eng == "pe":
                    continue
            if self.seen[eng].get(sk, 0) >= val:
                continue
            if need.get(sk, 0) < val:
                need[sk] = val
        for sk, val in need.items():
            self.seen[eng][sk] = val
        if dma is None:
            self.cnt[eng] += 1
            tick = (eng, self.cnt[eng], eng, idx, False)
            inc = (eng, 1)
        else:
            self.semkeys.add(dma)
            self.dcnt[dma] = self.dcnt.get(dma, 0) + 1
            tick = (dma, 16 * self.dcnt[dma], eng, idx, True)
            inc = (dma, 16)
        for r in reads:
            self.readers.setdefault(r, []).append(tick)
        for w in writes:
            self.last_w[w] = tick
            self.readers[w] = []
        self.ops[eng].append((list(need.items()), fn, inc))
        return tick

    def final_wait(self, eng, keys):
        need = {}
        for k in keys:
            t = self.last_w.get(k)
            cands = list(self.readers.get(k, ()))
            if t is not None:
                cands.append(t)
            for (sk, val, *_r) in cands:
                if need.get(sk, 0) < val:
                    need[sk] = val
        self.ops[eng].append((list(need.items()), None, None))


def build(layers):
    nc = bass.Bass("TRN2", target_bir_lowering=False)
    conv_layers = [L for L in layers if L % 2 == 0]
    pool_layers = [L for L in layers if L % 2 == 1]

    def dram(name, shape, kind="ExternalInput"):
        return nc.dram_tensor(name, list(shape), F32, kind=kind).ap()

    x_d = dram("x", [SEQ, D])
    y_d = dram("y", [SEQ, D], kind="ExternalOutput")
    ktab_d = dram("ktab", [128, 272])
    lnp_d = dram("lnp", [DEPTH, 4, 128, D])
    mixb_d = dram("mixb", [DEPTH, 2, 128, D])
    wg_d = {L: dram(f"wg{L}", [NJ, 128, 8 * 128]) for L in layers}
    wu_d = {L: dram(f"wu{L}", [NJ, 128, 8 * 128]) for L in layers}
    wd_d = {L: dram(f"wd{L}", [NJ, 128, D]) for L in layers}
    win_d = {L: dram(f"win{L}", [8, 128, 2 * 8 * 128]) for L in conv_layers}
    wout_d = {L: dram(f"wout{L}", [2, 4, 128, D]) for L in conv_layers}
    convc_d = {L: dram(f"convc{L}", [128, 288]) for L in conv_layers}
    wp_d = {L: dram(f"wp{L}", [128, 8 * 256]) for L in pool_layers}

    S = Sched()
    es = contextlib.ExitStack()
    with es:
        def sb(name, shape, dt=F32):
            return es.enter_context(nc.sbuf_tensor("sb_" + name, list(shape), dt))

        xtok = [sb(f"xtok{b}", [128, NTT, D]) for b in range(2)]
        xT = [sb(f"xT{b}", [128, 8, NT], BF16) for b in range(2)]
        s1 = sb("s1", [128, 8, NT + UP])
        s2 = sb("s2", [128, 8, NT])
        dgb = sb("dgb", [128, 2, KW * 128], BF16)
        sTm = sb("sTm", [128, 8, NT], BF16)
        hT = sb("hT", [128, 8, NT], BF16)
        NAF, NAM, NBF, NBM = 4, 2, 12, 4
        ringAF = sb("ringAF", [128, NAF, 2048], BF16)
        ringAM = sb("ringAM", [128, NAM, 2048], BF16)
        ringBF = sb("ringBF", [128, NBF, D], BF16)
        ringBM = sb("ringBM", [128, NBM, D], BF16)
        lnb = sb("lnb", [128, NLN, 2, D])
        mixb = sb("mixb", [128, 2, D])
        sgtF = sb("sgtF", [128, 2, 512])
        sgtM = sb("sgtM", [128, 1, 512])
        vbh = sb("vbh", [128, 1, 512])
        ktab = sb("ktab", [128, 272])
        convc = {L: sb(f"convc{L}", [128, 288]) for L in conv_layers}
        hbias = {L: sb(f"hb{L}", [128, 16]) for L in conv_layers}
        uh = {L: sb(f"uh{L}", [128, 8, UP], BF16) for L in conv_layers}
        xh = {L: sb(f"xh{L}", [128, 8, PP]) for L in pool_layers}
        stt = sb("stt", [128, 2, NTT, 2, 6])
        mv = sb("mv", [128, 2, NTT, 2])
        rs = sb("rs", [128, 2, 2, NTT])
        fx = sb("fx", [128, 2, 16])
        ps = es.enter_context(nc.psum_tensor("ps", [128, 8, 512], F32))

        ident = ktab[:, 0:128]
        onesm = ktab[:, 128:256]
        invc = ktab[:, 256:272]
        s1b = s1[:].bitcast(BF16)

        S.op("sp", lambda e: e.dma_start(out=ktab[:], in_=ktab_d), writes=[("KT",)], dma="d_kt")
        for L in conv_layers:
            S.op("sp", lambda e, L=L: e.dma_start(out=convc[L][:], in_=convc_d[L]),
                 writes=[("CC", L)], dma=f"d_cc{L}")
            S.op("dve", lambda e, L=L: e.tensor_scalar(
                out=hbias[L][:], in0=convc[L][:, 0:16], scalar1=0.5, scalar2=None, op0=ALU.mult),
                reads=[("CC", L)], writes=[("HB", L)])

        ra_n = [0]
        rb_n = [0]
        ln_n = [0]
        pp_n = {}

        def flip(k, n=2):
            v = pp_n.get(k, 0)
            pp_n[k] = v + 1
            return v % n

        def xt_keys(b):
            return [("XT", b, tt, kh) for tt in range(NTT) for kh in range(2)]

        rn = {"AF": 0, "AM": 0, "BF": 0, "BM": 0}

        def load_AM(src_ap):
            slot = rn["AM"] % NAM
            rn["AM"] += 1
            S.op("pool", lambda e: e.dma_start(out=ringAM[:, slot, :], in_=src_ap),
                 writes=[("RAM", slot)], dma=f"d_ram{slot}")
            return slot

        def load_AF2(src0, src1):
            slot = rn["AF"] % NAF
            rn["AF"] += 1
            S.op("pool", lambda e: e.dma_start(out=ringAF[:, slot, 0:1024], in_=src0),
                 writes=[("RAF", slot, 0)], dma=f"d_rafg{slot}")
            S.op("pool", lambda e: e.dma_start(out=ringAF[:, slot, 1024:2048], in_=src1),
                 writes=[("RAF", slot, 1)], dma=f"d_rafu{slot}")
            return slot

        def load_BF(src_ap):
            slot = rn["BF"] % NBF
            rn["BF"] += 1
            S.op("pool", lambda e: e.dma_start(out=ringBF[:, slot, :], in_=src_ap),
                 writes=[("RBF", slot)], dma=f"d_rbf{slot}")
            return slot

        def load_BM(src_ap):
            slot = rn["BM"] % NBM
            rn["BM"] += 1
            S.op("pool", lambda e: e.dma_start(out=ringBM[:, slot, :], in_=src_ap),
                 writes=[("RBM", slot)], dma=f"d_rbm{slot}")
            return slot

        def load_ln(L, which):
            par = 0 if which == 1 else 1
            src = lnp_d[L, 2 * which:2 * which + 2].rearrange("a p n -> p a n")
            S.op("sp", lambda e: e.dma_start(out=lnb[:, par, :, :], in_=src),
                 writes=[("LNB", par)], dma=f"d_ln{par}")
            return par

        def transposes(b, tt, banks, to_xT, to_s1):
            for kh in range(2):
                bank = banks[kh]

                def pe_fn(e, bank=bank, kh=kh):
                    ins = None
                    for q in range(4):
                        k = kh * 4 + q
                        ins = e.transpose(out=ps[:, bank, q * 128:(q + 1) * 128],
                                          in_=xtok[b][:, tt, k * 128:(k + 1) * 128],
                                          identity=ident)
                    return ins
                S.op("pe", pe_fn, reads=[("X", b, tt, kh), ("KT",)], writes=[("PS", bank)], dur=0.5)
                if to_xT:
                    S.op("act", lambda e, bank=bank, kh=kh: e.activation(
                        out=xT[b][:, kh * 4:(kh + 1) * 4, tt * 128:(tt + 1) * 128],
                        in_=ps[:, bank, :].rearrange("p (a b) -> p a b", a=4),
                        func=AF.Copy),
                        reads=[("PS", bank)], writes=[("XT", b, tt, kh)])
                if to_s1:
                    S.op("act", lambda e, bank=bank, kh=kh: e.activation(
                        out=s1[:, kh * 4:(kh + 1) * 4, PP + tt * 128:PP + (tt + 1) * 128],
                        in_=ps[:, bank, :].rearrange("p (a b) -> p a b", a=4), func=AF.Copy),
                        reads=[("PS", bank)],
                        writes=[("S1", c) for c in range(kh * 4, kh * 4 + 4)])

        def ln_stats(st, b, tt):
            for eh in range(2):
                S.op("dve", lambda e, eh=eh: e.bn_stats(
                    out=stt[:, st, tt, eh, :], in_=xtok[b][:, tt, eh * 512:(eh + 1) * 512]),
                    reads=[("X", b, tt, eh)], writes=[("STT", st, tt, eh)])
            S.op("dve", lambda e: e.bn_aggr(
                out=mv[:, st, tt, :], in_=stt[:, st, tt, :, :].rearrange("p a b -> p (a b)")),
                reads=[("STT", st, tt, 0), ("STT", st, tt, 1)], writes=[("MV", st, tt)], dur=0.25)

        def ln_finish(st, b, par, banks, do_tr, store_rows=None):
            S.op("act", lambda e: e.activation(out=rs[:, st, 0, :], in_=mv[:, st, :, 1], func=AF.Sqrt,
                                               bias=EPS, scale=1.0),
                 reads=[("MV", st, tt) for tt in range(NTT)], writes=[("RS", st, 0)], dur=3.0)
            S.op("dve", lambda e: e.reciprocal(out=rs[:, st, 1, :], in_=rs[:, st, 0, :]),
                 reads=[("RS", st, 0)], writes=[("RS", st, 1)], dur=0.2)
            yield
            for tt in range(NTT):
                xs = xtok[b][:, tt, :]
                xk = [("X", b, tt, 0), ("X", b, tt, 1)]
                S.op("dve", lambda e, tt=tt, xs=xs: e.scalar_tensor_tensor(
                    out=xs, in0=xs, scalar=mv[:, st, tt, 0:1],
                    in1=lnb[:, par, 0, :], op0=ALU.subtract, op1=ALU.mult),
                    reads=xk + [("MV", st, tt), ("LNB", par)], writes=xk, dur=1.25)
                S.op("dve", lambda e, tt=tt, xs=xs: e.scalar_tensor_tensor(
                    out=xs, in0=xs, scalar=rs[:, st, 1, tt:tt + 1],
                    in1=lnb[:, par, 1, :], op0=ALU.mult, op1=ALU.add),
                    reads=xk + [("RS", st, 1), ("LNB", par)], writes=xk, dur=1.25)
                if do_tr:
                    transposes(b, tt, banks, True, False)
                if store_rows is not None:
                    r0 = store_rows + tt * 128
                    S.op("sp", lambda e, tt=tt, r0=r0: e.dma_start(
                        out=y_d[r0:r0 + 128, :], in_=xtok[b][:, tt, :]),
                        reads=xk, dma=f"d_st{b}_{tt}")
                yield

        def ffn_stage(b, L, q, do_tr, store):
            groups = []
            j = 0
            first = NJ % G if NJ % G else G
            while j < NJ:
                gs = first if j == 0 else G
                groups.append(list(range(j, j + gs)))
                j += gs
            slotsB = {}

            def gu(gi):
                for jj, j in enumerate(groups[gi]):
                    sa = load_AF2(wg_d[L][j], wu_d[L][j])
                    slotsB[j] = load_BF(wd_d[L][j])
                    hidx = (gi % 2) * 4 + jj
                    bg = flip("Fg")
                    bu = 2

                    def mm(e, off, bank, sa=sa):
                        ins = None
                        for k in range(8):
                            ins = e.matmul(ps[:, bank, :],
                                           ringAF[:, sa, off + k * 128:off + (k + 1) * 128],
                                           xT[b][:, k, :], start=(k == 0), stop=(k == 7))
                        return ins
                    S.op("pe", lambda e, mm=mm, bg=bg: mm(e, 0, bg),
                         reads=[("RAF", sa, 0)] + xt_keys(b), writes=[("PS", bg)])
                    S.op("pe", lambda e, mm=mm, bu=bu: mm(e, 1024, bu),
                         reads=[("RAF", sa, 1)] + xt_keys(b), writes=[("PS", bu)])
                    qq = flip("Fs")
                    S.op("act", lambda e, bg=bg, qq=qq: e.activation(
                        out=sgtF[:, qq, :], in_=ps[:, bg, :], func=AF.Silu),
                        reads=[("PS", bg)], writes=[("SGF", qq)])
                    S.op("dve", lambda e, bu=bu, qq=qq, hidx=hidx: e.tensor_tensor(
                        out=hT[:, hidx, :], in0=ps[:, bu, :], in1=sgtF[:, qq, :], op=ALU.mult),
                        reads=[("PS", bu), ("SGF", qq)], writes=[("HT", hidx)])
                    yield

            def down(gi, last):
                grp = groups[gi]
                for tt in range(NTT):
                    def mm(e, tt=tt):
                        ins = None
                        for eh in range(2):
                            for jj, j in enumerate(grp):
                                hidx = (gi % 2) * 4 + jj
                                ins = e.matmul(ps[:, 3 + eh, :],
                                               hT[:, hidx, tt * 128:(tt + 1) * 128],
                                               ringBF[:, slotsB[j], eh * 512:(eh + 1) * 512],
                                               start=(jj == 0), stop=(jj == len(grp) - 1))
                        return ins
                    S.op("pe", mm,
                         reads=[("RBF", slotsB[j]) for j in grp] +
                               [("HT", (gi % 2) * 4 + jj) for jj in range(len(grp))],
                         writes=[("PS", 3), ("PS", 4)], dur=0.5 * len(grp))
                    xs = xtok[b][:, tt, :]
                    pv = ps[:, 3:5, :].rearrange("p a n -> p (a n)")
                    xk = [("X", b, tt, 0), ("X", b, tt, 1)]
                    if gi == 0:
                        S.op("dve", lambda e, xs=xs, pv=pv: e.scalar_tensor_tensor(
                            out=xs, in0=xs, scalar=ALPHA, in1=pv, op0=ALU.mult, op1=ALU.add),
                            reads=[("PS", 3), ("PS", 4)] + xk, writes=xk, dur=1.25)
                    else:
                        S.op("dve", lambda e, xs=xs, pv=pv: e.tensor_tensor(
                            out=xs, in0=xs, in1=pv, op=ALU.add),
                            reads=[("PS", 3), ("PS", 4)] + xk, writes=xk, dur=1.25)
                    if last:
                        ln_stats(0, b, tt)
                    yield

            ng = len(groups)
            for gi in range(ng):
                g_it = gu(gi)
                d_it = down(gi - 1, False) if gi >= 1 else iter(())
                g_alive = d_alive = True
                while g_alive or d_alive:
                    if g_alive:
                        try:
                            next(g_it)
                            yield
                        except StopIteration:
                            g_alive = False
                    if d_alive:
                        try:
                            next(d_it)
                            yield
                        except StopIteration:
                            d_alive = False
            par = load_ln(L, 1)
            yield from down(ng - 1, True)
            inline = yield "DECIDE"
            banks = (0, 1) if inline else (5, 6)
            yield from ln_finish(0, b, par, banks, do_tr,
                                 store_rows=(q * NT if store else None))

        def load_block(b, q):
            for tt in range(NTT):
                r0 = q * NT + tt * 128
                S.op("sp", lambda e, tt=tt, r0=r0: e.dma_start(out=xtok[b][:, tt, :], in_=x_d[r0:r0 + 128, :]),
                     writes=[("X", b, tt, 0), ("X", b, tt, 1)], dma=f"d_x{b}_{tt}")

        def conv_stage(b, L, q, first):
            cc = convc[L]
            hb = hbias[L]
            CK = ("CC", L)
            if first:
                load_block(b, q)
                for tt in range(NTT):
                    transposes(b, tt, (5, 6), True, False)
                    yield
            if q == 0:
                S.op("dve", lambda e: e.memset(s1b[:, :, 0:UP], 0.0),
                     writes=[("S1", c) for c in range(8)])
            else:
                S.op("dve", lambda e: e.tensor_copy(out=s1b[:, :, 0:UP], in_=uh[L][:]),
                     reads=[("UH", L)], writes=[("S1", c) for c in range(8)])
            S.op("sp", lambda e: e.dma_start(out=mixb[:, 0, :], in_=mixb_d[L, 0]),
                 writes=[("MB", 0)], dma="d_mb")
            par = load_ln(L, 0)

            def mkdiag(c):
                i = c % 2
                dst = dgb[:, i, :].rearrange("p (k j) -> p k j", k=KW)
                wv = cc[:, 16 + c * KW:16 + (c + 1) * KW]
                S.op("dve", lambda e: e.tensor_tensor(
                    out=dst, in0=ident.unsqueeze(1).to_broadcast([128, KW, 128]),
                    in1=wv.unsqueeze(2).to_broadcast([128, KW, 128]), op=ALU.mult),
                    reads=[CK, ("KT",)], writes=[("DG", i)], dur=4.3)

            def p1(c):
                sa = load_AM(win_d[L][c])

                def mm(e, off, bank):
                    ins = None
                    for k in range(8):
                        ins = e.matmul(ps[:, bank, :],
                                       ringAM[:, sa, off + k * 128:off + (k + 1) * 128],
                                       xT[b][:, k, :], start=(k == 0), stop=(k == 7))
                    return ins
                S.op("pe", lambda e: mm(e, 0, 5), reads=[("RAM", sa)] + xt_keys(b), writes=[("PS", 5)])
                S.op("pe", lambda e: mm(e, 1024, 6), reads=[("RAM", sa)] + xt_keys(b), writes=[("PS", 6)])
                qq = 0
                S.op("act", lambda e: e.activation(
                    out=vbh[:, qq, :], in_=ps[:, 5, :], func=AF.Identity,
                    bias=hb[:, c:c + 1], scale=0.5),
                    reads=[("PS", 5), ("HB", L)], writes=[("VB", qq)])
                S.op("act", lambda e: e.activation(
                    out=sgtM[:, qq, :], in_=ps[:, 6, :], func=AF.Tanh,
                    bias=hb[:, 8 + c:9 + c], scale=0.5),
                    reads=[("PS", 6), ("HB", L)], writes=[("SGM", qq)])
                S.op("dve", lambda e: e.scalar_tensor_tensor(
                    out=s1b[:, c, UP:UP + NT], in0=sgtM[:, qq, :], scalar=1.0, in1=vbh[:, qq, :],
                    op0=ALU.add, op1=ALU.mult),
                    reads=[("SGM", qq), ("VB", qq)], writes=[("S1", c)])

            def conv(c):
                i = c % 2

                def mm(e):
                    ins = None
                    for k in range(KW):
                        ins = e.matmul(ps[:, 7, :], dgb[:, i, k * 128:(k + 1) * 128],
                                       s1b[:, c, 2 + k:2 + k + NT], start=(k == 0), stop=(k == KW - 1))
                    return ins
                S.op("pe", mm, reads=[("S1", c), ("DG", i)], writes=[("PS", 7)], dur=7.0)
                S.op("act", lambda e: e.activation(
                    out=s2[:, c, :], in_=ps[:, 7, :], func=AF.Identity,
                    bias=cc[:, 264 + c:265 + c], scale=1.0),
                    reads=[("PS", 7), CK], writes=[("S2", c)])

            for c in range(8):
                mkdiag(c)
                p1(c)
                yield
                if c >= 1:
                    conv(c - 1)
                    yield
            if q < NQ - 1:
                S.op("dve", lambda e: e.tensor_copy(out=uh[L][:], in_=s1b[:, :, NT:NT + UP]),
                     reads=[("S1", c) for c in range(8)], writes=[("UH", L)])
            conv(7)
            yield
            for tt in range(NTT):
                xs = xtok[b][:, tt, :]
                xk = [("X", b, tt, 0), ("X", b, tt, 1)]
                S.op("dve", lambda e, xs=xs: e.scalar_tensor_tensor(
                    out=xs, in0=xs, scalar=ALPHA, in1=mixb[:, 0, :], op0=ALU.mult, op1=ALU.add),
                    reads=[("MB", 0)] + xk, writes=xk, dur=1.25)
            yield
            sB = {}
            for kp in range(4):
                sB[(0, kp)] = load_BM(wout_d[L][0, kp])
            sq = [s1[:, 4 + i, 0:NT] for i in range(2)]
            mean_t = s1[:, 0, 0:NT]
            rstd_t = s1[:, 1, 0:NT]
            tmp_t = s1[:, 2, 0:NT]
            for c in range(8):
                i = c % 2
                S.op("act", lambda e, c=c, i=i: e.activation(out=sq[i], in_=s2[:, c, :], func=AF.Square),
                     reads=[("S2", c)], writes=[("S1", 4 + i)])

                def mm(e, c=c, i=i):
                    e.matmul(ps[:, 5, :], onesm, s2[:, c, :], start=(c == 0), stop=(c == 7))
                    return e.matmul(ps[:, 6, :], onesm, sq[i], start=(c == 0), stop=(c == 7))
                S.op("pe", mm, reads=[("S2", c), ("S1", 4 + i), ("KT",)], writes=[("PS", 5), ("PS", 6)])
                if c % 2 == 1:
                    yield
            for kp in range(4):
                S.op("pool", lambda e, kp=kp: e.dma_start(out=s1b[:, 4 + kp, 0:D], in_=wout_d[L][1, kp]),
                     writes=[("S1", 4 + kp)], dma=f"d_w1_{kp}")
            S.op("dve", lambda e: e.tensor_copy(out=mean_t, in_=ps[:, 5, :]),
                 reads=[("PS", 5)], writes=[("S1", 0)])
            S.op("dve", lambda e: e.tensor_tensor(out=tmp_t, in0=mean_t, in1=mean_t, op=ALU.mult),
                 reads=[("S1", 0)], writes=[("S1", 2)])
            S.op("dve", lambda e: e.tensor_tensor(out=tmp_t, in0=ps[:, 6, :], in1=tmp_t, op=ALU.subtract),
                 reads=[("PS", 6), ("S1", 2)], writes=[("S1", 2)])
            S.op("act", lambda e: e.activation(out=tmp_t, in_=tmp_t, func=AF.Sqrt, bias=EPS, scale=1.0),
                 reads=[("S1", 2)], writes=[("S1", 2)], dur=3.0)
            S.op("dve", lambda e: e.reciprocal(out=rstd_t, in_=tmp_t),
                 reads=[("S1", 2)], writes=[("S1", 1)])
            yield
            for c0 in range(0, 8, 2):
                for c in (c0, c0 + 1):
                    S.op("dve", lambda e, c=c: e.tensor_tensor(out=s2[:, c, :], in0=s2[:, c, :],
                                                               in1=mean_t, op=ALU.subtract),
                         reads=[("S2", c), ("S1", 0)], writes=[("S2", c)])
                for c in (c0, c0 + 1):
                    S.op("dve", lambda e, c=c: e.tensor_tensor(out=s2[:, c, :], in0=s2[:, c, :],
                                                               in1=rstd_t, op=ALU.mult),
                         reads=[("S2", c), ("S1", 1)], writes=[("S2", c)])
                for c in (c0, c0 + 1):
                    S.op("act", lambda e, c=c: e.activation(
                        out=sTm[:, c, :], in_=s2[:, c, :], func=AF.Silu,
                        bias=cc[:, 280 + c:281 + c], scale=cc[:, 272 + c:273 + c]),
                        reads=[("S2", c), CK], writes=[("STM", c)])
                yield
            for eh in range(2):
                for tt in range(NTT):
                    bank = 5 + flip("Mo", 3)

                    def mm(e, bank=bank, tt=tt, eh=eh):
                        ins = None
                        for c in range(8):
                            if eh == 0:
                                w = ringBM[:, sB[(0, c // 2)], (c % 2) * 512:(c % 2 + 1) * 512]
                            else:
                                w = s1b[:, 4 + c // 2, (c % 2) * 512:(c % 2 + 1) * 512]
                            ins = e.matmul(ps[:, bank, :], sTm[:, c, tt * 128:(tt + 1) * 128], w,
                                           start=(c == 0), stop=(c == 7))
                        return ins
                    wk = ([("RBM", sB[(0, kp)]) for kp in range(4)] if eh == 0
                          else [("S1", 4 + kp) for kp in range(4)])
                    S.op("pe", mm, reads=wk + [("STM", c) for c in range(8)], writes=[("PS", bank)])
                    xs = xtok[b][:, tt, eh * 512:(eh + 1) * 512]
                    S.op("dve", lambda e, xs=xs, bank=bank: e.tensor_tensor(
                        out=xs, in0=xs, in1=ps[:, bank, :], op=ALU.add),
                        reads=[("PS", bank), ("X", b, tt, eh)], writes=[("X", b, tt, eh)])
                    if eh == 1:
                        ln_stats(1, b, tt)
                    yield
            yield from ln_finish(1, b, par, (5, 6), True)

        def pool_stage(b, L, q, first):
            if first:
                load_block(b, q)
            for tt in range(NTT):
                transposes(b, tt, (5, 6), False, True)
                yield
            if q == 0:
                S.op("dve", lambda e: e.memset(s1[:, :, 0:PP], 0.0),
                     writes=[("S1", c) for c in range(8)])
            else:
                S.op("dve", lambda e: e.tensor_copy(out=s1[:, :, 0:PP], in_=xh[L][:]),
                     reads=[("XH", L)], writes=[("S1", c) for c in range(8)])
            if q < NQ - 1:
                S.op("dve", lambda e: e.tensor_copy(out=xh[L][:], in_=s1[:, :, NT:NT + PP]),
                     reads=[("S1", c) for c in range(8)], writes=[("XH", L)])
            S.op("sp", lambda e: e.dma_start(out=mixb[:, :, :],
                                             in_=mixb_d[L].rearrange("a p n -> p a n")),
                 writes=[("MB", 0), ("MB", 1)], dma="d_mb")
            S.op("dve", lambda e: e.tensor_tensor(out=mixb[:, 0, :], in0=mixb[:, 0, :],
                                                  in1=mixb[:, 1, :], op=ALU.mult),
                 reads=[("MB", 0), ("MB", 1)], writes=[("MB", 0)])
            sa = load_AM(wp_d[L])
            par = load_ln(L, 0)
            W = NT + PP
            for g in range(4):
                w = POOL_W[g]
                cs = (2 * g, 2 * g + 1)
                cur = {c: (s1[:, c, 0:W], [("S1", c)]) for c in cs}
                shift = 1
                for step in range(g + 1):
                    lo = 2 * shift - 1
                    for i, c in enumerate(cs):
                        r0 = 4 * i + 2 * (step % 2)
                        dst = s2[:, r0:r0 + 2, :].rearrange("p a n -> p (a n)")[:, 0:W]
                        dkeys = [("S2", r0), ("S2", r0 + 1)]
                        src, skeys = cur[c]
                        S.op("dve", lambda e, dst=dst, src=src, lo=lo, shift=shift: e.tensor_tensor(
                            out=dst[:, lo:W], in0=src[:, lo:W], in1=src[:, lo - shift:W - shift],
                            op=ALU.add),
                            reads=skeys, writes=dkeys)
                        cur[c] = (dst, dkeys)
                    shift *= 2
                for i, c in enumerate(cs):
                    src, skeys = cur[c]
                    S.op("dve", lambda e, c=c, src=src, w=w: e.scalar_tensor_tensor(
                        out=sTm[:, c, :], in0=src[:, PP:PP + NT], scalar=1.0 / w,
                        in1=s1[:, c, PP:PP + NT], op0=ALU.mult, op1=ALU.subtract),
                        reads=[("S1", c)] + skeys, writes=[("STM", c)])
                if q == 0:
                    for i, c in enumerate(cs):
                        src, skeys = cur[c]
                        S.op("dve", lambda e, i=i, src=src, w=w: e.tensor_tensor(
                            out=fx[:, i, 0:w - 1], in0=src[:, PP:PP + w - 1], in1=invc[:, 0:w - 1],
                            op=ALU.mult),
                            reads=skeys + [("KT",)], writes=[("FX", i)])
                    for i, c in enumerate(cs):
                        S.op("dve", lambda e, i=i, c=c, w=w: e.tensor_tensor(
                            out=sTm[:, c, 0:w - 1], in0=fx[:, i, 0:w - 1], in1=s1[:, c, PP:PP + w - 1],
                            op=ALU.subtract),
                            reads=[("FX", i), ("S1", c), ("STM", c)], writes=[("STM", c)])
                yield
            for tt in range(NTT):
                for gh in range(2):
                    bank = 5 + flip("Mo", 3)

                    def mm(e, bank=bank, tt=tt, gh=gh):
                        ins = None
                        for gg in range(2):
                            g = 2 * gh + gg
                            for kk in range(2):
                                off = (g * 2 + kk) * 256
                                ins = e.matmul(ps[:, bank, gg * 256:(gg + 1) * 256],
                                               sTm[:, 2 * g + kk, tt * 128:(tt + 1) * 128],
                                               ringAM[:, sa, off:off + 256],
                                               start=(kk == 0), stop=(kk == 1))
                        return ins
                    S.op("pe", mm, reads=[("RAM", sa)] + [("STM", c) for c in range(4 * gh, 4 * gh + 4)],
                         writes=[("PS", bank)], dur=0.6)
                    qq = 0
                    sl = slice(gh * 512, (gh + 1) * 512)
                    xs = xtok[b][:, tt, sl]
                    S.op("dve", lambda e, bank=bank, qq=qq, sl=sl: e.tensor_tensor(
                        out=sgtM[:, qq, :], in0=ps[:, bank, :], in1=mixb[:, 1, sl], op=ALU.mult),
                        reads=[("PS", bank), ("MB", 1)], writes=[("SGM", qq)])
                    S.op("dve", lambda e, xs=xs, qq=qq: e.scalar_tensor_tensor(
                        out=xs, in0=xs, scalar=ALPHA, in1=sgtM[:, qq, :], op0=ALU.mult, op1=ALU.add),
                        reads=[("SGM", qq), ("X", b, tt, gh)], writes=[("X", b, tt, gh)])
                    S.op("dve", lambda e, xs=xs, sl=sl: e.tensor_tensor(
                        out=xs, in0=xs, in1=mixb[:, 0, sl], op=ALU.add),
                        reads=[("MB", 0), ("X", b, tt, gh)], writes=[("X", b, tt, gh)])
                ln_stats(1, b, tt)
                yield
            yield from ln_finish(1, b, par, (5, 6), True)

        stages = []
        for pair in ((0, 1), (2, 3)):
            for li, L in enumerate(layers):
                for q in pair:
                    stages.append((q, li, L))

        def mk_mixer(k):
            q, li, L = stages[k]
            b = q % 2
            if L % 2 == 0:
                return conv_stage(b, L, q, li == 0)
            return pool_stage(b, L, q, li == 0)

        def mk_ffn(k):
            q, li, L = stages[k]
            b = q % 2
            is_last = (li == len(layers) - 1)
            nxt_conv = (not is_last) and (layers[li + 1] % 2 == 0)
            return ffn_stage(b, L, q, nxt_conv, is_last)

        def drain(g):
            for _ in g:
                pass

        class Stream:
            def __init__(self, g, eager=False):
                self.g = g
                self.fifo = []
                self.alive = True
                self.at_decide = False
                self.send = None
                if eager:
                    while self.alive:
                        self.pump()

            def pump(self):
                S.capture = self.fifo
                try:
                    if self.send is not None:
                        r = self.g.send(self.send)
                        self.send = None
                    else:
                        r = next(self.g)
                    if r == "DECIDE":
                        self.at_decide = True
                        self.alive = False
                except StopIteration:
                    self.alive = False
                S.capture = None

            def fill(self):
                while self.alive and not self.fifo:
                    self.pump()
                return bool(self.fifo)

            def pe_left(self):
                return sum((S.DEF_DUR["pe"] if d[5] is None else d[5]) for d in self.fifo if d[0] == "pe")

        def merge(streams, until=None):
            while True:
                live = [s for s in streams if s.fill()]
                if until is not None and not until.fill():
                    return
                if not live:
                    return
                def cost(s):
                    d = s.fifo[0]
                    st = S.estimate(d)
                    if d[0] == "pe":
                        st += PE_STALL_W * max(0.0, st - S.eng_free["pe"])
                    return st + _jit.uniform(0.0, JITTER_US)
                best = min(live, key=cost)
                S.commit(best.fifo.pop(0))

        TAIL_INLINE_US = 0.0
        JITTER_US = 0.3
        _jit = random.Random(JITTER_SEED)
        PE_STALL_W = 0.0
        drain(mk_mixer(0))
        ftail = None
        for k in range(len(stages)):
            fs = Stream(mk_ffn(k))
            if ftail is not None:
                merge([fs, ftail], until=ftail)
                ftail = None
            ms = Stream(mk_mixer(k + 1), eager=True) if k + 1 < len(stages) else None
            while True:
                merge([s for s in (fs, ms) if s is not None], until=fs)
                if fs.at_decide:
                    fs.at_decide = False
                    inline = ms is not None and ms.pe_left() > TAIL_INLINE_US
                    fs.send = bool(inline)
                    fs.alive = True
                    if not inline:
                        ftail = fs
                        break
                else:
                    break
            if ms is not None:
                merge([ms])
        if ftail is not None:
            merge([ftail])
        S.final_wait("sp", [("X", b, tt, eh) for b in range(2) for tt in range(NTT) for eh in range(2)])

        sems = {k: es.enter_context(nc.semaphore(f"s_{k}")) for k in sorted(S.semkeys)}
        block = es.enter_context(nc.Block())

        class FirstIns:
            def __init__(self, e):
                self.e = e
                self.first = None

            def matmul(self, *a, **k):
                r = self.e.matmul(*a, **k)
                if self.first is None:
                    self.first = r
                return r

            def transpose(self, *a, **k):
                r = self.e.transpose(*a, **k)
                if self.first is None:
                    self.first = r
                return r

        def run(name, e):
            for waits, fn, inc in S.ops[name]:
                waits = list(waits)
                fused = None
                if fn is not None and waits:
                    fused = waits.pop()
                for sk, val in waits:
                    e.wait_ge(sems[sk], val)
                if fn is not None:
                    if name == "pe":
                        px = FirstIns(e)
                        ins = fn(px)
                        first = px.first
                    else:
                        ins = fn(e)
                        first = ins
                    if fused is not None:
                        first._wait_ge(sems[fused[0]], fused[1])
                    ins.then_inc(sems[inc[0]], inc[1])

        @block.tensor
        def _(e):
            run("pe", e)

        @block.scalar
        def _(e):
            run("act", e)

        @block.vector
        def _(e):
            run("dve", e)

        @block.gpsimd
        def _(e):
            run("pool", e)

        @block.sync
        def _(e):
            run("sp", e)
    return nc


def _bc(v):
    return np.ascontiguousarray(np.broadcast_to(np.asarray(v, np.float32).reshape(1, -1), (128, D)))


def _prep(inputs, layers):
    f = lambda a: np.ascontiguousarray(np.asarray(a, dtype=np.float32))
    m = {}
    ktab = np.zeros((128, 272), np.float32)
    ktab[:, 0:128] = np.eye(128, dtype=np.float32)
    ktab[:, 128:256] = 1.0 / D
    ktab[:, 256:272] = (1.0 / np.arange(1, 17, dtype=np.float32))[None, :]
    m["ktab"] = ktab
    lnp = np.zeros((DEPTH, 4, 128, D), np.float32)
    mixb = np.zeros((DEPTH, 2, 128, D), np.float32)
    for L in range(DEPTH):
        lnp[L, 0] = _bc(inputs["ln1_g"][L])
        lnp[L, 1] = _bc(inputs["ln1_b"][L])
        lnp[L, 2] = _bc(inputs["ln2_g"][L])
        lnp[L, 3] = _bc(inputs["ln2_b"][L])
        l = L // 2
        if L % 2 == 0:
            mixb[L, 0] = _bc(inputs["a_b_out"][l])
        else:
            mixb[L, 0] = _bc(np.asarray(inputs["p_b"][l]).reshape(-1))
            mixb[L, 1] = _bc(inputs["p_scale"][l])
    m["lnp"] = lnp
    m["mixb"] = mixb
    for L in layers:
        l = L // 2
        wg = f(inputs["ffn_w_gate"][L]).reshape(8, 128, NJ, 128).transpose(2, 1, 0, 3)
        m[f"wg{L}"] = np.ascontiguousarray(wg).reshape(NJ, 128, 1024)
        wu = f(inputs["ffn_w_up"][L]).reshape(8, 128, NJ, 128).transpose(2, 1, 0, 3)
        m[f"wu{L}"] = np.ascontiguousarray(wu).reshape(NJ, 128, 1024)
        m[f"wd{L}"] = f(inputs["ffn_w_down"][L]).reshape(NJ, 128, D)
        if L % 2 == 0:
            win = f(inputs["a_w_in"][l]).reshape(8, 128, 2, 8, 128).transpose(3, 1, 2, 0, 4)
            m[f"win{L}"] = np.ascontiguousarray(win).reshape(8, 128, 2048)
            wo = f(inputs["a_w_out"][l]).reshape(4, 2, 128, 2, 512).transpose(3, 0, 2, 1, 4)
            m[f"wout{L}"] = np.ascontiguousarray(wo).reshape(2, 4, 128, 1024)
            cc = np.zeros((128, 288), np.float32)
            cc[:, 0:16] = f(inputs["a_b_in"][l]).reshape(16, 128).T
            wdw = f(inputs["a_w_dw"][l])[:, 0, :]
            cc[:, 16:264] = wdw.T.reshape(8, 128, KW).transpose(1, 0, 2).reshape(128, 8 * KW)
            cc[:, 264:272] = f(inputs["a_b_dw"][l]).reshape(8, 128).T
            cc[:, 272:280] = f(inputs["a_ln_g"][l]).reshape(8, 128).T
            cc[:, 280:288] = f(inputs["a_ln_b"][l]).reshape(8, 128).T
            m[f"convc{L}"] = cc
        else:
            wp = f(inputs["p_w"][l]).reshape(4, 2, 128, 256).transpose(2, 0, 1, 3)
            m[f"wp{L}"] = np.ascontiguousarray(wp).reshape(128, 2048)
    return m


_CACHE = {}


def _get_nc(layers):
    key = tuple(layers)
    if key not in _CACHE:
        _CACHE[key] = build(list(layers))
    return _CACHE[key]


def _run(inputs, x, layers):
    common = _prep(inputs, layers)
    nc = _get_nc(layers)
    in_maps = []
    for b in range(NB):
        mm = dict(common)
        mm["x"] = np.ascontiguousarray(x[b])
        in_maps.append(mm)
    res = run_bass_kernel_spmd(nc, in_maps, core_ids=list(range(NB)))
    return np.stack([np.asarray(r["y"], dtype=np.float32) for r in res.results], axis=0)


def kernel(**inputs):
    x = np.asarray(inputs["x"], dtype=np.float32)
    return _run(inputs, x, [0, 1, 2, 3])
```

```python
import contextlib
import random
import numpy as np
import concourse.bass as bass
import concourse.mybir as mybir
from concourse.bass_utils import run_bass_kernel_spmd

F32 = mybir.dt.float32
BF16 = mybir.dt.bfloat16
AF = mybir.ActivationFunctionType
ALU = mybir.AluOpType

D = 1024
SEQ = 2048
NB = 8
DEPTH = 4
DFF = 2816
NJ = DFF // 128
KW = 31
ALPHA = float((2 * DEPTH) ** 0.25)
EPS = 1e-5
NT = 512
NTT = NT // 128
NQ = SEQ // NT
G = 4
NA = 4
NBS = 12
NLN = 2
UP = 32
PP = 16
POOL_W = (2, 4, 8, 16)
JITTER_SEED = 6
SELF_WIN = 1000000000


class Sched:
    ENGS = ("pe", "act", "dve", "pool", "sp")
    DEF_DUR = {"pe": 2.0, "act": 0.7, "dve": 0.7, "pool": 0.65, "sp": 0.15}
    DMA_LAT = 3.0
    SEM_LAT = 0.35

    def __init__(self):
        self.ops = {e: [] for e in self.ENGS}
        self.cnt = {e: 0 for e in self.ENGS}
        self.dcnt = {}
        self.last_w = {}
        self.readers = {}
        self.seen = {e: {} for e in self.ENGS}
        self.semkeys = set(self.ENGS)
        self.capture = None
        self.eng_free = {e: 0.0 for e in self.ENGS}
        self.t_w = {}
        self.t_r = {}

    def op(self, eng, fn, reads=(), writes=(), dma=None, dur=None):
        desc = (eng, fn, tuple(reads), tuple(writes), dma, dur)
        if self.capture is not None:
            self.capture.append(desc)
            return None
        return self.commit(desc)

    def estimate(self, desc):
        eng, fn, reads, writes, dma, dur = desc
        ready = 0.0
        for r in reads:
            ready = max(ready, self.t_w.get(r, 0.0))
        for w in writes:
            ready = max(ready, self.t_w.get(w, 0.0), self.t_r.get(w, 0.0))
        return max(self.eng_free[eng], ready + self.SEM_LAT)

    def commit(self, desc):
        eng, fn, reads, writes, dma, dur = desc
        start = self.estimate(desc)
        d = self.DEF_DUR[eng] if dur is None else dur
        if dma is None:
            fin = start + d
            self.eng_free[eng] = fin
        else:
            self.eng_free[eng] = start + d
            fin = start + d + self.DMA_LAT
        for r in reads:
            self.t_r[r] = max(self.t_r.get(r, 0.0), fin)
        for w in writes:
            self.t_w[w] = fin
            self.t_r[w] = 0.0
        idx = len(self.ops[eng])
        deps = []
        for r in reads:
            if r in self.last_w:
                deps.append(self.last_w[r])
        for w in writes:
            if w in self.last_w:
                deps.append(self.last_w[w])
            deps.extend(self.readers.get(w, ()))
        need = {}
        for (sk, val, peng, pidx, pdma) in deps:
            if (not pdma) and peng == eng and dma is None:
                if pidx < idx - SELF_WIN or eng == "pe":
                    continue
            if self.seen[eng].get(sk, 0) >= val:
                continue
            if need.get(sk, 0) < val:
                need[sk] = val
        for sk, val in need.items():
            self.seen[eng][sk] = val
        if dma is None:
            self.cnt[eng] += 1
            tick = (eng, self.cnt[eng], eng, idx, False)
            inc = (eng, 1)
        else:
            self.semkeys.add(dma)
            self.dcnt[dma] = self.dcnt.get(dma, 0) + 1
            tick = (dma, 16 * self.dcnt[dma], eng, idx, True)
            inc = (dma, 16)
        for r in reads:
            self.readers.setdefault(r, []).append(tick)
        for w in writes:
            self.last_w[w] = tick
            self.readers[w] = []
        self.ops[eng].append((list(need.items()), fn, inc))
        return tick

    def final_wait(self, eng, keys):
        need = {}
        for k in keys:
            t = self.last_w.get(k)
            cands = list(self.readers.get(k, ()))
            if t is not None:
                cands.append(t)
            for (sk, val, *_r) in cands:
                if need.get(sk, 0) < val:
                    need[sk] = val
        self.ops[eng].append((list(need.items()), None, None))


def build(layers):
    nc = bass.Bass("TRN2", target_bir_lowering=False)
    conv_layers = [L for L in layers if L % 2 == 0]
    pool_layers = [L for L in layers if L % 2 == 1]

    def dram(name, shape, kind="ExternalInput"):
        return nc.dram_tensor(name, list(shape), F32, kind=kind).ap()

    x_d = dram("x", [SEQ, D])
    y_d = dram("y", [SEQ, D], kind="ExternalOutput")
    ktab_d = dram("ktab", [128, 272])
    lnp_d = dram("lnp", [DEPTH, 4, 128, D])
    mixb_d = dram("mixb", [DEPTH, 2, 128, D])
    wg_d = {L: dram(f"wg{L}", [NJ, 128, 8 * 128]) for L in layers}
    wu_d = {L: dram(f"wu{L}", [NJ, 128, 8 * 128]) for L in layers}
    wd_d = {L: dram(f"wd{L}", [NJ, 128, D]) for L in layers}
    win_d = {L: dram(f"win{L}", [8, 128, 2 * 8 * 128]) for L in conv_layers}
    wout_d = {L: dram(f"wout{L}", [2, 4, 128, D]) for L in conv_layers}
    convc_d = {L: dram(f"convc{L}", [128, 288]) for L in conv_layers}
    wp_d = {L: dram(f"wp{L}", [128, 8 * 256]) for L in pool_layers}

    S = Sched()
    es = contextlib.ExitStack()
    with es:
        def sb(name, shape, dt=F32):
            return es.enter_context(nc.sbuf_tensor("sb_" + name, list(shape), dt))

        xtok = [sb(f"xtok{b}", [128, NTT, D]) for b in range(2)]
        xT = [sb(f"xT{b}", [128, 8, NT], BF16) for b in range(2)]
        s1 = sb("s1", [128, 8, NT + UP])
        s2 = sb("s2", [128, 8, NT])
        dgb = sb("dgb", [128, 2, KW * 128], BF16)
        sTm = sb("sTm", [128, 8, NT], BF16)
        hT = sb("hT", [128, 8, NT], BF16)
        NAF, NAM, NBF, NBM = 4, 2, 12, 4
        ringAF = sb("ringAF", [128, NAF, 2048], BF16)
        ringAM = sb("ringAM", [128, NAM, 2048], BF16)
        ringBF = sb("ringBF", [128, NBF, D], BF16)
        ringBM = sb("ringBM", [128, NBM, D], BF16)
        lnb = sb("lnb", [128, NLN, 2, D])
        mixb = sb("mixb", [128, 2, D])
        sgtF = sb("sgtF", [128, 2, 512])
        sgtM = sb("sgtM", [128, 1, 512])
        vbh = sb("vbh", [128, 1, 512])
        ktab = sb("ktab", [128, 272])
        convc = {L: sb(f"convc{L}", [128, 288]) for L in conv_layers}
        hbias = {L: sb(f"hb{L}", [128, 16]) for L in conv_layers}
        uh = {L: sb(f"uh{L}", [128, 8, UP], BF16) for L in conv_layers}
        xh = {L: sb(f"xh{L}", [128, 8, PP]) for L in pool_layers}
        stt = sb("stt", [128, 2, NTT, 2, 6])
        mv = sb("mv", [128, 2, NTT, 2])
        rs = sb("rs", [128, 2, 2, NTT])
        fx = sb("fx", [128, 2, 16])
        ps = es.enter_context(nc.psum_tensor("ps", [128, 8, 512], F32))

        ident = ktab[:, 0:128]
        onesm = ktab[:, 128:256]
        invc = ktab[:, 256:272]
        s1b = s1[:].bitcast(BF16)

        S.op("sp", lambda e: e.dma_start(out=ktab[:], in_=ktab_d), writes=[("KT",)], dma="d_kt")
        for L in conv_layers:
            S.op("sp", lambda e, L=L: e.dma_start(out=convc[L][:], in_=convc_d[L]),
                 writes=[("CC", L)], dma=f"d_cc{L}")
            S.op("dve", lambda e, L=L: e.tensor_scalar(
                out=hbias[L][:], in0=convc[L][:, 0:16], scalar1=0.5, scalar2=None, op0=ALU.mult),
                reads=[("CC", L)], writes=[("HB", L)])

        ra_n = [0]
        rb_n = [0]
        ln_n = [0]
        pp_n = {}

        def flip(k, n=2):
            v = pp_n.get(k, 0)
            pp_n[k] = v + 1
            return v % n

        def xt_keys(b):
            return [("XT", b, tt, kh) for tt in range(NTT) for kh in range(2)]

        rn = {"AF": 0, "AM": 0, "BF": 0, "BM": 0}

        def load_AM(src_ap):
            slot = rn["AM"] % NAM
            rn["AM"] += 1
            S.op("pool", lambda e: e.dma_start(out=ringAM[:, slot, :], in_=src_ap),
                 writes=[("RAM", slot)], dma=f"d_ram{slot}")
            return slot

        def load_AF2(src0, src1):
            slot = rn["AF"] % NAF
            rn["AF"] += 1
            S.op("pool", lambda e: e.dma_start(out=ringAF[:, slot, 0:1024], in_=src0),
                 writes=[("RAF", slot, 0)], dma=f"d_rafg{slot}")
            S.op("pool", lambda e: e.dma_start(out=ringAF[:, slot, 1024:2048], in_=src1),
                 writes=[("RAF", slot, 1)], dma=f"d_rafu{slot}")
            return slot

        def load_BF(src_ap):
            slot = rn["BF"] % NBF
            rn["BF"] += 1
            S.op("pool", lambda e: e.dma_start(out=ringBF[:, slot, :], in_=src_ap),
                 writes=[("RBF", slot)], dma=f"d_rbf{slot}")
            return slot

        def load_BM(src_ap):
            slot = rn["BM"] % NBM
            rn["BM"] += 1
            S.op("pool", lambda e: e.dma_start(out=ringBM[:, slot, :], in_=src_ap),
                 writes=[("RBM", slot)], dma=f"d_rbm{slot}")
            return slot

        def load_ln(L, which):
            par = 0 if which == 1 else 1
            src = lnp_d[L, 2 * which:2 * which + 2].rearrange("a p n -> p a n")
            S.op("sp", lambda e: e.dma_start(out=lnb[:, par, :, :], in_=src),
                 writes=[("LNB", par)], dma=f"d_ln{par}")
            return par

        def transposes(b, tt, banks, to_xT, to_s1):
            for kh in range(2):
                bank = banks[kh]

                def pe_fn(e, bank=bank, kh=kh):
                    ins = None
                    for q in range(4):
                        k = kh * 4 + q
                        ins = e.transpose(out=ps[:, bank, q * 128:(q + 1) * 128],
                                          in_=xtok[b][:, tt, k * 128:(k + 1) * 128],
                                          identity=ident)
                    return ins
                S.op("pe", pe_fn, reads=[("X", b, tt, kh), ("KT",)], writes=[("PS", bank)], dur=0.5)
                if to_xT:
                    S.op("act", lambda e, bank=bank, kh=kh: e.activation(
                        out=xT[b][:, kh * 4:(kh + 1) * 4, tt * 128:(tt + 1) * 128],
                        in_=ps[:, bank, :].rearrange("p (a b) -> p a b", a=4),
                        func=AF.Copy),
                        reads=[("PS", bank)], writes=[("XT", b, tt, kh)])
                if to_s1:
                    S.op("act", lambda e, bank=bank, kh=kh: e.activation(
                        out=s1[:, kh * 4:(kh + 1) * 4, PP + tt * 128:PP + (tt + 1) * 128],
                        in_=ps[:, bank, :].rearrange("p (a b) -> p a b", a=4), func=AF.Copy),
                        reads=[("PS", bank)],
                        writes=[("S1", c) for c in range(kh * 4, kh * 4 + 4)])

        def ln_stats(st, b, tt):
            for eh in range(2):
                S.op("dve", lambda e, eh=eh: e.bn_stats(
                    out=stt[:, st, tt, eh, :], in_=xtok[b][:, tt, eh * 512:(eh + 1) * 512]),
                    reads=[("X", b, tt, eh)], writes=[("STT", st, tt, eh)])
            S.op("dve", lambda e: e.bn_aggr(
                out=mv[:, st, tt, :], in_=stt[:, st, tt, :, :].rearrange("p a b -> p (a b)")),
                reads=[("STT", st, tt, 0), ("STT", st, tt, 1)], writes=[("MV", st, tt)], dur=0.25)

        def ln_finish(st, b, par, banks, do_tr, store_rows=None):
            S.op("act", lambda e: e.activation(out=rs[:, st, 0, :], in_=mv[:, st, :, 1], func=AF.Sqrt,
                                               bias=EPS, scale=1.0),
                 reads=[("MV", st, tt) for tt in range(NTT)], writes=[("RS", st, 0)], dur=3.0)
            S.op("dve", lambda e: e.reciprocal(out=rs[:, st, 1, :], in_=rs[:, st, 0, :]),
                 reads=[("RS", st, 0)], writes=[("RS", st, 1)], dur=0.2)
            yield
            for tt in range(NTT):
                xs = xtok[b][:, tt, :]
                xk = [("X", b, tt, 0), ("X", b, tt, 1)]
                S.op("dve", lambda e, tt=tt, xs=xs: e.scalar_tensor_tensor(
                    out=xs, in0=xs, scalar=mv[:, st, tt, 0:1],
                    in1=lnb[:, par, 0, :], op0=ALU.subtract, op1=ALU.mult),
                    reads=xk + [("MV", st, tt), ("LNB", par)], writes=xk, dur=1.25)
                S.op("dve", lambda e, tt=tt, xs=xs: e.scalar_tensor_tensor(
                    out=xs, in0=xs, scalar=rs[:, st, 1, tt:tt + 1],
                    in1=lnb[:, par, 1, :], op0=ALU.mult, op1=ALU.add),
                    reads=xk + [("RS", st, 1), ("LNB", par)], writes=xk, dur=1.25)
                if do_tr:
                    transposes(b, tt, banks, True, False)
                if store_rows is not None:
                    r0 = store_rows + tt * 128
                    S.op("sp", lambda e, tt=tt, r0=r0: e.dma_start(
                        out=y_d[r0:r0 + 128, :], in_=xtok[b][:, tt, :]),
                        reads=xk, dma=f"d_st{b}_{tt}")
                yield

        def ffn_stage(b, L, q, do_tr, store):
            groups = []
            j = 0
            first = NJ % G if NJ % G else G
            while j < NJ:
                gs = first if j == 0 else G
                groups.append(list(range(j, j + gs)))
                j += gs
            slotsB = {}

            def gu(gi):
                for jj, j in enumerate(groups[gi]):
                    sa = load_AF2(wg_d[L][j], wu_d[L][j])
                    slotsB[j] = load_BF(wd_d[L][j])
                    hidx = (gi % 2) * 4 + jj
                    bg = flip("Fg")
                    bu = 2

                    def mm(e, off, bank, sa=sa):
                        ins = None
                        for k in range(8):
                            ins = e.matmul(ps[:, bank, :],
                                           ringAF[:, sa, off + k * 128:off + (k + 1) * 128],
                                           xT[b][:, k, :], start=(k == 0), stop=(k == 7))
                        return ins
                    S.op("pe", lambda e, mm=mm, bg=bg: mm(e, 0, bg),
                         reads=[("RAF", sa, 0)] + xt_keys(b), writes=[("PS", bg)])
                    S.op("pe", lambda e, mm=mm, bu=bu: mm(e, 1024, bu),
                         reads=[("RAF", sa, 1)] + xt_keys(b), writes=[("PS", bu)])
                    qq = flip("Fs")
                    S.op("act", lambda e, bg=bg, qq=qq: e.activation(
                        out=sgtF[:, qq, :], in_=ps[:, bg, :], func=AF.Silu),
                        reads=[("PS", bg)], writes=[("SGF", qq)])
                    S.op("dve", lambda e, bu=bu, qq=qq, hidx=hidx: e.tensor_tensor(
                        out=hT[:, hidx, :], in0=ps[:, bu, :], in1=sgtF[:, qq, :], op=ALU.mult),
                        reads=[("PS", bu), ("SGF", qq)], writes=[("HT", hidx)])
                    yield

            def down(gi, last):
                grp = groups[gi]
                for tt in range(NTT):
                    def mm(e, tt=tt):
                        ins = None
                        for eh in range(2):
                            for jj, j in enumerate(grp):
                                hidx = (gi % 2) * 4 + jj
                                ins = e.matmul(ps[:, 3 + eh, :],
                                               hT[:, hidx, tt * 128:(tt + 1) * 128],
                                               ringBF[:, slotsB[j], eh * 512:(eh + 1) * 512],
                                               start=(jj == 0), stop=(jj == len(grp) - 1))
                        return ins
                    S.op("pe", mm,
                         reads=[("RBF", slotsB[j]) for j in grp] +
                               [("HT", (gi % 2) * 4 + jj) for jj in range(len(grp))],
                         writes=[("PS", 3), ("PS", 4)], dur=0.5 * len(grp))
                    xs = xtok[b][:, tt, :]
                    pv = ps[:, 3:5, :].rearrange("p a n -> p (a n)")
                    xk = [("X", b, tt, 0), ("X", b, tt, 1)]
                    if gi == 0:
                        S.op("dve", lambda e, xs=xs, pv=pv: e.scalar_tensor_tensor(
                            out=xs, in0=xs, scalar=ALPHA, in1=pv, op0=ALU.mult, op1=ALU.add),
                            reads=[("PS", 3), ("PS", 4)] + xk, writes=xk, dur=1.25)
                    else:
                        S.op("dve", lambda e, xs=xs, pv=pv: e.tensor_tensor(
                            out=xs, in0=xs, in1=pv, op=ALU.add),
                            reads=[("PS", 3), ("PS", 4)] + xk, writes=xk, dur=1.25)
                    if last:
                        ln_stats(0, b, tt)
                    yield

            ng = len(groups)
            for gi in range(ng):
                g_it = gu(gi)
                d_it = down(gi - 1, False) if gi >= 1 else iter(())
                g_alive = d_alive = True
                while g_alive or d_alive:
                    if g_alive:
                        try:
                            next(g_it)
                            yield
                        except StopIteration:
                            g_alive = False
                    if d_alive:
                        try:
                            next(d_it)
                            yield
                        except StopIteration:
                            d_alive = False
            par = load_ln(L, 1)
            yield from down(ng - 1, True)
            inline = yield "DECIDE"
            banks = (0, 1) if inline else (5, 6)
            yield from ln_finish(0, b, par, banks, do_tr,
                                 store_rows=(q * NT if store else None))

        def load_block(b, q):
            for tt in range(NTT):
                r0 = q * NT + tt * 128
                S.op("sp", lambda e, tt=tt, r0=r0: e.dma_start(out=xtok[b][:, tt, :], in_=x_d[r0:r0 + 128, :]),
                     writes=[("X", b, tt, 0), ("X", b, tt, 1)], dma=f"d_x{b}_{tt}")

        def conv_stage(b, L, q, first):
            cc = convc[L]
            hb = hbias[L]
            CK = ("CC", L)
            if first:
                load_block(b, q)
                for tt in range(NTT):
                    transposes(b, tt, (5, 6), True, False)
                    yield
            if q == 0:
                S.op("dve", lambda e: e.memset(s1b[:, :, 0:UP], 0.0),
                     writes=[("S1", c) for c in range(8)])
            else:
                S.op("dve", lambda e: e.tensor_copy(out=s1b[:, :, 0:UP], in_=uh[L][:]),
                     reads=[("UH", L)], writes=[("S1", c) for c in range(8)])
            S.op("sp", lambda e: e.dma_start(out=mixb[:, 0, :], in_=mixb_d[L, 0]),
                 writes=[("MB", 0)], dma="d_mb")
            par = load_ln(L, 0)

            def mkdiag(c):
                i = c % 2
                dst = dgb[:, i, :].rearrange("p (k j) -> p k j", k=KW)
                wv = cc[:, 16 + c * KW:16 + (c + 1) * KW]
                S.op("dve", lambda e: e.tensor_tensor(
                    out=dst, in0=ident.unsqueeze(1).to_broadcast([128, KW, 128]),
                    in1=wv.unsqueeze(2).to_broadcast([128, KW, 128]), op=ALU.mult),
                    reads=[CK, ("KT",)], writes=[("DG", i)], dur=4.3)

            def p1(c):
                sa = load_AM(win_d[L][c])

                def mm(e, off, bank):
                    ins = None
                    for k in range(8):
                        ins = e.matmul(ps[:, bank, :],
                                       ringAM[:, sa, off + k * 128:off + (k + 1) * 128],
                                       xT[b][:, k, :], start=(k == 0), stop=(k == 7))
                    return ins
                S.op("pe", lambda e: mm(e, 0, 5), reads=[("RAM", sa)] + xt_keys(b), writes=[("PS", 5)])
                S.op("pe", lambda e: mm(e, 1024, 6), reads=[("RAM", sa)] + xt_keys(b), writes=[("PS", 6)])
                qq = 0
                S.op("act", lambda e: e.activation(
                    out=vbh[:, qq, :], in_=ps[:, 5, :], func=AF.Identity,
                    bias=hb[:, c:c + 1], scale=0.5),
                    reads=[("PS", 5), ("HB", L)], writes=[("VB", qq)])
                S.op("act", lambda e: e.activation(
                    out=sgtM[:, qq, :], in_=ps[:, 6, :], func=AF.Tanh,
                    bias=hb[:, 8 + c:9 + c], scale=0.5),
                    reads=[("PS", 6), ("HB", L)], writes=[("SGM", qq)])
                S.op("dve", lambda e: e.scalar_tensor_tensor(
                    out=s1b[:, c, UP:UP + NT], in0=sgtM[:, qq, :], scalar=1.0, in1=vbh[:, qq, :],
                    op0=ALU.add, op1=ALU.mult),
                    reads=[("SGM", qq), ("VB", qq)], writes=[("S1", c)])

            def conv(c):
                i = c % 2

                def mm(e):
                    ins = None
                    for k in range(KW):
                        ins = e.matmul(ps[:, 7, :], dgb[:, i, k * 128:(k + 1) * 128],
                                       s1b[:, c, 2 + k:2 + k + NT], start=(k == 0), stop=(k == KW - 1))
                    return ins
                S.op("pe", mm, reads=[("S1", c), ("DG", i)], writes=[("PS", 7)], dur=7.0)
                S.op("act", lambda e: e.activation(
                    out=s2[:, c, :], in_=ps[:, 7, :], func=AF.Identity,
                    bias=cc[:, 264 + c:265 + c], scale=1.0),
                    reads=[("PS", 7), CK], writes=[("S2", c)])

            for c in range(8):
                mkdiag(c)
                p1(c)
                yield
                if c >= 1:
                    conv(c - 1)
                    yield
            if q < NQ - 1:
                S.op("dve", lambda e: e.tensor_copy(out=uh[L][:], in_=s1b[:, :, NT:NT + UP]),
                     reads=[("S1", c) for c in range(8)], writes=[("UH", L)])
            conv(7)
            yield
            for tt in range(NTT):
                xs = xtok[b][:, tt, :]
                xk = [("X", b, tt, 0), ("X", b, tt, 1)]
                S.op("dve", lambda e, xs=xs: e.scalar_tensor_tensor(
                    out=xs, in0=xs, scalar=ALPHA, in1=mixb[:, 0, :], op0=ALU.mult, op1=ALU.add),
                    reads=[("MB", 0)] + xk, writes=xk, dur=1.25)
            yield
            sB = {}
            for kp in range(4):
                sB[(0, kp)] = load_BM(wout_d[L][0, kp])
            sq = [s1[:, 4 + i, 0:NT] for i in range(2)]
            mean_t = s1[:, 0, 0:NT]
            rstd_t = s1[:, 1, 0:NT]
            tmp_t = s1[:, 2, 0:NT]
            for c in range(8):
                i = c % 2
                S.op("act", lambda e, c=c, i=i: e.activation(out=sq[i], in_=s2[:, c, :], func=AF.Square),
                     reads=[("S2", c)], writes=[("S1", 4 + i)])

                def mm(e, c=c, i=i):
                    e.matmul(ps[:, 5, :], onesm, s2[:, c, :], start=(c == 0), stop=(c == 7))
                    return e.matmul(ps[:, 6, :], onesm, sq[i], start=(c == 0), stop=(c == 7))
                S.op("pe", mm, reads=[("S2", c), ("S1", 4 + i), ("KT",)], writes=[("PS", 5), ("PS", 6)])
                if c % 2 == 1:
                    yield
            for kp in range(4):
                S.op("pool", lambda e, kp=kp: e.dma_start(out=s1b[:, 4 + kp, 0:D], in_=wout_d[L][1, kp]),
                     writes=[("S1", 4 + kp)], dma=f"d_w1_{kp}")
            S.op("dve", lambda e: e.tensor_copy(out=mean_t, in_=ps[:, 5, :]),
                 reads=[("PS", 5)], writes=[("S1", 0)])
            S.op("dve", lambda e: e.tensor_tensor(out=tmp_t, in0=mean_t, in1=mean_t, op=ALU.mult),
                 reads=[("S1", 0)], writes=[("S1", 2)])
            S.op("dve", lambda e: e.tensor_tensor(out=tmp_t, in0=ps[:, 6, :], in1=tmp_t, op=ALU.subtract),
                 reads=[("PS", 6), ("S1", 2)], writes=[("S1", 2)])
            S.op("act", lambda e: e.activation(out=tmp_t, in_=tmp_t, func=AF.Sqrt, bias=EPS, scale=1.0),
                 reads=[("S1", 2)], writes=[("S1", 2)], dur=3.0)
            S.op("dve", lambda e: e.reciprocal(out=rstd_t, in_=tmp_t),
                 reads=[("S1", 2)], writes=[("S1", 1)])
            yield
            for c0 in range(0, 8, 2):
                for c in (c0, c0 + 1):
                    S.op("dve", lambda e, c=c: e.tensor_tensor(out=s2[:, c, :], in0=s2[:, c, :],
                                                               in1=mean_t, op=ALU.subtract),
                         reads=[("S2", c), ("S1", 0)], writes=[("S2", c)])
                for c in (c0, c0 + 1):
                    S.op("dve", lambda e, c=c: e.tensor_tensor(out=s2[:, c, :], in0=s2[:, c, :],
                                                               in1=rstd_t, op=ALU.mult),
                         reads=[("S2", c), ("S1", 1)], writes=[("S2", c)])
                for c in (c0, c0 + 1):
                    S.op("act", lambda e, c=c: e.activation(
                        out=sTm[:, c, :], in_=s2[:, c, :], func=AF.Silu,
                        bias=cc[:, 280 + c:281 + c], scale=cc[:, 272 + c:273 + c]),
                        reads=[("S2", c), CK], writes=[("STM", c)])
                yield
            for eh in range(2):
                for tt in range(NTT):
                    bank = 5 + flip("Mo", 3)

                    def mm(e, bank=bank, tt=tt, eh=eh):
                        ins = None
                        for c in range(8):
                            if eh == 0:
                                w = ringBM[:, sB[(0, c // 2)], (c % 2) * 512:(c % 2 + 1) * 512]
                            else:
                                w = s1b[:, 4 + c // 2, (c % 2) * 512:(c % 2 + 1) * 512]
                            ins = e.matmul(ps[:, bank, :], sTm[:, c, tt * 128:(tt + 1) * 128], w,
                                           start=(c == 0), stop=(c == 7))
                        return ins
                    wk = ([("RBM", sB[(0, kp)]) for kp in range(4)] if eh == 0
                          else [("S1", 4 + kp) for kp in range(4)])
                    S.op("pe", mm, reads=wk + [("STM", c) for c in range(8)], writes=[("PS", bank)])
                    xs = xtok[b][:, tt, eh * 512:(eh + 1) * 512]
                    S.op("dve", lambda e, xs=xs, bank=bank: e.tensor_tensor(
                        out=xs, in0=xs, in1=ps[:, bank, :], op=ALU.add),
                        reads=[("PS", bank), ("X", b, tt, eh)], writes=[("X", b, tt, eh)])
                    if eh == 1:
                        ln_stats(1, b, tt)
                    yield
            yield from ln_finish(1, b, par, (5, 6), True)

        def pool_stage(b, L, q, first):
            if first:
                load_block(b, q)
            for tt in range(NTT):
                transposes(b, tt, (5, 6), False, True)
                yield
            if q == 0:
                S.op("dve", lambda e: e.memset(s1[:, :, 0:PP], 0.0),
                     writes=[("S1", c) for c in range(8)])
            else:
                S.op("dve", lambda e: e.tensor_copy(out=s1[:, :, 0:PP], in_=xh[L][:]),
                     reads=[("XH", L)], writes=[("S1", c) for c in range(8)])
            if q < NQ - 1:
                S.op("dve", lambda e: e.tensor_copy(out=xh[L][:], in_=s1[:, :, NT:NT + PP]),
                     reads=[("S1", c) for c in range(8)], writes=[("XH", L)])
            S.op("sp", lambda e: e.dma_start(out=mixb[:, :, :],
                                             in_=mixb_d[L].rearrange("a p n -> p a n")),
                 writes=[("MB", 0), ("MB", 1)], dma="d_mb")
            S.op("dve", lambda e: e.tensor_tensor(out=mixb[:, 0, :], in0=mixb[:, 0, :],
                                                  in1=mixb[:, 1, :], op=ALU.mult),
                 reads=[("MB", 0), ("MB", 1)], writes=[("MB", 0)])
            sa = load_AM(wp_d[L])
            par = load_ln(L, 0)
            W = NT + PP
            for g in range(4):
                w = POOL_W[g]
                cs = (2 * g, 2 * g + 1)
                cur = {c: (s1[:, c, 0:W], [("S1", c)]) for c in cs}
                shift = 1
                for step in range(g + 1):
                    lo = 2 * shift - 1
                    for i, c in enumerate(cs):
                        r0 = 4 * i + 2 * (step % 2)
                        dst = s2[:, r0:r0 + 2, :].rearrange("p a n -> p (a n)")[:, 0:W]
                        dkeys = [("S2", r0), ("S2", r0 + 1)]
                        src, skeys = cur[c]
                        S.op("dve", lambda e, dst=dst, src=src, lo=lo, shift=shift: e.tensor_tensor(
                            out=dst[:, lo:W], in0=src[:, lo:W], in1=src[:, lo - shift:W - shift],
                            op=ALU.add),
                            reads=skeys, writes=dkeys)
                        cur[c] = (dst, dkeys)
                    shift *= 2
                for i, c in enumerate(cs):
                    src, skeys = cur[c]
                    S.op("dve", lambda e, c=c, src=src, w=w: e.scalar_tensor_tensor(
                        out=sTm[:, c, :], in0=src[:, PP:PP + NT], scalar=1.0 / w,
                        in1=s1[:, c, PP:PP + NT], op0=ALU.mult, op1=ALU.subtract),
                        reads=[("S1", c)] + skeys, writes=[("STM", c)])
                if q == 0:
                    for i, c in enumerate(cs):
                        src, skeys = cur[c]
                        S.op("dve", lambda e, i=i, src=src, w=w: e.tensor_tensor(
                            out=fx[:, i, 0:w - 1], in0=src[:, PP:PP + w - 1], in1=invc[:, 0:w - 1],
                            op=ALU.mult),
                            reads=skeys + [("KT",)], writes=[("FX", i)])
                    for i, c in enumerate(cs):
                        S.op("dve", lambda e, i=i, c=c, w=w: e.tensor_tensor(
                            out=sTm[:, c, 0:w - 1], in0=fx[:, i, 0:w - 1], in1=s1[:, c, PP:PP + w - 1],
                            op=ALU.subtract),
                            reads=[("FX", i), ("S1", c), ("STM", c)], writes=[("STM", c)])
                yield
            for tt in range(NTT):
                for gh in range(2):
                    bank = 5 + flip("Mo", 3)

                    def mm(e, bank=bank, tt=tt, gh=gh):
                        ins = None
                        for gg in range(2):
                            g = 2 * gh + gg
                            for kk in range(2):
                                off = (g * 2 + kk) * 256
                                ins = e.matmul(ps[:, bank, gg * 256:(gg + 1) * 256],
                                               sTm[:, 2 * g + kk, tt * 128:(tt + 1) * 128],
                                               ringAM[:, sa, off:off + 256],
                                               start=(kk == 0), stop=(kk == 1))
                        return ins
                    S.op("pe", mm, reads=[("RAM", sa)] + [("STM", c) for c in range(4 * gh, 4 * gh + 4)],
                         writes=[("PS", bank)], dur=0.6)
                    qq = 0
                    sl = slice(gh * 512, (gh + 1) * 512)
                    xs = xtok[b][:, tt, sl]
                    S.op("dve", lambda e, bank=bank, qq=qq, sl=sl: e.tensor_tensor(
                        out=sgtM[:, qq, :], in0=ps[:, bank, :], in1=mixb[:, 1, sl], op=ALU.mult),
                        reads=[("PS", bank), ("MB", 1)], writes=[("SGM", qq)])
                    S.op("dve", lambda e, xs=xs, qq=qq: e.scalar_tensor_tensor(
                        out=xs, in0=xs, scalar=ALPHA, in1=sgtM[:, qq, :], op0=ALU.mult, op1=ALU.add),
                        reads=[("SGM", qq), ("X", b, tt, gh)], writes=[("X", b, tt, gh)])
                    S.op("dve", lambda e, xs=xs, sl=sl: e.tensor_tensor(
                        out=xs, in0=xs, in1=mixb[:, 0, sl], op=ALU.add),
                        reads=[("MB", 0), ("X", b, tt, gh)], writes=[("X", b, tt, gh)])
                ln_stats(1, b, tt)
                yield
            yield from ln_finish(1, b, par, (5, 6), True)

        stages = []
        for pair in ((0, 1), (2, 3)):
            for li, L in enumerate(layers):
                for q in pair:
                    stages.append((q, li, L))

        def mk_mixer(k):
            q, li, L = stages[k]
            b = q % 2
            if L % 2 == 0:
                return conv_stage(b, L, q, li == 0)
            return pool_stage(b, L, q, li == 0)

        def mk_ffn(k):
            q, li, L = stages[k]
            b = q % 2
            is_last = (li == len(layers) - 1)
            nxt_conv = (not is_last) and (layers[li + 1] % 2 == 0)
            return ffn_stage(b, L, q, nxt_conv, is_last)

        def drain(g):
            for _ in g:
                pass

        class Stream:
            def __init__(self, g, eager=False):
                self.g = g
                self.fifo = []
                self.alive = True
                self.at_decide = False
                self.send = None
                if eager:
                    while self.alive:
                        self.pump()

            def pump(self):
                S.capture = self.fifo
                try:
                    if self.send is not None:
                        r = self.g.send(self.send)
                        self.send = None
                    else:
                        r = next(self.g)
                    if r == "DECIDE":
                        self.at_decide = True
                        self.alive = False
                except StopIteration:
                    self.alive = False
                S.capture = None

            def fill(self):
                while self.alive and not self.fifo:
                    self.pump()
                return bool(self.fifo)

            def pe_left(self):
                return sum((S.DEF_DUR["pe"] if d[5] is None else d[5]) for d in self.fifo if d[0] == "pe")

        def merge(streams, until=None):
            while True:
                live = [s for s in streams if s.fill()]
                if until is not None and not until.fill():
                    return
                if not live:
                    return
                def cost(s):
                    d = s.fifo[0]
                    st = S.estimate(d)
                    if d[0] == "pe":
                        st += PE_STALL_W * max(0.0, st - S.eng_free["pe"])
                    return st + _jit.uniform(0.0, JITTER_US)
                best = min(live, key=cost)
                S.commit(best.fifo.pop(0))

        TAIL_INLINE_US = 0.0
        JITTER_US = 0.3
        _jit = random.Random(JITTER_SEED)
        PE_STALL_W = 0.0
        drain(mk_mixer(0))
        ftail = None
        for k in range(len(stages)):
            fs = Stream(mk_ffn(k))
            if ftail is not None:
                merge([fs, ftail], until=ftail)
                ftail = None
            ms = Stream(mk_mixer(k + 1), eager=True) if k + 1 < len(stages) else None
            while True:
                merge([s for s in (fs, ms) if s is not None], until=fs)
                if fs.at_decide:
                    fs.at_decide = False
                    inline = ms is not None and ms.pe_left() > TAIL_INLINE_US
                    fs.send = bool(inline)
                    fs.alive = True
                    if not inline:
                        ftail = fs
                        break
                else:
                    break
            if ms is not None:
                merge([ms])
        if ftail is not None:
            merge([ftail])
        S.final_wait("sp", [("X", b, tt, eh) for b in range(2) for tt in range(NTT) for eh in range(2)])

        sems = {k: es.enter_context(nc.semaphore(f"s_{k}")) for k in sorted(S.semkeys)}
        block = es.enter_context(nc.Block())

        class FirstIns:
            def __init__(self, e):
                self.e = e
                self.first = None

            def matmul(self, *a, **k):
                r = self.e.matmul(*a, **k)
                if self.first is None:
                    self.first = r
                return r

            def transpose(self, *a, **k):
                r = self.e.transpose(*a, **k)
                if self.first is None:
                    self.first = r
                return r

        def run(name, e):
            for waits, fn, inc in S.ops[name]:
                waits = list(waits)
                fused = None
                if fn is not None and waits:
                    fused = waits.pop()
                for sk, val in waits:
                    e.wait_ge(sems[sk], val)
                if fn is not None:
                    if name == "pe":
                        px = FirstIns(e)
                        ins = fn(px)
                        first = px.first
                    else:
                        ins = fn(e)
                        first = ins
                    if fused is not None:
                        first._wait_ge(sems[fused[0]], fused[1])
                    ins.then_inc(sems[inc[0]], inc[1])

        @block.tensor
        def _(e):
            run("pe", e)

        @block.scalar
        def _(e):
            run("act", e)

        @block.vector
        def _(e):
            run("dve", e)

        @block.gpsimd
        def _(e):
            run("pool", e)

        @block.sync
        def _(e):
            run("sp", e)
    return nc


def _bc(v):
    return np.ascontiguousarray(np.broadcast_to(np.asarray(v, np.float32).reshape(1, -1), (128, D)))


def _prep(inputs, layers):
    f = lambda a: np.ascontiguousarray(np.asarray(a, dtype=np.float32))
    m = {}
    ktab = np.zeros((128, 272), np.float32)
    ktab[:, 0:128] = np.eye(128, dtype=np.float32)
    ktab[:, 128:256] = 1.0 / D
    ktab[:, 256:272] = (1.0 / np.arange(1, 17, dtype=np.float32))[None, :]
    m["ktab"] = ktab
    lnp = np.zeros((DEPTH, 4, 128, D), np.float32)
    mixb = np.zeros((DEPTH, 2, 128, D), np.float32)
    for L in range(DEPTH):
        lnp[L, 0] = _bc(inputs["ln1_g"][L])
        lnp[L, 1] = _bc(inputs["ln1_b"][L])
        lnp[L, 2] = _bc(inputs["ln2_g"][L])
        lnp[L, 3] = _bc(inputs["ln2_b"][L])
        l = L // 2
        if L % 2 == 0:
            mixb[L, 0] = _bc(inputs["a_b_out"][l])
        else:
            mixb[L, 0] = _bc(np.asarray(inputs["p_b"][l]).reshape(-1))
            mixb[L, 1] = _bc(inputs["p_scale"][l])
    m["lnp"] = lnp
    m["mixb"] = mixb
    for L in layers:
        l = L // 2
        wg = f(inputs["ffn_w_gate"][L]).reshape(8, 128, NJ, 128).transpose(2, 1, 0, 3)
        m[f"wg{L}"] = np.ascontiguousarray(wg).reshape(NJ, 128, 1024)
        wu = f(inputs["ffn_w_up"][L]).reshape(8, 128, NJ, 128).transpose(2, 1, 0, 3)
        m[f"wu{L}"] = np.ascontiguousarray(wu).reshape(NJ, 128, 1024)
        m[f"wd{L}"] = f(inputs["ffn_w_down"][L]).reshape(NJ, 128, D)
        if L % 2 == 0:
            win = f(inputs["a_w_in"][l]).reshape(8, 128, 2, 8, 128).transpose(3, 1, 2, 0, 4)
            m[f"win{L}"] = np.ascontiguousarray(win).reshape(8, 128, 2048)
            wo = f(inputs["a_w_out"][l]).reshape(4, 2, 128, 2, 512).transpose(3, 0, 2, 1, 4)
            m[f"wout{L}"] = np.ascontiguousarray(wo).reshape(2, 4, 128, 1024)
            cc = np.zeros((128, 288), np.float32)
            cc[:, 0:16] = f(inputs["a_b_in"][l]).reshape(16, 128).T
            wdw = f(inputs["a_w_dw"][l])[:, 0, :]
            cc[:, 16:264] = wdw.T.reshape(8, 128, KW).transpose(1, 0, 2).reshape(128, 8 * KW)
            cc[:, 264:272] = f(inputs["a_b_dw"][l]).reshape(8, 128).T
            cc[:, 272:280] = f(inputs["a_ln_g"][l]).reshape(8, 128).T
            cc[:, 280:288] = f(inputs["a_ln_b"][l]).reshape(8, 128).T
            m[f"convc{L}"] = cc
        else:
            wp = f(inputs["p_w"][l]).reshape(4, 2, 128, 256).transpose(2, 0, 1, 3)
            m[f"wp{L}"] = np.ascontiguousarray(wp).reshape(128, 2048)
    return m


_CACHE = {}


def _get_nc(layers):
    key = tuple(layers)
    if key not in _CACHE:
        _CACHE[key] = build(list(layers))
    return _CACHE[key]


def _run(inputs, x, layers):
    common = _prep(inputs, layers)
    nc = _get_nc(layers)
    in_maps = []
    for b in range(NB):
        mm = dict(common)
        mm["x"] = np.ascontiguousarray(x[b])
        in_maps.append(mm)
    res = run_bass_kernel_spmd(nc, in_maps, core_ids=list(range(NB)))
    return np.stack([np.asarray(r["y"], dtype=np.float32) for r in res.results], axis=0)


def kernel(**inputs):
    x = np.asarray(inputs["x"], dtype=np.float32)
    return _run(inputs, x, [0, 1, 2, 3])
```
